# Optimizing a Trainium2 kernel written in Bass

```python
import math
import jax
import jax.numpy as jnp
from jax import lax
import numpy as np

D_MODEL = 2048
BATCH = 4
SEQ = 4096
DEPTH = 2

GRID_W = 64
CTX_LEN = 256
N_EVEN = (DEPTH + 1) // 2
N_ODD = DEPTH // 2
EPS = 1e-6

HY_WIDTH = D_MODEL // 2
HY_ORDER = 2
HY_SHORT = 3
HY_BANDS = 16
HY_EMB = 2 * HY_BANDS + 1
HY_FFN = 64
HY_TARGET = 1e-2
HY_FAST_PCT = 0.3
HY_SLOW_PCT = 1.5

S5_WIDTH = D_MODEL // 2
S5_GROUP = 16
S5_GROUPS = S5_WIDTH // S5_GROUP
S5_STATE = 64
S5_DT_MIN = 1e-3
S5_DT_MAX = 1e-1

EVEN_IN = (HY_ORDER + 2) * HY_WIDTH + 2 * S5_WIDTH
EVEN_MIX = HY_WIDTH + S5_WIDTH

DA_HEAD = 64
DA_HEADS = D_MODEL // (2 * DA_HEAD)
DA_QK = DA_HEADS * 2 * DA_HEAD
DA_V = DA_HEADS * 2 * DA_HEAD
ODD_IN = 2 * DA_QK + 2 * DA_V
Q_BLOCK = 128
ROPE_BASE = 10000.0

kernel_name = 'hybrid_hyena_s5_diffattn_dit'


def rms_norm(x, w):
    xf = x.astype(jnp.float32)
    y = xf * lax.rsqrt(jnp.mean(xf * xf, axis=-1, keepdims=True) + EPS)
    return (y * w.astype(jnp.float32)).astype(x.dtype)


def short_conv(u, w, b):
    n = u.shape[1]
    pad = HY_SHORT // 2
    up = jnp.pad(u, ((0, 0), (pad, pad), (0, 0)))
    return sum(up[:, j:j + n] * w[j] for j in range(HY_SHORT)) + b


def hyena_filters(n, w1, b1, w2, b2, w3, b3, freq):
    f32 = jnp.float32
    t = jnp.arange(n, dtype=f32)
    tn = t / n
    bands = jnp.linspace(1e-4, HY_BANDS - 1, HY_BANDS, dtype=f32)
    ang = (2.0 * math.pi / n) * t[:, None] * bands[None, :]
    feat = jnp.concatenate([tn[:, None], jnp.cos(ang), -jnp.sin(ang)], axis=-1)
    h = jnp.sin(freq[0].astype(f32) * (feat @ w1.astype(f32) + b1.astype(f32)))
    h = jnp.sin(freq[1].astype(f32) * (h @ w2.astype(f32) + b2.astype(f32)))
    h = (h @ w3.astype(f32) + b3.astype(f32)).reshape(n, HY_ORDER, 2, HY_WIDTH)
    deltas = jnp.abs(jnp.linspace(math.log(HY_TARGET) / HY_SLOW_PCT, math.log(HY_TARGET) / HY_FAST_PCT, HY_WIDTH, dtype=f32))
    h = h * jnp.exp(-tn[:, None] * deltas[None, :])[:, None, None, :]
    h_fwd = h[:, :, 0]
    h_bwd = h[1:, :, 1]
    l1 = jnp.sum(jnp.abs(h_fwd), axis=0) + jnp.sum(jnp.abs(h_bwd), axis=0)
    k = jnp.concatenate([h_fwd, jnp.zeros((1, HY_ORDER, HY_WIDTH), f32), h_bwd[::-1]], axis=0) / l1
    return jnp.fft.rfft(k, axis=0)


def fft_long_conv(u, kf, skip):
    n = u.shape[1]
    uf32 = u.astype(jnp.float32)
    uf = jnp.fft.rfft(uf32, n=2 * n, axis=1)
    y = jnp.fft.irfft(uf * kf[None], n=2 * n, axis=1)[:, :n]
    return y + uf32 * skip.astype(jnp.float32)


def hyena(proj, conv_w, conv_b, filt, skip):
    n = proj.shape[1]
    parts = jnp.split(short_conv(proj, conv_w, conv_b), HY_ORDER + 1, axis=-1)
    kf = hyena_filters(n, *filt)
    y = parts[0]
    for i in range(HY_ORDER):
        y = parts[i + 1].astype(jnp.float32) * fft_long_conv(y, kf[:, i], skip[i])
    return y.astype(proj.dtype)


def s5_discretise(a_re, a_im, log_dt, b_re, b_im):
    f32 = jnp.float32
    lam = lax.complex(a_re.astype(f32), a_im.astype(f32))
    dt = jnp.exp(log_dt.astype(f32))[:, None]
    lam_bar = jnp.exp(lam * dt)
    b = lax.complex(b_re.astype(f32), b_im.astype(f32))
    b_bar = ((lam_bar - 1.0) / lam)[..., None] * b
    return lam_bar, b_bar


def s5_scan(u, lam_bar, b_bar, s0, reverse):
    n = u.shape[1]
    bu = jnp.einsum('bngc,gpc->bngp', u.astype(jnp.complex64), b_bar)
    first = n - 1 if reverse else 0
    last = 0 if reverse else n - 1
    bu = bu.at[:, first].add(lam_bar[None] * s0)
    a = jnp.broadcast_to(lam_bar, (1,) + bu.shape[1:])

    def combine(e1, e2):
        a1, b1 = e1
        a2, b2 = e2
        return a2 * a1, a2 * b1 + b2

    _, states = lax.associative_scan(combine, (a, bu), reverse=reverse, axis=1)
    return states, states[:, last]


def s5_branch(u_ctx, u_lat, a_re, a_im, log_dt, b_re, b_im, c_re, c_im, d):
    f32 = jnp.float32
    bsz, n_ctx, _ = u_ctx.shape
    n_lat = u_lat.shape[1]
    uc = u_ctx.astype(f32)
    ul = u_lat.astype(f32)
    dd = d.astype(f32)
    y_ctx = uc * dd
    y_lat = ul * dd
    uc_g = uc.reshape(bsz, n_ctx, S5_GROUPS, S5_GROUP)
    ul_g = ul.reshape(bsz, n_lat, S5_GROUPS, S5_GROUP)
    s_zero = jnp.zeros((bsz, S5_GROUPS, S5_STATE), jnp.complex64)
    for direction, reverse in ((0, False), (1, True)):
        lam_bar, b_bar = s5_discretise(a_re[direction], a_im[direction], log_dt[direction], b_re[direction], b_im[direction])
        c_mat = lax.complex(c_re[direction].astype(f32), c_im[direction].astype(f32))
        st_c, s_fin = s5_scan(uc_g, lam_bar, b_bar, s_zero, reverse)
        st_l = s5_scan(ul_g, lam_bar, b_bar, s_fin, reverse)[0]
        y_ctx = y_ctx + jnp.real(jnp.einsum('bngp,gcp->bngc', st_c, c_mat)).reshape(bsz, n_ctx, S5_WIDTH)
        y_lat = y_lat + jnp.real(jnp.einsum('bngp,gcp->bngc', st_l, c_mat)).reshape(bsz, n_lat, S5_WIDTH)
    return y_ctx, y_lat


def s5_glu(y, w, b):
    g = jax.nn.gelu(y)
    return g * jax.nn.sigmoid(g @ w.astype(jnp.float32) + b.astype(jnp.float32))


def even_mixer(h_lat, h_ctx, in_w, out_w, conv_w, conv_b, w1, b1, w2, b2, w3, b3, freq, skip,
               a_re, a_im, log_dt, b_re, b_im, c_re, c_im, d, glu_w, glu_b):
    cuts = [(HY_ORDER + 1) * HY_WIDTH, (HY_ORDER + 2) * HY_WIDTH, (HY_ORDER + 2) * HY_WIDTH + S5_WIDTH]
    hp_l, hg_l, su_l, sg_l = jnp.split(h_lat @ in_w, cuts, axis=-1)
    hp_c, hg_c, su_c, sg_c = jnp.split(h_ctx @ in_w, cuts, axis=-1)
    filt = (w1, b1, w2, b2, w3, b3, freq)
    hy_l = hyena(hp_l, conv_w, conv_b, filt, skip)
    hy_c = hyena(hp_c, conv_w, conv_b, filt, skip)
    s5_c, s5_l = s5_branch(su_c, su_l, a_re, a_im, log_dt, b_re, b_im, c_re, c_im, d)
    s5_l = s5_glu(s5_l, glu_w, glu_b)
    s5_c = s5_glu(s5_c, glu_w, glu_b)

    def merge(hy, hg, s5, sg):
        mixed = jnp.concatenate([hy * jax.nn.silu(hg), s5.astype(hy.dtype) * jax.nn.silu(sg)], axis=-1)
        return mixed @ out_w

    return merge(hy_l, hg_l, s5_l, sg_l), merge(hy_c, hg_c, s5_c, sg_c)


def rope_2d(x):
    f32 = jnp.float32
    n = x.shape[1]
    rows = n // GRID_W
    row = jnp.broadcast_to(jnp.arange(rows, dtype=f32)[:, None], (rows, GRID_W)).reshape(n)
    col = jnp.broadcast_to(jnp.arange(GRID_W, dtype=f32)[None, :], (rows, GRID_W)).reshape(n)
    half = DA_HEAD // 2
    quarter = DA_HEAD // 4
    freqs = ROPE_BASE ** (-jnp.arange(quarter, dtype=f32) / quarter)
    xf = x.astype(f32)

    def rot(xa, pos):
        ang = pos[:, None] * freqs[None, :]
        cos = jnp.cos(ang)[None, :, None, None, :]
        sin = jnp.sin(ang)[None, :, None, None, :]
        x1, x2 = xa[..., :quarter], xa[..., quarter:]
        return jnp.concatenate([x1 * cos - x2 * sin, x1 * sin + x2 * cos], axis=-1)

    return jnp.concatenate([rot(xf[..., :half], row), rot(xf[..., half:], col)], axis=-1).astype(x.dtype)


def diff_attend(q, k, v, lam):
    s = jnp.einsum('bqhmd,bkhmd->bhmqk', q.astype(jnp.float32), k.astype(jnp.float32)) * (DA_HEAD ** -0.5)
    p = jax.nn.softmax(s, axis=-1)
    w = p[:, :, 0] - lam * p[:, :, 1]
    return jnp.einsum('bhqk,bkhe->bqhe', w, v.astype(jnp.float32))


def odd_mixer(h_lat, h_ctx, layer, need_ctx, in_w, out_w, q_norm, k_norm, lq1, lk1, lq2, lk2, subln_w):
    f32 = jnp.float32
    bsz, n_lat, _ = h_lat.shape
    n_ctx = h_ctx.shape[1]
    lam_init = 0.8 - 0.6 * math.exp(-0.3 * layer)
    lam = (jnp.exp(jnp.sum(lq1.astype(f32) * lk1.astype(f32)))
           - jnp.exp(jnp.sum(lq2.astype(f32) * lk2.astype(f32))) + lam_init)

    def qk_shape(t, n):
        return t.reshape(bsz, n, DA_HEADS, 2, DA_HEAD)

    def v_shape(t, n):
        return t.reshape(bsz, n, DA_HEADS, 2 * DA_HEAD)

    cuts = [DA_QK, 2 * DA_QK, 2 * DA_QK + DA_V]
    q_l, k_l, v_l, g_l = jnp.split(h_lat @ in_w, cuts, axis=-1)
    q_l = rope_2d(rms_norm(qk_shape(q_l, n_lat), q_norm))
    k_l = rope_2d(rms_norm(qk_shape(k_l, n_lat), k_norm))
    v_l = v_shape(v_l, n_lat)
    if need_ctx:
        q_c, k_c, v_c, g_c = jnp.split(h_ctx @ in_w, cuts, axis=-1)
    else:
        k_c, v_c = jnp.split(h_ctx @ in_w[:, DA_QK:2 * DA_QK + DA_V], [DA_QK], axis=-1)
    k_c = rms_norm(qk_shape(k_c, n_ctx), k_norm)
    v_c = v_shape(v_c, n_ctx)
    k_all = jnp.concatenate([k_c, k_l], axis=1)
    v_all = jnp.concatenate([v_c, v_l], axis=1)
    n_blk = n_lat // Q_BLOCK
    q_blocks = q_l.reshape(bsz, n_blk, Q_BLOCK, DA_HEADS, 2, DA_HEAD).transpose(1, 0, 2, 3, 4, 5)
    o_l = lax.map(lambda qb: diff_attend(qb, k_all, v_all, lam), q_blocks)
    o_l = o_l.transpose(1, 0, 2, 3, 4).reshape(bsz, n_lat, DA_HEADS, 2 * DA_HEAD)

    def post(o, g):
        o = rms_norm(o, subln_w) * (1.0 - lam_init)
        o = o.reshape(bsz, o.shape[1], DA_V).astype(g.dtype) * jax.nn.silu(g)
        return o @ out_w

    out_l = post(o_l, g_l)
    if need_ctx:
        q_c = rms_norm(qk_shape(q_c, n_ctx), q_norm)
        out_c = post(diff_attend(q_c, k_c, v_c, lam), g_c)
    else:
        out_c = None
    return out_l, out_c


def setup_inputs(seed: int = 0) -> dict:
    key = jax.random.key(seed)
    keys = iter(jax.random.split(key, 64))
    f32 = jnp.float32

    def nrm(shape, scale):
        return jax.random.normal(next(keys), shape, f32) * scale

    D = D_MODEL
    x = nrm((BATCH, SEQ, D), 1.0)
    c = nrm((BATCH, D), 1.0)
    ctx = nrm((BATCH, CTX_LEN, D), 1.0)
    c_ctx = nrm((D,), 1.0)
    mod_w = nrm((DEPTH, D, 3 * D), 0.5 * D ** -0.5)
    mod_b = nrm((DEPTH, 3 * D), 0.02)
    norm_w = 1.0 + nrm((DEPTH, D), 0.02)
    ev_in_w = nrm((N_EVEN, D, EVEN_IN), D ** -0.5)
    ev_out_w = nrm((N_EVEN, EVEN_MIX, D), EVEN_MIX ** -0.5)
    hy_conv_w = nrm((N_EVEN, HY_SHORT, (HY_ORDER + 1) * HY_WIDTH), HY_SHORT ** -0.5)
    hy_conv_b = nrm((N_EVEN, (HY_ORDER + 1) * HY_WIDTH), 0.02)
    hy_w1 = nrm((N_EVEN, HY_EMB, HY_FFN), HY_EMB ** -0.5)
    hy_b1 = nrm((N_EVEN, HY_FFN), 0.02)
    hy_w2 = nrm((N_EVEN, HY_FFN, HY_FFN), HY_FFN ** -0.5)
    hy_b2 = nrm((N_EVEN, HY_FFN), 0.02)
    hy_w3 = nrm((N_EVEN, HY_FFN, HY_ORDER * 2 * HY_WIDTH), HY_FFN ** -0.5)
    hy_b3 = nrm((N_EVEN, HY_ORDER * 2 * HY_WIDTH), 0.02)
    hy_freq = 1.0 + nrm((N_EVEN, 2, HY_FFN), 0.02)
    hy_skip = nrm((N_EVEN, HY_ORDER, HY_WIDTH), 0.5)
    s5_a_re = -0.5 + nrm((N_EVEN, 2, S5_GROUPS, S5_STATE), 0.01)
    s5_a_im = math.pi * jnp.arange(S5_STATE, dtype=f32) + nrm((N_EVEN, 2, S5_GROUPS, S5_STATE), 0.01)
    s5_log_dt = math.log(S5_DT_MIN) + jax.random.uniform(next(keys), (N_EVEN, 2, S5_GROUPS), f32) * (math.log(S5_DT_MAX) - math.log(S5_DT_MIN))
    s5_b_re = nrm((N_EVEN, 2, S5_GROUPS, S5_STATE, S5_GROUP), (2 * S5_GROUP) ** -0.5)
    s5_b_im = nrm((N_EVEN, 2, S5_GROUPS, S5_STATE, S5_GROUP), (2 * S5_GROUP) ** -0.5)
    s5_c_re = nrm((N_EVEN, 2, S5_GROUPS, S5_GROUP, S5_STATE), (2 * S5_STATE) ** -0.5)
    s5_c_im = nrm((N_EVEN, 2, S5_GROUPS, S5_GROUP, S5_STATE), (2 * S5_STATE) ** -0.5)
    s5_d = nrm((N_EVEN, S5_WIDTH), 1.0)
    s5_glu_w = nrm((N_EVEN, S5_WIDTH, S5_WIDTH), S5_WIDTH ** -0.5)
    s5_glu_b = nrm((N_EVEN, S5_WIDTH), 0.02)
    od_in_w = nrm((N_ODD, D, ODD_IN), D ** -0.5)
    od_out_w = nrm((N_ODD, DA_V, D), DA_V ** -0.5)
    da_q_norm = 1.0 + nrm((N_ODD, DA_HEAD), 0.02)
    da_k_norm = 1.0 + nrm((N_ODD, DA_HEAD), 0.02)
    da_lq1 = nrm((N_ODD, DA_HEAD), 0.1)
    da_lk1 = nrm((N_ODD, DA_HEAD), 0.1)
    da_lq2 = nrm((N_ODD, DA_HEAD), 0.1)
    da_lk2 = nrm((N_ODD, DA_HEAD), 0.1)
    da_subln = 1.0 + nrm((N_ODD, 2 * DA_HEAD), 0.02)
    return {'x': x, 'c': c, 'ctx': ctx, 'c_ctx': c_ctx,
            'mod_w': mod_w, 'mod_b': mod_b, 'norm_w': norm_w,
            'ev_in_w': ev_in_w, 'ev_out_w': ev_out_w,
            'hy_conv_w': hy_conv_w, 'hy_conv_b': hy_conv_b,
            'hy_w1': hy_w1, 'hy_b1': hy_b1, 'hy_w2': hy_w2, 'hy_b2': hy_b2, 'hy_w3': hy_w3, 'hy_b3': hy_b3,
            'hy_freq': hy_freq, 'hy_skip': hy_skip,
            's5_a_re': s5_a_re, 's5_a_im': s5_a_im, 's5_log_dt': s5_log_dt,
            's5_b_re': s5_b_re, 's5_b_im': s5_b_im, 's5_c_re': s5_c_re, 's5_c_im': s5_c_im,
            's5_d': s5_d, 's5_glu_w': s5_glu_w, 's5_glu_b': s5_glu_b,
            'od_in_w': od_in_w, 'od_out_w': od_out_w, 'da_q_norm': da_q_norm, 'da_k_norm': da_k_norm,
            'da_lq1': da_lq1, 'da_lk1': da_lk1, 'da_lq2': da_lq2, 'da_lk2': da_lk2, 'da_subln': da_subln}


def reference(x, c, ctx, c_ctx, mod_w, mod_b, norm_w, ev_in_w, ev_out_w, hy_conv_w, hy_conv_b,
              hy_w1, hy_b1, hy_w2, hy_b2, hy_w3, hy_b3, hy_freq, hy_skip,
              s5_a_re, s5_a_im, s5_log_dt, s5_b_re, s5_b_im, s5_c_re, s5_c_im, s5_d, s5_glu_w, s5_glu_b,
              od_in_w, od_out_w, da_q_norm, da_k_norm, da_lq1, da_lk1, da_lq2, da_lk2, da_subln):
    for layer in range(DEPTH):
        last = layer == DEPTH - 1
        i = layer // 2
        mod = jax.nn.silu(c) @ mod_w[layer] + mod_b[layer]
        mod_c = jax.nn.silu(c_ctx) @ mod_w[layer] + mod_b[layer]
        shift, scale, gate = jnp.split(mod, 3, axis=-1)
        shift_c, scale_c, gate_c = jnp.split(mod_c, 3, axis=-1)
        h_lat = rms_norm(x, norm_w[layer]) * (1.0 + scale[:, None, :]) + shift[:, None, :]
        h_ctx = rms_norm(ctx, norm_w[layer]) * (1.0 + scale_c) + shift_c
        if layer % 2 == 0:
            out_l, out_c = even_mixer(h_lat, h_ctx, ev_in_w[i], ev_out_w[i], hy_conv_w[i], hy_conv_b[i],
                                      hy_w1[i], hy_b1[i], hy_w2[i], hy_b2[i], hy_w3[i], hy_b3[i], hy_freq[i], hy_skip[i],
                                      s5_a_re[i], s5_a_im[i], s5_log_dt[i], s5_b_re[i], s5_b_im[i],
                                      s5_c_re[i], s5_c_im[i], s5_d[i], s5_glu_w[i], s5_glu_b[i])
        else:
            out_l, out_c = odd_mixer(h_lat, h_ctx, layer, not last, od_in_w[i], od_out_w[i],
                                     da_q_norm[i], da_k_norm[i], da_lq1[i], da_lk1[i], da_lq2[i], da_lk2[i], da_subln[i])
        x = x + gate[:, None, :] * out_l
        if not last:
            ctx = ctx + gate_c * out_c
    return x
```

```python
import math
from contextlib import ExitStack

import numpy as np
import concourse.bass as bass
import concourse.mybir as mybir
from concourse.bass_utils import run_bass_kernel_spmd

F32 = mybir.dt.float32
BF16 = mybir.dt.bfloat16
I32 = mybir.dt.int32
AF = mybir.ActivationFunctionType
ALU = mybir.AluOpType
AX = mybir.AxisListType

N_DMA_SEMS = 6
D = 2048
NLAT = 4096
NCTX = 256
NTOK = NLAT + NCTX
PAIRS = [[0, 1], [2, 3], [4, 5], [6, 7]]
SAMEH = [[0, 2, 4, 6], [1, 3, 5, 7]]
ALL8 = [list(range(8))]
TWO_PI = 2.0 * math.pi


class Prog:
    ENGS = ("pe", "act", "dve", "pool", "sp")

    def __init__(self, nc):
        self.nc = nc
        self.ops = []
        self.last_w = {}
        self.readers = {}
        self.cnt = {}
        self.epoch = 0
        self.dma_rr = {e: 0 for e in self.ENGS}
        self.dma_cnt = {}
        self.dma_last = {}
        self.cc_cnt = 0
        self.last_on = {}
        self.bar = set()
        self.bar_seen = {}

    def _deps(self, eng, reads, writes):
        deps = set()
        for k in reads:
            if k in self.last_w:
                deps.add(self.last_w[k])
        for k in writes:
            if k in self.last_w:
                deps.add(self.last_w[k])
            for r in self.readers.get(k, ()):
                deps.add(r)
        if self.bar and self.bar_seen.get(eng) is not self.bar:
            deps |= self.bar
            self.bar_seen[eng] = self.bar
        return deps

    def _commit(self, oid, eng, reads, writes):
        for k in reads:
            self.readers.setdefault(k, []).append(oid)
        for k in writes:
            self.last_w[k] = oid
            self.readers[k] = []
        self.last_on[eng] = oid

    def barrier(self):
        b = set(self.last_on.values()) | set(self.dma_last.values())
        self.bar = frozenset(b)
        self.bar_seen = {}
        self.last_w = {}
        self.readers = {}

    def op(self, eng, fn, reads=(), writes=()):
        deps = self._deps(eng, reads, writes)
        oid = len(self.ops)
        self.tot = getattr(self, "tot", {})
        self.tot[eng] = self.tot.get(eng, 0) + 1
        ck = (eng, (self.tot[eng] - 1) // 30000)
        self.cnt[ck] = self.cnt.get(ck, 0) + 1
        self.ops.append(dict(eng=eng, fn=fn, deps=deps, kind="c", sem=("c",) + ck, val=self.cnt[ck]))
        self._commit(oid, eng, reads, writes)
        return oid

    def dma(self, eng, fn, reads=(), writes=()):
        deps = self._deps(eng, reads, writes)
        oid = len(self.ops)
        k = self.dma_rr[eng]
        self.dma_rr[eng] = (k + 1) % N_DMA_SEMS
        key = (eng, k)
        if key in self.dma_last:
            deps.add(self.dma_last[key])
        self.dma_cnt[key] = self.dma_cnt.get(key, 0) + 1
        self.dma_last[key] = oid
        self.ops.append(dict(eng=eng, fn=fn, deps=deps, kind="d", sem=("d",) + key, val=16 * self.dma_cnt[key]))
        self._commit(oid, eng, reads, writes)
        return oid

    def cc(self, fn, reads=(), writes=()):
        deps = self._deps("pool", reads, writes)
        oid = len(self.ops)
        key = ("pool", "cc")
        if key in self.dma_last:
            deps.add(self.dma_last[key])
        self.cc_cnt += 1
        self.dma_last[key] = oid
        self.ops.append(dict(eng="pool", fn=fn, deps=deps, kind="cc", sem=("cc",), val=self.cc_cnt))
        self._commit(oid, "pool", reads, writes)
        return oid

    def emit(self):
        nc = self.nc
        ops = self.ops
        with ExitStack() as st:
            sems = {}
            for ck in self.cnt:
                sems[("c",) + ck] = st.enter_context(nc.semaphore(f"c_{ck[0]}_{ck[1]}"))
            for key in self.dma_cnt:
                sems[("d",) + key] = st.enter_context(nc.semaphore(f"d_{key[0]}_{key[1]}"))
            if self.cc_cnt:
                sems[("cc",)] = st.enter_context(nc.semaphore("cc_sem"))
            block = st.enter_context(nc.Block())

            def run(eng_name):
                def body(eng):
                    waited = {}
                    for op in ops:
                        if op["eng"] != eng_name:
                            continue
                        need = {}
                        for d in op["deps"]:
                            dop = ops[d]
                            if dop["eng"] == "pe" and eng_name == "pe" and dop["kind"] == "c" and op["kind"] == "c":
                                continue
                            s = dop["sem"]
                            if dop["val"] > need.get(s, 0):
                                need[s] = dop["val"]
                        for s, v in need.items():
                            if waited.get(s, 0) >= v:
                                continue
                            eng.wait_ge(sems[s], v)
                            waited[s] = v
                        ins = op["fn"](eng)
                        if op["kind"] == "cc":
                            ins.then_inc(sems[op["sem"]])
                        else:
                            ins.then_inc(sems[op["sem"]], 16 if op["kind"] == "d" else 1)
                    if eng_name == "sp":
                        for key, c in self.dma_cnt.items():
                            eng.wait_ge(sems[("d",) + key], 16 * c)
                        if self.cc_cnt:
                            eng.wait_ge(sems[("cc",)], self.cc_cnt)
                        for ck, c in self.cnt.items():
                            eng.wait_ge(sems[("c",) + ck], c)
                return body

            block.sync(run("sp"))
            block.tensor(run("pe"))
            block.scalar(run("act"))
            block.vector(run("dve"))
            block.gpsimd(run("pool"))


class T:
    _n = 0

    def __init__(self, ap, name, psum_banks=None):
        T._n += 1
        self.ap = ap
        self.key = f"{name}#{T._n}"
        self.psum_banks = psum_banks

    def __getitem__(self, idx):
        return self.ap[idx]


SB_BYTES = 204 * 1024


class KB:
    def __init__(self, ext_in=(), ext_out=()):
        self.nc = bass.Bass("TRN2", target_bir_lowering=False)
        self.P = Prog(self.nc)
        self.ext_in = set(ext_in)
        self.ext_out = set(ext_out)
        self.D = {}
        self.st = ExitStack()
        self.arena = self.st.enter_context(self.nc.sbuf_tensor("arena", [128, SB_BYTES // 2], BF16))
        self.parena = self.st.enter_context(self.nc.psum_tensor("parena", [128, 8 * 1024], BF16))
        self.sb_off = 0
        self.sb_base = 0
        self.rr = {}

    def dram(self, name, shape, dt=F32):
        if name in self.D:
            return self.D[name]
        kind = "ExternalInput" if name in self.ext_in else ("ExternalOutput" if name in self.ext_out else "Internal")
        t = self.nc.dram_tensor(name, list(shape), dt, kind=kind)
        self.D[name] = t.ap()
        return self.D[name]

    @staticmethod
    def _view(base, off, shape, dt):
        n = int(np.prod(shape[1:]))
        sz = 4 if dt in (F32, I32) else 2
        assert off % 4 == 0
        v = base[0:shape[0], off // 2: off // 2 + n * sz // 2]
        if dt != BF16:
            v = v.bitcast(dt)
        if len(shape) == 3:
            v = v.rearrange("p (a b) -> p a b", a=shape[1])
        elif len(shape) == 4:
            v = v.rearrange("p (a b c) -> p a b c", a=shape[1], b=shape[2])
        return v

    def sb(self, name, shape, dt=F32):
        n = int(np.prod(shape[1:]))
        sz = 4 if dt in (F32, I32) else 2
        nbytes = (n * sz + 31) // 32 * 32
        off = self.sb_off
        self.sb_off += nbytes
        assert self.sb_off <= SB_BYTES, f"SBUF arena overflow allocating {name}: {self.sb_off}"
        return T(self._view(self.arena, off, shape, dt), name)

    def ps(self, name, bank, shape, dt=F32, off=0):
        n = int(np.prod(shape[1:]))
        sz = 4 if dt in (F32, I32) else 2
        nb = (off + n * sz + 2047) // 2048
        return T(self._view(self.parena, bank * 2048 + off, shape, dt), name, psum_banks=list(range(bank, bank + nb)))

    def phase(self, persistent=False):
        self.P.barrier()
        if persistent:
            self.sb_base = self.sb_off
        self.sb_off = self.sb_base

    def q(self, group="ld"):
        order = ("sp", "act", "pool")
        i = self.rr.get(group, 0)
        self.rr[group] = i + 1
        return order[i % len(order)]

    def finish(self):
        self.P.emit()
        self.st.close()
        return self.nc


def _keys(reads, writes):
    rk, wk = [], []
    for t in reads:
        if isinstance(t, T):
            if t.psum_banks is not None:
                wk += [f"psb{b}" for b in t.psum_banks]
            else:
                rk.append(t.key)
        else:
            rk.append(t)
    for t in writes:
        if isinstance(t, T):
            if t.psum_banks is not None:
                wk += [f"psb{b}" for b in t.psum_banks]
            else:
                wk.append(t.key)
        else:
            wk.append(t)
    return rk, wk


def E(P, eng, method, reads=(), writes=(), **kw):
    rk, wk = _keys(reads, writes)
    return P.op(eng, lambda e: getattr(e, method)(**kw), rk, wk)


def DMA(P, eng, out, in_, reads=(), writes=(), **kw):
    rk, wk = _keys(reads, writes)
    return P.dma(eng, lambda e: e.dma_start(out=out, in_=in_, **kw), rk, wk)


AG_MAX_BYTES = 1 << 20


def ag_rows(rows, cols, dt):
    sz = 4 if dt in (F32, I32) else 2
    rc = rows
    while rc * cols * sz > AG_MAX_BYTES:
        assert rc % 2 == 0
        rc //= 2
    return rc


def allgather(kb, src_name, dst_name, rows, cols, dt, groups, rc=None):
    P = kb.P
    R = len(groups[0])
    rc = rc or ag_rows(rows, cols, dt)
    nch = rows // rc
    src = kb.dram(src_name, [rows, cols], dt)
    dst = kb.dram(dst_name, [nch * R * rc, cols], dt)
    for ch in range(nch):
        P.cc(lambda e, ch=ch: e.collective_compute("AllGather", ALU.bypass, replica_groups=groups, ins=[src[ch * rc:(ch + 1) * rc, :]],
                                                   outs=[dst[ch * R * rc:(ch + 1) * R * rc, :]]),
             reads=[src_name], writes=[dst_name])
    return nch, rc


def setup_consts(kb):
    P = kb.P
    c = {}
    idf = kb.sb("idf", [128, 128], F32)
    idb = kb.sb("idb", [128, 128], BF16)
    E(P, "pool", "memset", writes=[idf], ap=idf.ap, constant=0.0)
    E(P, "pool", "affine_select", reads=[idf], writes=[idf], out=idf.ap, in_=idf.ap, pattern=[[-1, 128]],
      compare_op=ALU.not_equal, fill=1.0, base=0, channel_multiplier=1)
    E(P, "dve", "tensor_copy", reads=[idf], writes=[idb], out=idb.ap, in_=idf.ap)
    c["idf"] = idf
    c["idb"] = idb
    kb.C = c
    kb.sb_base = kb.sb_off


def phase_mod(kb):
    P = kb.P
    kb.phase()
    cvec = kb.dram("cvec", [5, D])
    modw = kb.dram("modw", [2, D, 768])
    modb = kb.dram("modb", [2, 768])
    modp = kb.dram("modp", [10, 768])
    modg = kb.dram("modg", [80, 768])
    cT = kb.sb("cT", [128, 5, 16])
    cS = kb.sb("cS", [128, 16, 5])
    DMA(P, "sp", cT.ap, cvec.rearrange("b (p k) -> p b k", p=128), writes=[cT])
    E(P, "act", "activation", reads=[cT], writes=[cS], out=cS.ap, in_=cT.ap.rearrange("p b k -> p k b"), func=AF.Silu)
    for l in range(2):
        W = kb.sb(f"modW{l}", [128, 16, 768])
        bias = kb.sb(f"modbias{l}", [5, 768])
        res = kb.sb(f"modres{l}", [5, 768])
        for kq in range(4):
            DMA(P, kb.q(), W[:, kq * 4:(kq + 1) * 4, :], modw[l].rearrange("(p k) n -> p k n", p=128)[:, kq * 4:(kq + 1) * 4, :],
                writes=[W.key + f":{kq}"])
        DMA(P, "pool", bias.ap, modb[l:l + 1, :].broadcast_to([5, 768]), writes=[bias])
        for (n0, nn, bank) in ((0, 512, 0), (512, 256, 1)):
            ps = kb.ps(f"modps{l}_{n0}", bank + 2 * l, [5, nn])
            for k in range(16):
                E(P, "pe", "matmul", reads=[cS, W.key + f":{k // 4}"], writes=[ps], out=ps.ap, lhsT=cS[:, k, :], rhs=W[:, k, n0:n0 + nn],
                  start=(k == 0), stop=(k == 15))
            E(P, "dve", "tensor_tensor", reads=[ps, bias], writes=[res.key + f":{n0}"], out=res[:, n0:n0 + nn], in0=ps.ap,
              in1=bias[:, n0:n0 + nn], op=ALU.add)
        DMA(P, "sp", modp[l * 5:(l + 1) * 5, :], res.ap, reads=[res.key + ":0", res.key + ":512"], writes=["modp"])
    modg2 = kb.dram("modg2", [20, 768])
    P.cc(lambda e: e.collective_compute("AllGather", ALU.bypass, replica_groups=PAIRS, ins=[modp[:, :]], outs=[modg2[:, :]]),
         reads=["modp"], writes=["modg2"])
    P.cc(lambda e: e.collective_compute("AllGather", ALU.bypass, replica_groups=SAMEH, ins=[modg2[:, :]], outs=[modg[:, :]]),
         reads=["modg2"], writes=["modg"])


def load_rows(kb, layer, rows, segs):
    P = kb.P
    modg = kb.D["modg"]
    src = modg.rearrange("(r q) (s j) -> q s r j", q=10, s=3)[layer * 5:(layer + 1) * 5]
    for i, s in enumerate(segs):
        DMA(P, kb.q(), rows[:, i], src[:, s], reads=["modg"], writes=[rows.key + f":{i}"])


def bcast_row(kb, rows, i, sel_ap, out_tile, psums):
    P = kb.P
    for nc4 in range(4):
        ps = psums[nc4 % len(psums)]
        E(P, "pe", "matmul", reads=[rows.key + f":{i}", "sel"], writes=[ps], out=ps.ap, lhsT=sel_ap,
          rhs=rows[:, i].rearrange("q r j -> q (r j)")[:, nc4 * 512:(nc4 + 1) * 512], start=True, stop=True)
        E(P, "act", "activation", reads=[ps], writes=[out_tile.key + f":{nc4}"], out=out_tile[:, nc4 * 512:(nc4 + 1) * 512], in_=ps.ap,
          func=AF.Copy)


def allkeys(t, n=4):
    return [t.key + f":{i}" for i in range(n)]


def norm_tile(kb, src_ap, xt, junk, ssq, t1, hb, A, S, eng_alt):
    P = kb.P
    DMA(P, kb.q(), xt.ap, src_ap, writes=[xt])
    E(P, "act", "activation", reads=[xt], writes=[junk, ssq], out=junk.ap, in_=xt.ap, func=AF.Square, accum_out=ssq[:, 0:1])
    E(P, "dve", "tensor_scalar", reads=[ssq], writes=[ssq], out=ssq[:, 1:2], in0=ssq[:, 0:1], scalar1=1.0 / D, scalar2=1e-6,
      op0=ALU.mult, op1=ALU.add)
    E(P, "act", "activation", reads=[ssq], writes=[ssq], out=ssq[:, 2:3], in_=ssq[:, 1:2], func=AF.Sqrt)
    E(P, "dve", "reciprocal", reads=[ssq], writes=[ssq], out=ssq[:, 3:4], in_=ssq[:, 2:3])
    E(P, "dve", "scalar_tensor_tensor", reads=[xt, ssq] + allkeys(A), writes=[t1], out=t1.ap, in0=xt.ap, scalar=ssq[:, 3:4], in1=A.ap,
      op0=ALU.mult, op1=ALU.mult)
    E(P, "pool", "tensor_tensor", reads=[t1] + allkeys(S), writes=[hb], out=hb.ap, in0=t1.ap, in1=S.ap, op=ALU.add)


def phase_l0_pre(kb):
    P = kb.P
    kb.phase()
    x = kb.dram("x_b", [NLAT, D])
    ctx = kb.dram("ctx_b", [NCTX, D])
    normw = kb.dram("normw", [2, D])
    inw = kb.dram("inw0", [D, 3072])
    selm = kb.dram("selm", [2, 5, 128])
    proj = kb.dram("proj0", [3072, NTOK])
    idb = kb.C["idb"]

    W16 = kb.sb("W16", [128, 16, 3072], BF16)
    wst = kb.sb("wst", [128, 3072])
    A = kb.sb("A", [128, D])
    S = kb.sb("S", [128, D])
    sel = kb.sb("sel", [5, 2, 128])
    rows = kb.sb("rows", [5, 2, 8, 256])
    xts = [kb.sb(f"xt{i}", [128, D]) for i in range(2)]
    junk = kb.sb("junk", [128, D], BF16)
    t1 = kb.sb("t1", [128, D])
    nw = t1
    hbs = [kb.sb(f"hb{i}", [128, D], BF16) for i in range(2)]
    hT = kb.sb("hT", [128, 16, 512], BF16)
    osts = [kb.sb(f"ost{i}", [128, 512]) for i in range(3)]
    ssqs = [kb.sb(f"ssq{i}", [128, 4]) for i in range(2)]
    pTs = [kb.ps(f"pT{i}", 2 * i, [128, 16, 128], BF16) for i in range(2)]
    pjs = [kb.ps(f"pj{i}", 4 + i, [128, 512]) for i in range(4)]
    load_rows(kb, 0, rows, (0, 1))

    for k in range(16):
        DMA(P, kb.q(), wst.ap, inw[k * 128:(k + 1) * 128, :], writes=[wst])
        if k % 2 == 0:
            E(P, "pool", "tensor_copy", reads=[wst], writes=[W16.key + f":{k}"], out=W16[:, k, :], in_=wst.ap)
        else:
            E(P, "act", "activation", reads=[wst], writes=[W16.key + f":{k}"], out=W16[:, k, :], in_=wst.ap, func=AF.Copy)
    DMA(P, "sp", sel.ap, selm.rearrange("a q p -> q a p"), writes=["sel"])

    tiles = [("ctx", 0, 256, 0)] + [("lat", 512 * i, 512, 256 + 512 * i) for i in range(8)]
    cur = None
    ti = 0
    for (srcname, row0, ntok, col0) in tiles:
        if srcname != cur:
            cur = srcname
            sel_ap = sel[:, 0 if srcname == "lat" else 1, :]
            bcast_row(kb, rows, 0, sel_ap, S, pjs)
            bcast_row(kb, rows, 1, sel_ap, A, pjs)
            DMA(P, "act", nw.ap, normw[0:1, :].broadcast_to([128, D]), writes=[nw])
            E(P, "dve", "scalar_tensor_tensor", reads=allkeys(A) + [nw], writes=allkeys(A), out=A.ap, in0=A.ap, scalar=1.0, in1=nw.ap,
              op0=ALU.add, op1=ALU.mult)
        src = x if srcname == "lat" else ctx
        for tt in range(ntok // 128):
            xt = xts[ti % 2]
            hb = hbs[ti % 2]
            ssq = ssqs[ti % 2]
            norm_tile(kb, src[row0 + tt * 128: row0 + (tt + 1) * 128, :], xt, junk, ssq, t1, hb, A, S, ti)
            pT = pTs[ti % 2]
            for k in range(16):
                E(P, "pe", "transpose", reads=[hb, idb], writes=[pT], out=pT[:, k, :], in_=hb[:, k * 128:(k + 1) * 128], identity=idb.ap)
            if ti % 2 == 0:
                E(P, "act", "activation", reads=[pT], writes=[hT.key + f":{tt}"], out=hT[:, :, tt * 128:(tt + 1) * 128], in_=pT.ap, func=AF.Copy)
            else:
                E(P, "dve", "tensor_copy", reads=[pT], writes=[hT.key + f":{tt}"], out=hT[:, :, tt * 128:(tt + 1) * 128], in_=pT.ap)
            ti += 1
        hkeys = [hT.key + f":{tt}" for tt in range(ntok // 128)]
        for cc in range(24):
            ps = pjs[cc % 4]
            for k in range(16):
                E(P, "pe", "matmul", reads=hkeys + [W16.key + f":{k}"], writes=[ps], out=ps[:, 0:ntok], lhsT=W16[:, k, cc * 128:(cc + 1) * 128],
                  rhs=hT[:, k, 0:ntok], start=(k == 0), stop=(k == 15))
            ost = osts[cc % 3]
            if cc % 2 == 0:
                E(P, "act", "activation", reads=[ps], writes=[ost], out=ost[:, 0:ntok], in_=ps[:, 0:ntok], func=AF.Copy)
            else:
                E(P, "dve", "tensor_copy", reads=[ps], writes=[ost], out=ost[:, 0:ntok], in_=ps[:, 0:ntok])
            DMA(P, kb.q("st"), proj[cc * 128:(cc + 1) * 128, col0:col0 + ntok], ost[:, 0:ntok], reads=[ost], writes=["proj0"])


def seg_cols(h, nseg, seg_w=1024, half=512):
    return np.concatenate([np.arange(s * seg_w + half * h, s * seg_w + half * (h + 1)) for s in range(nseg)])


def host_inputs(inp):
    f32 = np.float32
    per_core = []
    cvec = np.concatenate([inp["c"], inp["c_ctx"][None, :]], 0).astype(f32)
    for r in range(8):
        b, h = r // 2, r % 2
        m = {}
        m["cvec"] = cvec
        mcols = np.concatenate([np.arange(2048 * s + 256 * r, 2048 * s + 256 * (r + 1)) for s in range(3)])
        m["modw"] = np.ascontiguousarray(inp["mod_w"][:, :, mcols])
        m["modb"] = np.ascontiguousarray(inp["mod_b"][:, mcols])
        m["x_b"] = np.ascontiguousarray(inp["x"][b])
        m["ctx_b"] = np.ascontiguousarray(inp["ctx"][b])
        m["normw"] = np.ascontiguousarray(inp["norm_w"])
        m["inw0"] = np.ascontiguousarray(inp["ev_in_w"][0][:, seg_cols(h, 6)])
        selm = np.zeros((2, 5, 128), f32)
        selm[0, b, :] = 1.0
        selm[1, 4, :] = 1.0
        m["selm"] = selm
        cq = 512 * h + 128 * b + np.arange(128)
        w3idx = np.concatenate([o * 2048 + d * 1024 + cq for o in range(2) for d in range(2)])
        m["hy_w1"] = np.ascontiguousarray(inp["hy_w1"][0]); m["hy_b1"] = np.ascontiguousarray(inp["hy_b1"][0][:, None])
        m["hy_w2"] = np.ascontiguousarray(inp["hy_w2"][0]); m["hy_b2"] = np.ascontiguousarray(inp["hy_b2"][0][:, None])
        m["hy_w3s"] = np.ascontiguousarray(inp["hy_w3"][0][:, w3idx]); m["hy_b3s"] = np.ascontiguousarray(inp["hy_b3"][0][None, w3idx])
        m["hy_freqT"] = np.ascontiguousarray(inp["hy_freq"][0].T)
        m["hy_skips"] = np.ascontiguousarray(inp["hy_skip"][0][:, cq].reshape(1, 256))
        m["hy_delta"] = np.ascontiguousarray(hy_deltas()[cq][None, :])
        mask0 = np.ones((128, 1), f32); mask0[0, 0] = 0.0
        m["hy_mask0"] = mask0
        for tag, n in (("l", NLAT), ("c", NCTX)):
            hc = hy_consts(n)
            for k in ("featT", "ntn", "cphi", "sphi", "ncphi", "gc", "gs"):
                m[k + tag] = hc[k]
        ccols = np.concatenate([s_ * 1024 + 512 * h + np.arange(512) for s_ in range(3)])
        m["hy_cwT"] = np.ascontiguousarray(inp["hy_conv_w"][0][:, ccols].T)
        m["hy_cbs"] = np.ascontiguousarray(inp["hy_conv_b"][0][ccols][:, None])
        gs_ = slice(32 * h, 32 * h + 32)
        def chp(a):
            return np.ascontiguousarray(a[0][:, gs_].transpose(0, 1, 3, 2).reshape(2, 512, 64))
        m["s5_bre_chp"] = chp(inp["s5_b_re"]); m["s5_bim_chp"] = chp(inp["s5_b_im"])
        rep = lambda a: np.ascontiguousarray(np.repeat(a[0][:, gs_], 16, axis=1))
        m["s5_are_chp"] = rep(inp["s5_a_re"]); m["s5_aim_chp"] = rep(inp["s5_a_im"])
        m["s5_ldt_chp"] = np.ascontiguousarray(np.repeat(inp["s5_log_dt"][0][:, gs_], 16, axis=1)[:, :, None])
        pch = lambda a: np.ascontiguousarray(a[0][:, gs_].transpose(0, 3, 1, 2).reshape(2, 64, 512))
        m["s5_cre_pch"] = pch(inp["s5_c_re"]); m["s5_cim_pch"] = pch(inp["s5_c_im"])
        pg = lambda a: np.ascontiguousarray(np.concatenate([a[0][:, gs_].transpose(0, 2, 1)] * 2, axis=1))
        m["s5_are_pg"] = pg(inp["s5_a_re"]); m["s5_aim_pg"] = pg(inp["s5_a_im"])
        m["s5_ldt_row"] = np.ascontiguousarray(inp["s5_log_dt"][0][:, None, gs_])
        m["s5_ds"] = np.ascontiguousarray(inp["s5_d"][0][512 * h:512 * h + 512, None])
        perm = np.zeros((128, 128), f32); perm[np.arange(128), (np.arange(128) + 64) % 128] = 1.0
        m["s5_perm"] = perm
        sg_ = np.ones((128, 2), f32); sg_[64:, 0] = -1.0; sg_[:64, 1] = -1.0
        m["s5_sgn"] = sg_
        m["s5_tau"] = np.arange(S5_HALF, dtype=f32)[None, :]
        gm = np.zeros((128, 8), f32); gm[np.arange(128), np.arange(128) // 16] = 1.0
        m["s5_gmask"] = gm
        prow = np.concatenate([np.concatenate([64 * k + np.arange(64), 512 + 64 * k + np.arange(64)]) for k in range(8)])
        m["s5_gluw"] = np.ascontiguousarray(inp["s5_glu_w"][0][prow][:, 512 * h:512 * h + 512])
        m["s5_glub"] = np.ascontiguousarray(inp["s5_glu_b"][0][512 * h:512 * h + 512, None])
        krow = []
        for k in range(16):
            for r_ in range(2):
                loc = 64 * k + np.arange(64)
                krow.append(np.where(loc < 512, 512 * r_ + loc, 1024 + 512 * r_ + (loc - 512)))
        krow = np.concatenate(krow)
        m["outw0"] = np.ascontiguousarray(inp["ev_out_w"][0][krow][:, 1024 * h:1024 * h + 1024])
        m["xh0"] = np.ascontiguousarray(np.concatenate([inp["ctx"][b], inp["x"][b]], 0)[:, 1024 * h:1024 * h + 1024])
        hs = np.zeros((128, 2), f32); hs[:, h] = 1.0
        m["hsel"] = hs
        W1 = inp["od_in_w"][0]
        hc = np.arange(1024 * h, 1024 * h + 1024)
        m["inw1qk"] = np.ascontiguousarray(np.concatenate([W1[:, hc], W1[:, 2048 + hc]], 1))
        m["inw1vg"] = np.ascontiguousarray(np.concatenate([W1[:, 4096 + hc], W1[:, 6144 + hc]], 1))
        rc_, rs_ = rope_tables()
        m["ropeC"] = rc_; m["ropeS"] = rs_
        m["qkw"] = np.ascontiguousarray(np.stack([np.tile(inp["da_q_norm"][0], 8), np.tile(inp["da_k_norm"][0], 8)], 0))
        m["da_lqk"] = np.ascontiguousarray(np.stack([inp["da_lq1"][0], inp["da_lk1"][0], inp["da_lq2"][0], inp["da_lk2"][0]], 0))
        m["da_sublnw"] = np.ascontiguousarray(inp["da_subln"][0][:, None])
        k1 = np.concatenate([1024 * r_ + 128 * ch + np.arange(128) for ch in range(8) for r_ in range(2)])
        m["outw1"] = np.ascontiguousarray(inp["od_out_w"][0][k1][:, 1024 * h:1024 * h + 1024])
        per_core.append(m)
    return per_core


import ml_dtypes
_BF = ml_dtypes.bfloat16
_CONST_CACHE = {}


def hy_consts(n):
    if n in _CONST_CACHE:
        return _CONST_CACHE[n]
    N = 2 * n
    nch = n // 128
    a = np.arange(n, dtype=np.float64) + 0.5
    ang = 2.0 * np.pi * np.outer(a, a) / N
    out = {}
    for nm, fn in (("gc", np.cos), ("gs", np.sin)):
        g = fn(ang).astype(np.float32)
        g4 = g.reshape(nch, 128, nch, 128).transpose(2, 1, 0, 3)
        out[nm] = np.ascontiguousarray(g4).astype(_BF)
    phi = np.pi * a / N
    lay = lambda v: np.ascontiguousarray(v.reshape(nch, 128).T.astype(np.float32))
    out["cphi"] = lay(np.cos(phi))
    out["sphi"] = lay(np.sin(phi))
    out["ncphi"] = lay(-np.cos(phi))
    f32 = np.float32
    t = np.arange(n, dtype=f32)
    tn = (t / f32(n)).astype(f32)
    bands = np.linspace(1e-4, 15, 16, dtype=f32)
    angf = (f32(2.0 * math.pi / n) * t[:, None] * bands[None, :]).astype(f32)
    feat = np.concatenate([tn[:, None], np.cos(angf), -np.sin(angf)], axis=-1).astype(f32)
    out["featT"] = np.ascontiguousarray(feat.T)
    out["ntn"] = lay(-tn)
    _CONST_CACHE[n] = out
    return out


def hy_deltas():
    f32 = np.float32
    return np.abs(np.linspace(math.log(1e-2) / 1.5, math.log(1e-2) / 0.3, 1024, dtype=f32)).astype(f32)


def phase_filter(kb, n, tag):
    P = kb.P
    kb.phase()
    nlc = n // 128
    N = 2 * n
    blk = min(512, n)
    w1d = kb.dram("hy_w1", [33, 64]); b1d = kb.dram("hy_b1", [64, 1]); w2d = kb.dram("hy_w2", [64, 64]); b2d = kb.dram("hy_b2", [64, 1])
    w3d = kb.dram("hy_w3s", [64, 512]); b3d = kb.dram("hy_b3s", [1, 512]); frd = kb.dram("hy_freqT", [64, 2]); skd = kb.dram("hy_skips", [1, 256])
    dld = kb.dram("hy_delta", [1, 128]); m0d = kb.dram("hy_mask0", [128, 1])
    featd = kb.dram(f"featT{tag}", [33, n]); ntnd = kb.dram(f"ntn{tag}", [128, nlc])
    cphd = kb.dram(f"cphi{tag}", [128, nlc]); sphd = kb.dram(f"sphi{tag}", [128, nlc]); ncphd = kb.dram(f"ncphi{tag}", [128, nlc])
    gcd = kb.dram(f"gc{tag}", [nlc, 128, nlc, 128], BF16); gsd = kb.dram(f"gs{tag}", [nlc, 128, nlc, 128], BF16)
    KF = kb.dram(f"KF{tag}", [2 * n, 256])

    w1 = kb.sb("w1", [33, 64]); w2 = kb.sb("w2", [64, 64]); w3 = kb.sb("w3", [64, 512])
    b12 = kb.sb("b12", [64, 2]); fr = kb.sb("fr", [64, 2]); fs = kb.sb("fs", [64, 2]); fb = kb.sb("fb", [64, 2])
    b3r = kb.sb("b3r", [128, 512]); skr = kb.sb("skr", [128, 256]); dlr = kb.sb("dlr", [128, 128]); m0 = kb.sb("m0", [128, 1])
    ntn = kb.sb("ntn", [128, nlc]); cph = kb.sb("cph", [128, nlc]); sph = kb.sb("sph", [128, nlc]); ncph = kb.sb("ncph", [128, nlc])
    ones = kb.sb("ones", [128, 128])
    feat = kb.sb("feat", [33, n]); h1T = kb.sb("h1T", [64, n]); h2T = kb.sb("h2T", [64, n])
    tt = kb.sb("tt", [64, 512]); ti = kb.sb("ti", [64, 512], I32)
    hall = kb.sb("hall", [128, nlc, 512])
    dec = kb.sb("dec", [128, 128]); absh = kb.sb("absh", [128, 512])
    rs = kb.sb("rs", [128, 256])
    Pp = kb.sb("Pp", [128, nlc, 256], BF16); Pm = kb.sb("Pm", [128, nlc, 256], BF16)
    tmpa = kb.sb("tmpa", [128, 256]); tmpb = kb.sb("tmpb", [128, 256])
    G = [[kb.sb(f"G{i}{j}", [128, nlc, 128], BF16) for j in range(2)] for i in range(2)]
    ko = [[kb.sb(f"ko{i}{j}", [128, 256]) for j in range(2)] for i in range(2)]
    psA = kb.ps("psA", 0, [64, 512]); psB = kb.ps("psB", 1, [128, 512]); psL = kb.ps("psL", 2, [128, 512])
    psT = [kb.ps(f"psT{i}", 3 + i, [128, 256]) for i in range(4)]

    for (t_, d_) in ((w1, w1d), (w2, w2d), (w3, w3d), (m0, m0d), (ntn, ntnd), (cph, cphd), (sph, sphd), (ncph, ncphd), (feat, featd)):
        DMA(P, kb.q(), t_.ap, d_, writes=[t_])
    DMA(P, kb.q(), b12[:, 0:1], b1d, writes=[b12.key + ":0"]); DMA(P, kb.q(), b12[:, 1:2], b2d, writes=[b12.key + ":1"])
    DMA(P, kb.q(), fr.ap, frd, writes=[fr])
    DMA(P, kb.q(), b3r.ap, b3d.broadcast_to([128, 512]), writes=[b3r])
    DMA(P, kb.q(), skr.ap, skd.broadcast_to([128, 256]), writes=[skr])
    DMA(P, kb.q(), dlr.ap, dld.broadcast_to([128, 128]), writes=[dlr])
    E(P, "pool", "memset", writes=[ones], ap=ones.ap, constant=1.0)
    E(P, "dve", "tensor_scalar", reads=[fr], writes=[fs], out=fs.ap, in0=fr.ap, scalar1=1.0 / TWO_PI, scalar2=None, op0=ALU.mult)
    E(P, "dve", "tensor_tensor", reads=[fs, b12.key + ":0", b12.key + ":1"], writes=[fb], out=fb.ap, in0=fs.ap, in1=b12.ap, op=ALU.mult)
    E(P, "dve", "tensor_scalar", reads=[skr], writes=[skr], out=skr.ap, in0=skr.ap, scalar1=2.0 / N, scalar2=None, op0=ALU.mult)

    import os
    _stop = int(os.environ.get("FILT_STOP", "99"))
    if _stop <= 0:
        return
    for layer, (wt, src, dst) in enumerate(((w1, feat, h1T), (w2, h1T, h2T))):
        for b0 in range(0, n, blk):
            E(P, "pe", "matmul", reads=[wt, src], writes=[psA], out=psA[:, 0:blk], lhsT=wt.ap, rhs=src[:, b0:b0 + blk], start=True, stop=True)
            E(P, "dve", "tensor_scalar", reads=[psA, fs, fb], writes=[tt], out=tt[:, 0:blk], in0=psA[:, 0:blk], scalar1=fs[:, layer:layer + 1],
              scalar2=fb[:, layer:layer + 1], op0=ALU.mult, op1=ALU.add)
            E(P, "dve", "tensor_copy", reads=[tt], writes=[ti], out=ti[:, 0:blk], in_=tt[:, 0:blk])
            E(P, "dve", "tensor_tensor", reads=[tt, ti], writes=[tt], out=tt[:, 0:blk], in0=tt[:, 0:blk], in1=ti[:, 0:blk], op=ALU.subtract)
            E(P, "act", "activation", reads=[tt], writes=[dst], out=dst[:, b0:b0 + blk], in_=tt[:, 0:blk], func=AF.Sin, scale=TWO_PI)
    if _stop <= 1:
        return
    for lc in range(nlc):
        E(P, "pe", "matmul", reads=[h2T, w3], writes=[psB], out=psB.ap, lhsT=h2T[:, lc * 128:(lc + 1) * 128], rhs=w3.ap, start=True, stop=True)
        hk = hall.key + f":{lc}"
        E(P, "dve", "tensor_tensor", reads=[psB, b3r], writes=[hk], out=hall[:, lc, :], in0=psB.ap, in1=b3r.ap, op=ALU.add)
        E(P, "act", "activation", reads=[dlr, ntn], writes=[dec], out=dec.ap, in_=dlr.ap, func=AF.Exp, scale=ntn[:, lc:lc + 1])
        for q4 in range(4):
            E(P, "pool" if q4 % 2 else "dve", "tensor_tensor", reads=[hk, dec], writes=[hk], out=hall[:, lc, q4 * 128:(q4 + 1) * 128],
              in0=hall[:, lc, q4 * 128:(q4 + 1) * 128], in1=dec.ap, op=ALU.mult)
        if lc == 0:
            for q4 in (1, 3):
                E(P, "dve", "tensor_scalar", reads=[hk, m0], writes=[hk], out=hall[:, 0, q4 * 128:(q4 + 1) * 128],
                  in0=hall[:, 0, q4 * 128:(q4 + 1) * 128], scalar1=m0[:, 0:1], scalar2=None, op0=ALU.mult)
        E(P, "act", "activation", reads=[hk], writes=[absh], out=absh.ap, in_=hall[:, lc, :], func=AF.Abs)
        E(P, "pe", "matmul", reads=[ones, absh], writes=[psL], out=psL.ap, lhsT=ones.ap, rhs=absh.ap, start=(lc == 0), stop=(lc == nlc - 1))
    if _stop <= 2:
        return
    l1v = psL.ap.rearrange("p (o d c) -> p o d c", o=2, d=2)
    rs3 = rs.ap.rearrange("p (o c) -> p o c", o=2)
    E(P, "act", "activation", reads=[psL], writes=[absh], out=absh.ap, in_=psL.ap, func=AF.Copy)
    ab4 = absh.ap.rearrange("p (o d c) -> p o d c", o=2, d=2)
    E(P, "dve", "tensor_tensor", reads=[absh], writes=[rs], out=rs3, in0=ab4[:, :, 0, :], in1=ab4[:, :, 1, :], op=ALU.add)
    E(P, "dve", "reciprocal", reads=[rs], writes=[rs], out=rs.ap, in_=rs.ap)
    E(P, "dve", "tensor_scalar", reads=[rs], writes=[rs], out=rs.ap, in0=rs.ap, scalar1=2.0 / N, scalar2=None, op0=ALU.mult)
    for lc in range(nlc):
        hk = hall.key + f":{lc}"
        h4 = hall[:, lc, :].rearrange("p (o d c) -> p o d c", o=2, d=2)
        E(P, "dve", "tensor_tensor", reads=[hk], writes=[tmpa], out=tmpa.ap.rearrange("p (o c) -> p o c", o=2), in0=h4[:, :, 0, :], in1=h4[:, :, 1, :], op=ALU.add)
        E(P, "pool", "tensor_tensor", reads=[hk], writes=[tmpb], out=tmpb.ap.rearrange("p (o c) -> p o c", o=2), in0=h4[:, :, 0, :], in1=h4[:, :, 1, :], op=ALU.subtract)
        E(P, "dve", "tensor_tensor", reads=[tmpa, rs], writes=[Pp.key + f":{lc}"], out=Pp[:, lc, :], in0=tmpa.ap, in1=rs.ap, op=ALU.mult)
        E(P, "pool", "tensor_tensor", reads=[tmpb, rs], writes=[Pm.key + f":{lc}"], out=Pm[:, lc, :], in0=tmpb.ap, in1=rs.ap, op=ALU.mult)
    if _stop <= 3:
        return
    pkeys = [Pp.key + f":{lc}" for lc in range(nlc)]
    mkeys = [Pm.key + f":{lc}" for lc in range(nlc)]
    for fc in range(nlc):
        gc_, gs_ = G[fc % 2]
        DMA(P, "sp", gc_.ap, gcd[fc], writes=[gc_])
        DMA(P, "act", gs_.ap, gsd[fc], writes=[gs_])
        for i, (g_, src, keys) in enumerate(((gc_, Pp, pkeys), (gs_, Pp, pkeys), (gc_, Pm, mkeys), (gs_, Pm, mkeys))):
            for lc in range(nlc):
                E(P, "pe", "matmul", reads=[g_] + keys, writes=[psT[i]], out=psT[i].ap, lhsT=g_[:, lc, :], rhs=src[:, lc, :], start=(lc == 0),
                  stop=(lc == nlc - 1))
        kre, kim = ko[fc % 2]
        _sub = int(os.environ.get("FILT_SUB", "0"))
        if _sub == 1:
            continue
        E(P, "dve", "tensor_scalar", reads=[psT[0], cph], writes=[kre], out=kre.ap, in0=psT[0].ap, scalar1=cph[:, fc:fc + 1], scalar2=None, op0=ALU.mult)
        E(P, "dve", "scalar_tensor_tensor", reads=[psT[1], sph, kre], writes=[kre], out=kre.ap, in0=psT[1].ap, scalar=sph[:, fc:fc + 1], in1=kre.ap,
          op0=ALU.mult, op1=ALU.add)
        E(P, "pool", "tensor_tensor", reads=[kre, skr], writes=[kre], out=kre.ap, in0=kre.ap, in1=skr.ap, op=ALU.add)
        E(P, "dve", "tensor_scalar", reads=[psT[3], ncph], writes=[kim], out=kim.ap, in0=psT[3].ap, scalar1=ncph[:, fc:fc + 1], scalar2=None, op0=ALU.mult)
        E(P, "dve", "scalar_tensor_tensor", reads=[psT[2], sph, kim], writes=[kim], out=kim.ap, in0=psT[2].ap, scalar=sph[:, fc:fc + 1], in1=kim.ap,
          op0=ALU.mult, op1=ALU.add)
        if _sub == 2:
            continue
        DMA(P, "pool", KF[fc * 128:(fc + 1) * 128, :], kre.ap, reads=[kre], writes=[f"KF{tag}"])
        DMA(P, "pool", KF[n + fc * 128:n + (fc + 1) * 128, :], kim.ap, reads=[kim], writes=[f"KF{tag}"])
    if _stop <= 4:
        return
    allgather(kb, f"KF{tag}", f"KFg{tag}", 2 * n, 256, F32, SAMEH)


def phase_shortconv(kb, n, tag, col0):
    P = kb.P
    kb.phase()
    nlc = n // 128
    proj = kb.dram("proj0", [3072, NTOK])
    cwd = kb.dram("hy_cwT", [1536, 3]); cbd = kb.dram("hy_cbs", [1536, 1])
    outs = [kb.dram(f"{nm}{tag}", [n, 512], BF16) for nm in ("vT", "x1T", "x2gT")]
    idb = kb.C["idb"]
    us = [kb.sb(f"u{i}", [128, n]) for i in range(2)]
    scs = [kb.sb(f"sc{i}", [128, n]) for i in range(2)]
    hg = kb.sb("hg", [128, n])
    scb = kb.sb("scb", [128, n], BF16)
    cws = [kb.sb(f"cw{i}", [128, 4]) for i in range(2)]
    stg = [kb.sb(f"stg{i}", [128, 8, 128], BF16) for i in range(2)]
    pTs = [kb.ps(f"scpT{i}", i, [128, 8, 128], BF16) for i in range(2)]
    it = 0
    g8 = min(8, nlc)
    for seg in range(3):
        for cc in range(4):
            u, sc, cw = us[it % 2], scs[it % 2], cws[it % 2]
            r0 = seg * 512 + cc * 128
            DMA(P, kb.q(), u.ap, proj[r0:r0 + 128, col0:col0 + n], reads=["proj0"], writes=[u])
            DMA(P, kb.q(), cw[:, 0:3], cwd[r0:r0 + 128, :], writes=[cw.key + ":w"])
            DMA(P, kb.q(), cw[:, 3:4], cbd[r0:r0 + 128, :], writes=[cw.key + ":b"])
            ck = [cw.key + ":w", cw.key + ":b"]
            E(P, "act", "activation", reads=[u] + ck, writes=[sc], out=sc.ap, in_=u.ap, func=AF.Identity, bias=cw[:, 3:4], scale=cw[:, 1:2])
            E(P, "dve", "scalar_tensor_tensor", reads=[u, sc] + ck, writes=[sc], out=sc[:, 1:n], in0=u[:, 0:n - 1], scalar=cw[:, 0:1], in1=sc[:, 1:n],
              op0=ALU.mult, op1=ALU.add)
            E(P, "dve", "scalar_tensor_tensor", reads=[u, sc] + ck, writes=[sc], out=sc[:, 0:n - 1], in0=u[:, 1:n], scalar=cw[:, 2:3], in1=sc[:, 0:n - 1],
              op0=ALU.mult, op1=ALU.add)
            if seg == 2:
                DMA(P, kb.q(), hg.ap, proj[3 * 512 + cc * 128:3 * 512 + (cc + 1) * 128, col0:col0 + n], reads=["proj0"], writes=[hg])
                E(P, "act", "activation", reads=[hg], writes=[hg], out=hg.ap, in_=hg.ap, func=AF.Silu)
                E(P, "dve", "tensor_tensor", reads=[sc, hg], writes=[scb], out=scb.ap, in0=sc.ap, in1=hg.ap, op=ALU.mult)
            else:
                E(P, "act", "activation", reads=[sc], writes=[scb], out=scb.ap, in_=sc.ap, func=AF.Copy)
            outd = outs[seg].rearrange("(tc p) c -> p tc c", p=128)
            for t0 in range(0, nlc, g8):
                j = (t0 // g8) % 2
                for tq in range(g8):
                    E(P, "pe", "transpose", reads=[scb, idb], writes=[pTs[j]], out=pTs[j][:, tq, :], in_=scb[:, (t0 + tq) * 128:(t0 + tq + 1) * 128],
                      identity=idb.ap)
                if j == 0:
                    E(P, "dve", "tensor_copy", reads=[pTs[j]], writes=[stg[j]], out=stg[j][:, 0:g8, :], in_=pTs[j][:, 0:g8, :])
                else:
                    E(P, "act", "activation", reads=[pTs[j]], writes=[stg[j]], out=stg[j][:, 0:g8, :], in_=pTs[j][:, 0:g8, :], func=AF.Copy)
                DMA(P, kb.q("st"), outd[:, t0:t0 + g8, cc * 128:(cc + 1) * 128], stg[j][:, 0:g8, :], reads=[stg[j]], writes=[f"{('vT', 'x1T', 'x2gT')[seg]}{tag}"])
            it += 1


def phase_hyconv(kb, n, tag, col0):
    P = kb.P
    kb.phase()
    nlc = n // 128
    idb = kb.C["idb"]
    gcd = kb.dram(f"gc{tag}", [nlc, 128, nlc, 128], BF16); gsd = kb.dram(f"gs{tag}", [nlc, 128, nlc, 128], BF16)
    KFg = kb.dram(f"KFg{tag}", [4 * 2 * n, 256])
    rc = ag_rows(2 * n, 256, F32)
    nch = 2 * n // rc
    kv = KFg.rearrange("(ch k p) (o c) -> ch o p k c", ch=nch, k=4, o=2)

    def kslice(ri, o, fc):
        row = ri * n + fc * 128
        return kv[row // rc, o, row % rc: row % rc + 128]
    srcs = [kb.dram(f"{nm}{tag}", [n, 512], BF16).rearrange("(tc p) c -> p tc c", p=128) for nm in ("vT", "x1T", "x2gT")]
    mixT = kb.dram("mixT", [1024, NTOK], BF16)
    A = kb.sb("hyA", [128, nlc, 512], BF16)
    B = kb.sb("hyB", [128, nlc, 512], BF16)
    Yp = kb.sb("Yp", [128, nlc, 512], BF16)
    Yq = kb.sb("Yq", [128, nlc, 512], BF16)
    G = [[kb.sb(f"hG{i}{j}", [128, nlc, 128], BF16) for j in range(2)] for i in range(2)]
    Kt = [[kb.sb(f"hK{i}{j}", [128, 4, 128]) for j in range(2)] for i in range(2)]
    UV = [[kb.sb(f"hUV{i}{j}", [128, 512]) for j in range(2)] for i in range(2)]
    tmp = [kb.sb(f"htmp{i}", [128, 512]) for i in range(4)]
    mt = [kb.sb(f"hmt{i}", [128, 512], BF16) for i in range(2)]
    g4 = min(4, nlc)
    mstg = [kb.sb(f"hmst{i}", [128, 4, g4 * 128], BF16) for i in range(2)]
    psUV = [[kb.ps(f"psU{i}{j}", 2 * i + j, [128, 512]) for j in range(2)] for i in range(2)]
    psI = [kb.ps(f"psI{i}", 4 + i, [128, 512]) for i in range(2)]
    psM = [kb.ps(f"psM{i}", 6 + i, [128, 4, 128], BF16) for i in range(2)]

    def akeys(t_):
        return [t_.key + f":{i}" for i in range(nlc)]

    for tcn in range(nlc):
        DMA(P, kb.q(), A[:, tcn, :], srcs[0][:, tcn, :], reads=[f"vT{tag}"], writes=[A.key + f":{tcn}"])
        DMA(P, kb.q(), B[:, tcn, :], srcs[1][:, tcn, :], reads=[f"x1T{tag}"], writes=[B.key + f":{tcn}"])

    for o in range(2):
        inT = A if o == 0 else B
        for fc in range(nlc):
            gc_, gs_ = G[fc % 2]
            kre, kim = Kt[fc % 2]
            DMA(P, "sp", gc_.ap, gcd[fc], writes=[gc_])
            DMA(P, "act", gs_.ap, gsd[fc], writes=[gs_])
            DMA(P, "pool", kre.ap, kslice(0, o, fc), reads=[f"KFg{tag}"], writes=[kre])
            DMA(P, "pool", kim.ap, kslice(1, o, fc), reads=[f"KFg{tag}"], writes=[kim])
            pu, pv = psUV[fc % 2]
            for (g_, ps_) in ((gc_, pu), (gs_, pv)):
                for tcn in range(nlc):
                    E(P, "pe", "matmul", reads=[g_, inT.key + f":{tcn}"], writes=[ps_], out=ps_.ap, lhsT=g_[:, tcn, :], rhs=inT[:, tcn, :], start=(tcn == 0),
                      stop=(tcn == nlc - 1))
            us_, vs_ = UV[fc % 2]
            E(P, "act", "activation", reads=[pu], writes=[us_], out=us_.ap, in_=pu.ap, func=AF.Copy)
            E(P, "act", "activation", reads=[pv], writes=[vs_], out=vs_.ap, in_=pv.ap, func=AF.Copy)
            kr = kre.ap.rearrange("p k c -> p (k c)")
            ki = kim.ap.rearrange("p k c -> p (k c)")
            E(P, "dve", "tensor_tensor", reads=[us_, kre], writes=[tmp[0]], out=tmp[0].ap, in0=us_.ap, in1=kr, op=ALU.mult)
            E(P, "dve", "tensor_tensor", reads=[vs_, kim], writes=[tmp[1]], out=tmp[1].ap, in0=vs_.ap, in1=ki, op=ALU.mult)
            E(P, "dve", "tensor_tensor", reads=[tmp[0], tmp[1]], writes=[Yp.key + f":{fc}"], out=Yp[:, fc, :], in0=tmp[0].ap, in1=tmp[1].ap, op=ALU.add)
            E(P, "pool", "tensor_tensor", reads=[vs_, kre], writes=[tmp[2]], out=tmp[2].ap, in0=vs_.ap, in1=kr, op=ALU.mult)
            E(P, "pool", "tensor_tensor", reads=[us_, kim], writes=[tmp[3]], out=tmp[3].ap, in0=us_.ap, in1=ki, op=ALU.mult)
            E(P, "pool", "tensor_tensor", reads=[tmp[2], tmp[3]], writes=[Yq.key + f":{fc}"], out=Yq[:, fc, :], in0=tmp[2].ap, in1=tmp[3].ap, op=ALU.subtract)
        if o == 0:
            for tcn in range(nlc):
                DMA(P, kb.q(), A[:, tcn, :], srcs[2][:, tcn, :], reads=[f"x2gT{tag}"], writes=[A.key + f":{tcn}"])
        for tcn in range(nlc):
            gc_, gs_ = G[tcn % 2]
            DMA(P, "sp", gc_.ap, gcd[tcn], writes=[gc_])
            DMA(P, "act", gs_.ap, gsd[tcn], writes=[gs_])
            pi = psI[tcn % 2]
            for gi, (g_, Y_) in enumerate(((gc_, Yp), (gs_, Yq))):
                for fc in range(nlc):
                    E(P, "pe", "matmul", reads=[g_, Y_.key + f":{fc}"], writes=[pi], out=pi.ap, lhsT=g_[:, fc, :], rhs=Y_[:, fc, :],
                      start=(gi == 0 and fc == 0), stop=(gi == 1 and fc == nlc - 1))
            if o == 0:
                E(P, "dve", "tensor_tensor", reads=[pi, B.key + f":{tcn}"], writes=[B.key + f":{tcn}"], out=B[:, tcn, :], in0=pi.ap, in1=B[:, tcn, :], op=ALU.mult)
            else:
                m_ = mt[tcn % 2]
                E(P, "dve", "tensor_tensor", reads=[pi, A.key + f":{tcn}"], writes=[m_], out=m_.ap, in0=pi.ap, in1=A[:, tcn, :], op=ALU.mult)
                pm = psM[tcn % 2]
                for cc in range(4):
                    E(P, "pe", "transpose", reads=[m_, idb], writes=[pm], out=pm[:, cc, :], in_=m_[:, cc * 128:(cc + 1) * 128], identity=idb.ap)
                sidx = (tcn // g4) % 2
                q = tcn % g4
                E(P, "act", "activation", reads=[pm], writes=[mstg[sidx].key + f":{q}"], out=mstg[sidx][:, :, q * 128:(q + 1) * 128], in_=pm.ap, func=AF.Copy)
                if q == g4 - 1:
                    t0 = (tcn // g4) * g4 * 128
                    DMA(P, kb.q("st"), mixT[0:512, col0 + t0:col0 + t0 + g4 * 128].rearrange("(k c) t -> c k t", k=4), mstg[sidx].ap,
                        reads=[mstg[sidx].key + f":{i}" for i in range(g4)], writes=["mixT"])


S5_HALF = 1280
S5_BLOCKS = [[(0, 256), (256, 768), (768, 1280)], [(1280, 1792), (1792, 2304)], [(2304, 2816), (2816, 3328)], [(3328, 3840), (3840, 4352)]]


def phase_s5(kb):
    P = kb.P
    kb.phase()
    proj = kb.dram("proj0", [3072, NTOK])
    bre = kb.dram("s5_bre_chp", [2, 512, 64]); bim = kb.dram("s5_bim_chp", [2, 512, 64])
    are_c = kb.dram("s5_are_chp", [2, 512, 64]); aim_c = kb.dram("s5_aim_chp", [2, 512, 64]); ldt_c = kb.dram("s5_ldt_chp", [2, 512, 1])
    cre = kb.dram("s5_cre_pch", [2, 64, 512]); cim = kb.dram("s5_cim_pch", [2, 64, 512])
    are_p = kb.dram("s5_are_pg", [2, 128, 32]); aim_p = kb.dram("s5_aim_pg", [2, 128, 32]); ldt_r = kb.dram("s5_ldt_row", [2, 1, 32])
    dd = kb.dram("s5_ds", [512, 1]); permd = kb.dram("s5_perm", [128, 128]); sgnd = kb.dram("s5_sgn", [128, 2]); taud = kb.dram("s5_tau", [1, S5_HALF])
    gmaskd = kb.dram("s5_gmask", [128, 8])
    gT = kb.dram("s5_gT", [512, NTOK], BF16)
    g32 = kb.dram("s5_g32", [512, NTOK])

    H = S5_HALF
    su32 = kb.sb("su32", [128, NTOK]); yacc = kb.sb("yacc", [128, NTOK]); su16 = kb.sb("su16", [128, NTOK], BF16)
    tau = kb.sb("tau", [128, H])
    sets = [dict(tt=kb.sb(f"s5tt{i}", [128, H]), ti=kb.sb(f"s5ti{i}", [128, H], I32), Ct=kb.sb(f"Ct{i}", [128, H]), St=kb.sb(f"St{i}", [128, H]),
                 vv=kb.sb(f"vv{i}", [128, H]), st=kb.sb(f"st{i}", [128, H])) for i in range(2)]
    gsets = [dict(dtile=kb.sb(f"dtile{i}", [128, H]), carry=kb.sb(f"carry{i}", [128, 2])) for i in range(2)]
    ta2 = [kb.sb(f"s5ta2{i}", [128, 512]) for i in range(2)]
    tb2 = [kb.sb(f"s5tb2{i}", [128, 512]) for i in range(2)]
    cnt = {"a": 0, "b": 0}
    hpi = kb.sb("hpi", [128, 1])
    Sb = [kb.sb(f"Sb{i}", [128, 512], BF16) for i in range(2)]
    ta = [kb.sb(f"s5ta{i}", [128, 512]) for i in range(2)]
    tb = [kb.sb(f"s5tb{i}", [128, 512]) for i in range(2)]
    perm = kb.sb("perm", [128, 128]); sgn = kb.sb("sgn", [128, 2]); gmask = kb.sb("gmask", [128, 8])
    thp = kb.sb("thp", [128, 2, 32]); mdp = kb.sb("mdp", [128, 2, 32]); ldr = kb.sb("ldr", [128, 2, 32]); aip = kb.sb("aip", [128, 2, 32])
    dcol = kb.sb("dcol", [128, 1])
    thpq = [kb.sb(f"thpq{q}", [128, 2, 32]) for q in range(4)]
    cb = {nm: kb.sb("c_" + nm, [128, 64]) for nm in ("bre", "bim", "are", "aim", "mag", "cs", "sn", "lr", "li", "den", "cr", "ci", "t0", "t1", "t2", "t3")}
    cti = kb.sb("c_ti", [128, 64], I32)
    ldc = kb.sb("ldc", [128, 2])
    Ball = [kb.sb(f"Ball{v}", [128, 128]) for v in range(2)]
    Bpad = [[kb.sb(f"Bpad{v}_{g}", [128, 128], BF16) for g in range(8)] for v in range(2)]
    Call = kb.sb("Call", [128, 128])
    Cpad = [kb.sb(f"Cpad{g}", [128, 128], BF16) for g in range(8)]
    psB = [[kb.ps(f"s5bu{i}{v}", 2 * i + v, [128, 512]) for v in range(2)] for i in range(2)]
    psW = [kb.ps(f"s5sw{i}", 4 + i, [128, 512]) for i in range(2)]
    psY = [kb.ps(f"s5y{i}", 6 + i, [128, 512]) for i in range(2)]

    DMA(P, "sp", perm.ap, permd, writes=[perm]); DMA(P, "act", sgn.ap, sgnd, writes=[sgn]); DMA(P, "pool", gmask.ap, gmaskd, writes=[gmask])
    DMA(P, "sp", tau.ap, taud.broadcast_to([128, H]), writes=[tau])
    for d in range(2):
        DMA(P, kb.q(), thp[:, d, :], are_p[d], writes=[thp.key + f":{d}"])
        DMA(P, kb.q(), aip[:, d, :], aim_p[d], writes=[aip.key + f":{d}"])
        DMA(P, kb.q(), ldr[:, d, :], ldt_r[d].broadcast_to([128, 32]), writes=[ldr.key + f":{d}"])
    tk = [thp.key + ":0", thp.key + ":1"]; ak = [aip.key + ":0", aip.key + ":1"]; lk = [ldr.key + ":0", ldr.key + ":1"]
    E(P, "act", "activation", reads=lk, writes=lk, out=ldr.ap, in_=ldr.ap, func=AF.Exp)
    E(P, "dve", "tensor_tensor", reads=tk + lk, writes=[mdp], out=mdp.ap, in0=thp.ap, in1=ldr.ap, op=ALU.mult)
    E(P, "act", "activation", reads=[mdp], writes=[mdp], out=mdp.ap, in_=mdp.ap, func=AF.Exp)
    E(P, "dve", "tensor_tensor", reads=ak + lk, writes=tk, out=thp.ap, in0=aip.ap, in1=ldr.ap, op=ALU.mult)
    E(P, "dve", "tensor_scalar", reads=tk, writes=tk, out=thp.ap, in0=thp.ap, scalar1=1.0 / TWO_PI, scalar2=None, op0=ALU.mult)

    for q in range(4):
        E(P, "dve", "tensor_scalar", reads=tk, writes=[thpq[q]], out=thpq[q].ap, in0=thp.ap, scalar1=float(S5_BLOCKS[q][0][0]), scalar2=None, op0=ALU.mult)
    E(P, "pool", "memset", writes=[hpi], ap=hpi.ap, constant=math.pi / 2)

    def sincos(src, n, cs_out, sn_out, t_, ti_, eng_sub="pool"):
        E(P, "dve", "tensor_copy", reads=[src[1]], writes=[ti_[1]], out=ti_[0], in_=src[0])
        E(P, eng_sub, "tensor_tensor", reads=[src[1], ti_[1]], writes=[t_[1]], out=t_[0], in0=src[0], in1=ti_[0], op=ALU.subtract)
        E(P, "act", "activation", reads=[t_[1]], writes=[sn_out[1]], out=sn_out[0], in_=t_[0], func=AF.Sin, scale=TWO_PI)
        E(P, "dve", "tensor_scalar", reads=[src[1]], writes=[t_[1]], out=t_[0], in0=src[0], scalar1=0.25, scalar2=None, op0=ALU.add)
        E(P, "dve", "tensor_copy", reads=[t_[1]], writes=[ti_[1]], out=ti_[0], in_=t_[0])
        E(P, eng_sub, "tensor_tensor", reads=[t_[1], ti_[1]], writes=[t_[1]], out=t_[0], in0=t_[0], in1=ti_[0], op=ALU.subtract)
        E(P, "act", "activation", reads=[t_[1]], writes=[cs_out[1]], out=cs_out[0], in_=t_[0], func=AF.Sin, scale=TWO_PI)

    for cc in range(4):
        r0 = 4 * 512 + cc * 128
        DMA(P, "sp", su32.ap, proj[r0:r0 + 128, :], reads=["proj0"], writes=[su32])
        DMA(P, "act", dcol.ap, dd[cc * 128:(cc + 1) * 128, :], writes=[dcol])
        E(P, "dve", "tensor_scalar", reads=[su32, dcol], writes=[yacc], out=yacc.ap, in0=su32.ap, scalar1=dcol[:, 0:1], scalar2=None, op0=ALU.mult)
        for d in range(2):
            for nm, src in (("bre", bre), ("bim", bim), ("are", are_c), ("aim", aim_c)):
                DMA(P, kb.q(), cb[nm].ap, src[d, cc * 128:(cc + 1) * 128, :], writes=[cb[nm]])
            DMA(P, kb.q(), ldc[:, 0:1], ldt_c[d, cc * 128:(cc + 1) * 128, :], writes=[ldc])
            E(P, "act", "activation", reads=[ldc], writes=[ldc], out=ldc[:, 1:2], in_=ldc[:, 0:1], func=AF.Exp)
            E(P, "act", "activation", reads=[cb["are"], ldc], writes=[cb["mag"]], out=cb["mag"].ap, in_=cb["are"].ap, func=AF.Exp, scale=ldc[:, 1:2])
            E(P, "dve", "tensor_scalar", reads=[cb["aim"], ldc], writes=[cb["t0"]], out=cb["t0"].ap, in0=cb["aim"].ap, scalar1=ldc[:, 1:2], scalar2=1.0 / TWO_PI,
              op0=ALU.mult, op1=ALU.mult)
            sincos((cb["t0"].ap, cb["t0"]), 64, (cb["cs"].ap, cb["cs"]), (cb["sn"].ap, cb["sn"]), (cb["t1"].ap, cb["t1"]), (cti.ap, cti), eng_sub="dve")
            E(P, "dve", "tensor_tensor", reads=[cb["mag"], cb["cs"]], writes=[cb["lr"]], out=cb["lr"].ap, in0=cb["mag"].ap, in1=cb["cs"].ap, op=ALU.mult)
            E(P, "dve", "tensor_scalar", reads=[cb["lr"]], writes=[cb["lr"]], out=cb["lr"].ap, in0=cb["lr"].ap, scalar1=-1.0, scalar2=None, op0=ALU.add)
            E(P, "dve", "tensor_tensor", reads=[cb["mag"], cb["sn"]], writes=[cb["li"]], out=cb["li"].ap, in0=cb["mag"].ap, in1=cb["sn"].ap, op=ALU.mult)
            E(P, "dve", "tensor_tensor", reads=[cb["are"]], writes=[cb["den"]], out=cb["den"].ap, in0=cb["are"].ap, in1=cb["are"].ap, op=ALU.mult)
            E(P, "dve", "tensor_tensor", reads=[cb["aim"]], writes=[cb["t0"]], out=cb["t0"].ap, in0=cb["aim"].ap, in1=cb["aim"].ap, op=ALU.mult)
            E(P, "dve", "tensor_tensor", reads=[cb["den"], cb["t0"]], writes=[cb["den"]], out=cb["den"].ap, in0=cb["den"].ap, in1=cb["t0"].ap, op=ALU.add)
            E(P, "dve", "reciprocal", reads=[cb["den"]], writes=[cb["den"]], out=cb["den"].ap, in_=cb["den"].ap)
            E(P, "dve", "tensor_tensor", reads=[cb["lr"], cb["are"]], writes=[cb["t0"]], out=cb["t0"].ap, in0=cb["lr"].ap, in1=cb["are"].ap, op=ALU.mult)
            E(P, "dve", "tensor_tensor", reads=[cb["li"], cb["aim"]], writes=[cb["t1"]], out=cb["t1"].ap, in0=cb["li"].ap, in1=cb["aim"].ap, op=ALU.mult)
            E(P, "dve", "tensor_tensor", reads=[cb["t0"], cb["t1"]], writes=[cb["cr"]], out=cb["cr"].ap, in0=cb["t0"].ap, in1=cb["t1"].ap, op=ALU.add)
            E(P, "dve", "tensor_tensor", reads=[cb["cr"], cb["den"]], writes=[cb["cr"]], out=cb["cr"].ap, in0=cb["cr"].ap, in1=cb["den"].ap, op=ALU.mult)
            E(P, "dve", "tensor_tensor", reads=[cb["li"], cb["are"]], writes=[cb["t0"]], out=cb["t0"].ap, in0=cb["li"].ap, in1=cb["are"].ap, op=ALU.mult)
            E(P, "dve", "tensor_tensor", reads=[cb["lr"], cb["aim"]], writes=[cb["t1"]], out=cb["t1"].ap, in0=cb["lr"].ap, in1=cb["aim"].ap, op=ALU.mult)
            E(P, "dve", "tensor_tensor", reads=[cb["t0"], cb["t1"]], writes=[cb["ci"]], out=cb["ci"].ap, in0=cb["t0"].ap, in1=cb["t1"].ap, op=ALU.subtract)
            E(P, "dve", "tensor_tensor", reads=[cb["ci"], cb["den"]], writes=[cb["ci"]], out=cb["ci"].ap, in0=cb["ci"].ap, in1=cb["den"].ap, op=ALU.mult)
            E(P, "dve", "tensor_tensor", reads=[cb["cr"], cb["bre"]], writes=[cb["t0"]], out=cb["t0"].ap, in0=cb["cr"].ap, in1=cb["bre"].ap, op=ALU.mult)
            E(P, "dve", "tensor_tensor", reads=[cb["ci"], cb["bim"]], writes=[cb["t1"]], out=cb["t1"].ap, in0=cb["ci"].ap, in1=cb["bim"].ap, op=ALU.mult)
            E(P, "dve", "tensor_tensor", reads=[cb["cr"], cb["bim"]], writes=[cb["t2"]], out=cb["t2"].ap, in0=cb["cr"].ap, in1=cb["bim"].ap, op=ALU.mult)
            E(P, "dve", "tensor_tensor", reads=[cb["ci"], cb["bre"]], writes=[cb["t3"]], out=cb["t3"].ap, in0=cb["ci"].ap, in1=cb["bre"].ap, op=ALU.mult)
            for v in range(2):
                E(P, "dve", "tensor_tensor", reads=[cb["t0"], cb["t1"]], writes=[Ball[v]], out=Ball[v][:, 64 * v:64 * v + 64], in0=cb["t0"].ap, in1=cb["t1"].ap,
                  op=ALU.subtract)
                E(P, "dve", "tensor_tensor", reads=[cb["t2"], cb["t3"]], writes=[Ball[v]], out=Ball[v][:, 64 * (1 - v):64 * (1 - v) + 64], in0=cb["t2"].ap,
                  in1=cb["t3"].ap, op=ALU.add)
                for g in range(8):
                    E(P, "pool" if g % 2 else "dve", "tensor_scalar", reads=[Ball[v], gmask], writes=[Bpad[v][g]], out=Bpad[v][g].ap, in0=Ball[v].ap,
                      scalar1=gmask[:, g:g + 1], scalar2=None, op0=ALU.mult)
            DMA(P, "sp", Call[0:64, :], cre[d, :, cc * 128:(cc + 1) * 128], writes=[Call.key + ":r"])
            DMA(P, "act", Call[64:128, :], cim[d, :, cc * 128:(cc + 1) * 128], writes=[Call.key + ":i"])
            E(P, "dve", "tensor_scalar", reads=[Call.key + ":i"], writes=[Call.key + ":i"], out=Call[64:128, :], in0=Call[64:128, :], scalar1=-1.0, scalar2=None,
              op0=ALU.mult)
            for g in range(8):
                E(P, "pool", "memset", writes=[Cpad[g]], ap=Cpad[g].ap, constant=0.0)
                E(P, "act", "activation", reads=[Call.key + ":r", Call.key + ":i"], writes=[Cpad[g]], out=Cpad[g][:, 16 * g:16 * g + 16], in_=Call[:, 16 * g:16 * g + 16],
                  func=AF.Copy)
            if d == 0:
                E(P, "act", "activation", reads=[su32], writes=[su16], out=su16.ap, in_=su32.ap, func=AF.Copy)
            else:
                E(P, "act", "activation", reads=[su32], writes=[su16], out=su16[:, 0:256], in_=su32[:, 0:256][:, ::-1], func=AF.Copy)
                E(P, "act", "activation", reads=[su32], writes=[su16], out=su16[:, 256:NTOK], in_=su32[:, 256:NTOK][:, ::-1], func=AF.Copy)
            items = [(g, qn) for g in range(8) for qn in range(4)]

            def stage_a(k):
                g, qn = items[k]
                gi = cc * 8 + g
                W_ = sets[k % 2]
                tt, ti, Ct, St, vv = (W_[x] for x in ("tt", "ti", "Ct", "St", "vv"))
                G_ = gsets[g % 2]
                thc = thp[:, d, gi:gi + 1]
                if qn == 0:
                    E(P, "pool", "memset", writes=[G_["dtile"]], ap=G_["dtile"].ap, constant=1.0)
                    E(P, "dve", "tensor_scalar", reads=[G_["dtile"], mdp], writes=[G_["dtile"]], out=G_["dtile"].ap, in0=G_["dtile"].ap, scalar1=mdp[:, d, gi:gi + 1],
                      scalar2=None, op0=ALU.mult)
                    E(P, "pool", "memset", writes=[G_["carry"]], ap=G_["carry"].ap, constant=0.0)
                q_lo = S5_BLOCKS[qn][0][0]
                n_h = S5_BLOCKS[qn][-1][1] - q_lo
                E(P, "act", "activation", reads=[tau, thpq[qn]] + tk, writes=[tt], out=tt[:, 0:n_h], in_=tau[:, 0:n_h], func=AF.Identity, scale=thc,
                  bias=thpq[qn][:, d, gi:gi + 1])
                E(P, "dve", "tensor_copy", reads=[tt], writes=[ti], out=ti[:, 0:n_h], in_=tt[:, 0:n_h])
                E(P, "dve", "tensor_tensor", reads=[tt, ti], writes=[tt], out=tt[:, 0:n_h], in0=tt[:, 0:n_h], in1=ti[:, 0:n_h], op=ALU.subtract)
                E(P, "act", "activation", reads=[tt], writes=[St], out=St[:, 0:n_h], in_=tt[:, 0:n_h], func=AF.Sin, scale=TWO_PI)
                E(P, "act", "activation", reads=[tt], writes=[vv], out=vv[:, 0:n_h], in_=tt[:, 0:n_h], func=AF.Abs)
                E(P, "act", "activation", reads=[vv, hpi], writes=[Ct], out=Ct[:, 0:n_h], in_=vv[:, 0:n_h], func=AF.Sin, scale=-TWO_PI, bias=hpi[:, 0:1])
                for (a, b) in S5_BLOCKS[qn]:
                    nb = b - a
                    lo = a - q_lo
                    bi = cnt["a"]; cnt["a"] += 1
                    p1, p2 = psB[bi % 2]
                    E(P, "pe", "matmul", reads=[Bpad[0][g], su16], writes=[p1], out=p1[:, 0:nb], lhsT=Bpad[0][g].ap, rhs=su16[:, a:b], start=True, stop=True)
                    E(P, "pe", "matmul", reads=[Bpad[1][g], su16], writes=[p2], out=p2[:, 0:nb], lhsT=Bpad[1][g].ap, rhs=su16[:, a:b], start=True, stop=True)
                    ta_, tb_ = ta[bi % 2], tb[bi % 2]
                    E(P, "dve", "tensor_tensor", reads=[p1, Ct], writes=[ta_], out=ta_[:, 0:nb], in0=p1[:, 0:nb], in1=Ct[:, lo:lo + nb], op=ALU.mult)
                    E(P, "dve", "scalar_tensor_tensor", reads=[p2, St, sgn], writes=[tb_], out=tb_[:, 0:nb], in0=p2[:, 0:nb], scalar=sgn[:, 0:1], in1=St[:, lo:lo + nb],
                      op0=ALU.mult, op1=ALU.mult)
                    E(P, "pool", "tensor_tensor", reads=[ta_, tb_], writes=[vv.key + f":{lo}"], out=vv[:, lo:lo + nb], in0=ta_[:, 0:nb], in1=tb_[:, 0:nb], op=ALU.add)

            def stage_b(k):
                g, qn = items[k]
                W_ = sets[k % 2]
                Ct, St, vv, st = (W_[x] for x in ("Ct", "St", "vv", "st"))
                G_ = gsets[g % 2]
                dtile, carry = G_["dtile"], G_["carry"]
                q_lo = S5_BLOCKS[qn][0][0]
                n_h = S5_BLOCKS[qn][-1][1] - q_lo
                vkeys = [vv.key + f":{a - q_lo}" for (a, b) in S5_BLOCKS[qn]]
                E(P, "dve", "tensor_tensor_scan", reads=vkeys + [dtile, carry, vv], writes=[st], out=st[:, 0:n_h], data0=dtile[:, 0:n_h], data1=vv[:, 0:n_h],
                  initial=carry[:, 0:1], op0=ALU.mult, op1=ALU.add)
                E(P, "act", "activation", reads=[st], writes=[carry], out=carry[:, 0:1], in_=st[:, n_h - 1:n_h], func=AF.Copy)
                for (a, b) in S5_BLOCKS[qn]:
                    nb = b - a
                    lo = a - q_lo
                    bi = cnt["b"]; cnt["b"] += 1
                    pw = psW[bi % 2]
                    E(P, "pe", "matmul", reads=[perm, st], writes=[pw], out=pw[:, 0:nb], lhsT=perm.ap, rhs=st[:, lo:lo + nb], start=True, stop=True)
                    ta_, tb_ = ta2[bi % 2], tb2[bi % 2]
                    E(P, "pool", "tensor_tensor", reads=[st, Ct], writes=[ta_], out=ta_[:, 0:nb], in0=st[:, lo:lo + nb], in1=Ct[:, lo:lo + nb], op=ALU.mult)
                    E(P, "dve", "scalar_tensor_tensor", reads=[pw, St, sgn], writes=[tb_], out=tb_[:, 0:nb], in0=pw[:, 0:nb], scalar=sgn[:, 1:2], in1=St[:, lo:lo + nb],
                      op0=ALU.mult, op1=ALU.mult)
                    sb_ = Sb[bi % 2]
                    E(P, "dve", "tensor_tensor", reads=[ta_, tb_], writes=[sb_], out=sb_[:, 0:nb], in0=ta_[:, 0:nb], in1=tb_[:, 0:nb], op=ALU.add)
                    py = psY[bi % 2]
                    E(P, "pe", "matmul", reads=[Cpad[g], sb_], writes=[py], out=py[:, 0:nb], lhsT=Cpad[g].ap, rhs=sb_[:, 0:nb], start=True, stop=True)
                    if d == 0:
                        yv = yacc[:, a:b]
                    elif a < 256:
                        yv = yacc[:, 256 - b:256 - a][:, ::-1]
                    else:
                        yv = yacc[:, 4608 - b:4608 - a][:, ::-1]
                    E(P, "dve", "tensor_tensor", reads=[py, yacc], writes=[yacc], out=yv, in0=py[:, 0:nb], in1=yv, op=ALU.add)

            stage_a(0)
            for k in range(len(items)):
                if k + 1 < len(items):
                    stage_a(k + 1)
                stage_b(k)
        E(P, "act", "activation", reads=[yacc], writes=[yacc], out=yacc.ap, in_=yacc.ap, func=AF.Gelu_apprx_tanh)
        E(P, "pool", "tensor_copy", reads=[yacc], writes=[su16], out=su16.ap, in_=yacc.ap)
        DMA(P, "sp", g32[cc * 128:(cc + 1) * 128, :], yacc.ap, reads=[yacc], writes=["s5_g32"])
        DMA(P, "act", gT[cc * 128:(cc + 1) * 128, :], su16.ap, reads=[su16], writes=["s5_gT"])
    allgather(kb, "s5_gT", "s5_gTg", 512, NTOK, BF16, PAIRS)


TOK_BLOCKS = [(0, 256)] + [(256 + 512 * k, 256 + 512 * (k + 1)) for k in range(8)]


def phase_glu(kb):
    P = kb.P
    kb.phase()
    proj = kb.dram("proj0", [3072, NTOK])
    rc = ag_rows(512, NTOK, BF16)
    gTg = kb.dram("s5_gTg", [(512 // rc) * 2 * rc, NTOK], BF16)
    g32 = kb.dram("s5_g32", [512, NTOK])
    gwd = kb.dram("s5_gluw", [1024, 512]); gbd = kb.dram("s5_glub", [512, 1])
    mixT = kb.dram("mixT", [1024, NTOK], BF16)
    W16 = kb.sb("gW16", [128, 8, 512], BF16)
    wst = [kb.sb(f"gwst{i}", [128, 512]) for i in range(2)]
    gb = kb.sb("gb", [128, 4])
    gall = [kb.sb(f"gall{i}", [128, 8, 512], BF16) for i in range(2)]
    gt = [kb.sb(f"ggt{i}", [128, 512]) for i in range(2)]
    sgt = [kb.sb(f"gsg{i}", [128, 512]) for i in range(2)]
    sig = [kb.sb(f"gsig{i}", [128, 512]) for i in range(2)]
    ob = [kb.sb(f"gob{i}", [128, 512], BF16) for i in range(2)]
    ps = [kb.ps(f"gps{i}", i, [128, 512]) for i in range(2)]
    for k in range(8):
        DMA(P, kb.q(), wst[k % 2].ap, gwd[k * 128:(k + 1) * 128, :], writes=[wst[k % 2]])
        E(P, "act" if k % 2 else "pool", "activation" if k % 2 else "tensor_copy", reads=[wst[k % 2]], writes=[W16.key + f":{k}"],
          **(dict(out=W16[:, k, :], in_=wst[k % 2].ap, func=AF.Copy) if k % 2 else dict(out=W16[:, k, :], in_=wst[k % 2].ap)))
    for c4 in range(4):
        DMA(P, kb.q(), gb[:, c4:c4 + 1], gbd[c4 * 128:(c4 + 1) * 128, :], writes=[gb.key + f":{c4}"])
    wk = [W16.key + f":{k}" for k in range(8)]
    gsrc = gTg.rearrange("(k p) t -> p k t", p=128)
    it = 0
    for bi, (a, b) in enumerate(TOK_BLOCKS):
        nb = b - a
        ga = gall[bi % 2]
        DMA(P, kb.q(), ga[:, :, 0:nb], gsrc[:, :, a:b], reads=["s5_gTg"], writes=[ga])
        for c4 in range(4):
            j = it % 2
            it += 1
            for k in range(8):
                E(P, "pe", "matmul", reads=[ga] + wk, writes=[ps[j]], out=ps[j][:, 0:nb], lhsT=W16[:, k, c4 * 128:(c4 + 1) * 128], rhs=ga[:, k, 0:nb],
                  start=(k == 0), stop=(k == 7))
            E(P, "act", "activation", reads=[ps[j], gb.key + f":{c4}"], writes=[sig[j]], out=sig[j][:, 0:nb], in_=ps[j][:, 0:nb], func=AF.Sigmoid, bias=gb[:, c4:c4 + 1])
            DMA(P, kb.q(), gt[j][:, 0:nb], g32[c4 * 128:(c4 + 1) * 128, a:b], reads=["s5_g32"], writes=[gt[j]])
            DMA(P, kb.q(), sgt[j][:, 0:nb], proj[5 * 512 + c4 * 128:5 * 512 + (c4 + 1) * 128, a:b], reads=["proj0"], writes=[sgt[j]])
            E(P, "act", "activation", reads=[sgt[j]], writes=[sgt[j]], out=sgt[j][:, 0:nb], in_=sgt[j][:, 0:nb], func=AF.Silu)
            E(P, "dve", "tensor_tensor", reads=[gt[j], sig[j]], writes=[sig[j]], out=sig[j][:, 0:nb], in0=gt[j][:, 0:nb], in1=sig[j][:, 0:nb], op=ALU.mult)
            E(P, "pool", "tensor_tensor", reads=[sig[j], sgt[j]], writes=[ob[j]], out=ob[j][:, 0:nb], in0=sig[j][:, 0:nb], in1=sgt[j][:, 0:nb], op=ALU.mult)
            DMA(P, kb.q("st"), mixT[512 + c4 * 128:512 + (c4 + 1) * 128, a:b], ob[j][:, 0:nb], reads=[ob[j]], writes=["mixT"])


def load_gate_half(kb, layer, sel, hsel, Gl, Gc, Gfull, rows, psums):
    P = kb.P
    load_rows(kb, layer, rows, (2,))
    for (G_, si) in ((Gl, 0), (Gc, 1)):
        if G_ is None:
            continue
        bcast_row(kb, rows, 0, sel[:, si, :], Gfull, psums)
        E(P, "dve", "tensor_scalar", reads=allkeys(Gfull) + ["hsel"], writes=[G_], out=G_.ap, in0=Gfull[:, 0:1024], scalar1=hsel[:, 0:1], scalar2=None, op0=ALU.mult)
        E(P, "dve", "scalar_tensor_tensor", reads=allkeys(Gfull) + ["hsel", G_], writes=[G_], out=G_.ap, in0=Gfull[:, 1024:2048], scalar=hsel[:, 1:2], in1=G_.ap,
          op0=ALU.mult, op1=ALU.add)


def phase_post(kb, layer, src_name, gathered_name, krows, xin_name, xout_name, blocks, wname, xin_row0=0):
    P = kb.P
    kb.phase()
    ntok_src = kb.D[src_name].shape[1] if src_name in kb.D else None
    ncols = NTOK if layer == 0 else NLAT
    nch, rc = allgather(kb, src_name, gathered_name, krows, ncols, BF16, PAIRS)
    kch = 2 * krows // 128
    mg = kb.D[gathered_name].rearrange("(k p) t -> p k t", p=128)
    outw = kb.dram(wname, [2 * krows, 1024])
    xin = kb.dram(xin_name, [ncols, 1024]) if xin_name not in kb.D else kb.D[xin_name][xin_row0:xin_row0 + ncols, :]
    xout = kb.dram(xout_name, [ncols, 1024])
    selm = kb.dram("selm", [2, 5, 128]); hseld = kb.dram("hsel", [128, 2])
    W16 = kb.sb("pW16", [128, kch, 1024], BF16)
    wst = [kb.sb(f"pwst{i}", [128, 1024]) for i in range(2)]
    sel = kb.sb("psel", [5, 2, 128]); hsel = kb.sb("phsel", [128, 2])
    rows = kb.sb("prows", [5, 1, 8, 256])
    Gfull = kb.sb("pGfull", [128, D])
    Gl = kb.sb("pGl", [128, 1024]); Gc = kb.sb("pGc", [128, 1024]) if layer == 0 else None
    mall = [kb.sb(f"pmall{i}", [128, kch, 512], BF16) for i in range(2)]
    xt = [kb.sb(f"pxt{i}", [128, 1024]) for i in range(2)]
    tmp = [kb.sb(f"ptmp{i}", [128, 512]) for i in range(2)]
    xn = [kb.sb(f"pxn{i}", [128, 1024]) for i in range(2)]
    ps = [kb.ps(f"pps{i}", i, [128, 512]) for i in range(4)]
    bps = [kb.ps(f"pbps{i}", 4 + i, [128, 512]) for i in range(4)]
    DMA(P, "sp", sel.ap, selm.rearrange("a q p -> q a p"), writes=["sel"])
    DMA(P, "act", hsel.ap, hseld, writes=["hsel"])
    for k in range(kch):
        DMA(P, kb.q(), wst[k % 2].ap, outw[k * 128:(k + 1) * 128, :], writes=[wst[k % 2]])
        if k % 2:
            E(P, "act", "activation", reads=[wst[k % 2]], writes=[W16.key + f":{k}"], out=W16[:, k, :], in_=wst[k % 2].ap, func=AF.Copy)
        else:
            E(P, "pool", "tensor_copy", reads=[wst[k % 2]], writes=[W16.key + f":{k}"], out=W16[:, k, :], in_=wst[k % 2].ap)
    load_gate_half(kb, layer, sel, hsel, Gl, Gc, Gfull, rows, bps)
    wk = [W16.key + f":{k}" for k in range(kch)]
    it = 0
    ti = 0
    for bi, (a, b) in enumerate(blocks):
        nb = b - a
        ma = mall[bi % 2]
        DMA(P, kb.q(), ma[:, :, 0:nb], mg[:, :, a:b], reads=[gathered_name], writes=[ma])
        G_ = Gc if (layer == 0 and a < 256) else Gl
        for ts in range(nb // 128):
            x_ = xt[ti % 2]; xn_ = xn[ti % 2]
            ti += 1
            DMA(P, kb.q(), x_.ap, xin[a + ts * 128:a + (ts + 1) * 128, :], writes=[x_])
            for nchunk in range(2):
                p_ = ps[it % 4]; t_ = tmp[it % 2]
                it += 1
                for k in range(kch):
                    E(P, "pe", "matmul", reads=[ma] + wk, writes=[p_], out=p_.ap, lhsT=ma[:, k, ts * 128:(ts + 1) * 128], rhs=W16[:, k, nchunk * 512:(nchunk + 1) * 512],
                      start=(k == 0), stop=(k == kch - 1))
                E(P, "dve", "tensor_tensor", reads=[p_, G_], writes=[t_], out=t_.ap, in0=p_.ap, in1=G_[:, nchunk * 512:(nchunk + 1) * 512], op=ALU.mult)
                E(P, "pool", "tensor_tensor", reads=[t_, x_], writes=[xn_.key + f":{nchunk}"], out=xn_[:, nchunk * 512:(nchunk + 1) * 512], in0=t_.ap,
                  in1=x_[:, nchunk * 512:(nchunk + 1) * 512], op=ALU.add)
            DMA(P, kb.q("st"), xout[a + ts * 128:a + (ts + 1) * 128, :], xn_.ap, reads=[xn_.key + ":0", xn_.key + ":1"], writes=[xout_name])


def rope_tables():
    f32 = np.float32
    n = NLAT
    rows = n // 64
    row = np.broadcast_to(np.arange(rows, dtype=f32)[:, None], (rows, 64)).reshape(n)
    col = np.broadcast_to(np.arange(64, dtype=f32)[None, :], (rows, 64)).reshape(n)
    freqs = (f32(10000.0) ** (-np.arange(16, dtype=f32) / f32(16))).astype(f32)
    ar = (row[:, None] * freqs[None, :]).astype(f32)
    ac = (col[:, None] * freqs[None, :]).astype(f32)
    C64 = np.concatenate([np.cos(ar), np.cos(ar), np.cos(ac), np.cos(ac)], 1).astype(f32)
    S64 = np.concatenate([-np.sin(ar), np.sin(ar), -np.sin(ac), np.sin(ac)], 1).astype(f32)
    C = np.concatenate([np.ones((NCTX, 64), f32), C64], 0)
    S = np.concatenate([np.zeros((NCTX, 64), f32), S64], 0)
    return np.ascontiguousarray(np.tile(C, (1, 8))), np.ascontiguousarray(np.tile(S, (1, 8)))


def l1_norm_setup(kb, W16, inw, ncols_w, tagp):
    P = kb.P
    wst = kb.sb(tagp + "wst", [128, ncols_w])
    for k in range(16):
        DMA(P, kb.q(), wst.ap, inw[k * 128:(k + 1) * 128, :], writes=[wst])
        if k % 2 == 0:
            E(P, "pool", "tensor_copy", reads=[wst], writes=[W16.key + f":{k}"], out=W16[:, k, :], in_=wst.ap)
        else:
            E(P, "act", "activation", reads=[wst], writes=[W16.key + f":{k}"], out=W16[:, k, :], in_=wst.ap, func=AF.Copy)


def l1_load_x(kb, xt, x1g, tile_idx):
    P = kb.P
    for r in range(2):
        row0 = (tile_idx * 2 + r) * 128
        DMA(P, kb.q(), xt[:, r * 1024:(r + 1) * 1024], x1g[row0:row0 + 128, :], reads=["x1g"], writes=[xt.key + f":{r}"])


def norm_tile2(kb, xt, junk, ssq, t1, hb, A, S):
    P = kb.P
    xk = [xt.key + ":0", xt.key + ":1"]
    E(P, "act", "activation", reads=xk, writes=[junk, ssq], out=junk.ap, in_=xt.ap, func=AF.Square, accum_out=ssq[:, 0:1])
    E(P, "dve", "tensor_scalar", reads=[ssq], writes=[ssq], out=ssq[:, 1:2], in0=ssq[:, 0:1], scalar1=1.0 / D, scalar2=1e-6, op0=ALU.mult, op1=ALU.add)
    E(P, "act", "activation", reads=[ssq], writes=[ssq], out=ssq[:, 2:3], in_=ssq[:, 1:2], func=AF.Sqrt)
    E(P, "dve", "reciprocal", reads=[ssq], writes=[ssq], out=ssq[:, 3:4], in_=ssq[:, 2:3])
    E(P, "dve", "scalar_tensor_tensor", reads=xk + [ssq] + allkeys(A), writes=[t1], out=t1.ap, in0=xt.ap, scalar=ssq[:, 3:4], in1=A.ap, op0=ALU.mult, op1=ALU.mult)
    E(P, "pool", "tensor_tensor", reads=[t1] + allkeys(S), writes=[hb], out=hb.ap, in0=t1.ap, in1=S.ap, op=ALU.add)


def phase_l1_proj(kb, which):
    P = kb.P
    kb.phase()
    if which == "qk":
        allgather(kb, "x1h", "x1g", NTOK, 1024, F32, PAIRS, rc=128)
    x1g = kb.dram("x1g", [34 * 2 * 128, 1024])
    normw = kb.dram("normw", [2, D]); selm = kb.dram("selm", [2, 5, 128])
    inw = kb.dram("inw1" + which, [D, 2048])
    idb = kb.C["idb"]
    W16 = kb.sb("l1W16", [128, 16, 2048], BF16)
    A = kb.sb("l1A", [128, D]); S = kb.sb("l1S", [128, D]); sel = kb.sb("l1sel", [5, 2, 128]); rows = kb.sb("l1rows", [5, 2, 8, 256])
    xt = kb.sb("l1xt", [128, D]); junk = kb.sb("l1junk", [128, D], BF16); t1 = kb.sb("l1t1", [128, D]); nw = t1
    hbs = [kb.sb(f"l1hb{i}", [128, D], BF16) for i in range(2)]
    hT = kb.sb("l1hT", [128, 16, 512], BF16)
    ssqs = [kb.sb(f"l1ssq{i}", [128, 4]) for i in range(2)]
    pTs = [kb.ps(f"l1pT{i}", 2 * i, [128, 16, 128], BF16) for i in range(2)]
    pjs = [kb.ps(f"l1pj{i}", 4 + i, [128, 512]) for i in range(2)]
    pq = [kb.ps(f"l1pq{i}", 6 + i, [128, 4, 128], BF16) for i in range(2)]
    l1_norm_setup(kb, W16, inw, 2048, "l1")
    DMA(P, "sp", sel.ap, selm.rearrange("a q p -> q a p"), writes=["sel"])
    load_rows(kb, 1, rows, (0, 1))
    wk = [W16.key + f":{k}" for k in range(16)]
    if which == "qk":
        qT = kb.dram("qT", [8, 128, NLAT], BF16); kT = kb.dram("kT", [8, 128, NTOK], BF16)
        ropeC = kb.dram("ropeC", [NTOK, 512]); ropeS = kb.dram("ropeS", [NTOK, 512]); qkw = kb.dram("qkw", [2, 512])
        wrep = kb.sb("wrep", [128, 2, 512])
        Ct = [kb.sb(f"rC{i}", [128, 512]) for i in range(2)]; St_ = [kb.sb(f"rS{i}", [128, 512]) for i in range(2)]
        sq = kb.sb("rsq", [128, 512]); ss = kb.sb("rss", [128, 32]); xn = kb.sb("rxn", [128, 512]); r1 = kb.sb("rr1", [128, 512]); r2 = kb.sb("rr2", [128, 512])
        qr = [kb.sb(f"rqr{i}", [128, 512], BF16) for i in range(2)]
        stg = [kb.sb(f"rstg{i}", [128, 8, 512], BF16) for i in range(2)]
        for i in range(2):
            DMA(P, kb.q(), wrep[:, i, :], qkw[i:i + 1, :].broadcast_to([128, 512]), writes=[wrep.key + f":{i}"])
    else:
        vtok = kb.dram("vtok", [NTOK, 1024], BF16); gsil = kb.dram("gsil", [1024, NLAT])
        vb = [kb.sb(f"vvb{i}", [128, 1024], BF16) for i in range(2)]
        gst = [kb.sb(f"vgst{i}", [128, 512]) for i in range(2)]

    cur = None
    ti = 0
    it = 0
    for (a, b) in TOK_BLOCKS:
        nb = b - a
        is_ctx = a < 256
        if cur != is_ctx:
            cur = is_ctx
            sel_ap = sel[:, 1 if is_ctx else 0, :]
            bcast_row(kb, rows, 0, sel_ap, S, pjs)
            bcast_row(kb, rows, 1, sel_ap, A, pjs)
            DMA(P, "act", nw.ap, normw[1:2, :].broadcast_to([128, D]), writes=[nw])
            E(P, "dve", "scalar_tensor_tensor", reads=allkeys(A) + [nw], writes=allkeys(A), out=A.ap, in0=A.ap, scalar=1.0, in1=nw.ap, op0=ALU.add, op1=ALU.mult)
        for tt in range(nb // 128):
            hb = hbs[ti % 2]; ssq = ssqs[ti % 2]
            l1_load_x(kb, xt, x1g, (a // 128) + tt)
            norm_tile2(kb, xt, junk, ssq, t1, hb, A, S)
            pT = pTs[ti % 2]
            for k in range(16):
                E(P, "pe", "transpose", reads=[hb, idb], writes=[pT], out=pT[:, k, :], in_=hb[:, k * 128:(k + 1) * 128], identity=idb.ap)
            if ti % 2 == 0:
                E(P, "act", "activation", reads=[pT], writes=[hT.key + f":{tt}"], out=hT[:, :, tt * 128:(tt + 1) * 128], in_=pT.ap, func=AF.Copy)
            else:
                E(P, "dve", "tensor_copy", reads=[pT], writes=[hT.key + f":{tt}"], out=hT[:, :, tt * 128:(tt + 1) * 128], in_=pT.ap)
            ti += 1
        hkeys = [hT.key + f":{tt}" for tt in range(nb // 128)]
        if which == "qk":
            for part in ((1,) if is_ctx else (0, 1)):
                stg_ = stg[part]
                for tt in range(nb // 128):
                    tok0 = a + tt * 128
                    for half in range(2):
                        j = it % 2
                        it += 1
                        ps = pjs[j]
                        for k in range(16):
                            E(P, "pe", "matmul", reads=[hT.key + f":{tt}"] + wk, writes=[ps], out=ps.ap, lhsT=hT[:, k, tt * 128:(tt + 1) * 128],
                              rhs=W16[:, k, part * 1024 + half * 512: part * 1024 + (half + 1) * 512], start=(k == 0), stop=(k == 15))
                        DMA(P, kb.q(), Ct[j].ap, ropeC[tok0:tok0 + 128, :], writes=[Ct[j]])
                        DMA(P, kb.q(), St_[j].ap, ropeS[tok0:tok0 + 128, :], writes=[St_[j]])
                        E(P, "act", "activation", reads=[ps], writes=[sq], out=sq.ap, in_=ps.ap, func=AF.Square)
                        E(P, "dve", "tensor_reduce", reads=[sq], writes=[ss], out=ss[:, 0:8], in_=sq.ap.rearrange("p (g d) -> p g d", d=64), axis=AX.X, op=ALU.add)
                        E(P, "dve", "tensor_scalar", reads=[ss], writes=[ss], out=ss[:, 8:16], in0=ss[:, 0:8], scalar1=1.0 / 64, scalar2=1e-6, op0=ALU.mult, op1=ALU.add)
                        E(P, "act", "activation", reads=[ss], writes=[ss], out=ss[:, 16:24], in_=ss[:, 8:16], func=AF.Sqrt)
                        E(P, "dve", "reciprocal", reads=[ss], writes=[ss], out=ss[:, 24:32], in_=ss[:, 16:24])
                        for g in range(8):
                            E(P, "dve", "scalar_tensor_tensor", reads=[ps, ss, wrep.key + f":{part}"], writes=[xn], out=xn[:, g * 64:(g + 1) * 64], in0=ps[:, g * 64:(g + 1) * 64],
                              scalar=ss[:, 24 + g:25 + g], in1=wrep[:, part, g * 64:(g + 1) * 64], op0=ALU.mult, op1=ALU.mult)
                        xsw = xn.ap.rearrange("p (gp w d) -> p gp w d", w=2, d=16)[:, :, ::-1, :]
                        E(P, "pool", "tensor_tensor", reads=[xn, Ct[j]], writes=[r1], out=r1.ap, in0=xn.ap, in1=Ct[j].ap, op=ALU.mult)
                        E(P, "dve", "tensor_tensor", reads=[xn, St_[j]], writes=[r2], out=r2.ap.rearrange("p (gp w d) -> p gp w d", w=2, d=16), in0=xsw,
                          in1=St_[j].ap.rearrange("p (gp w d) -> p gp w d", w=2, d=16), op=ALU.mult)
                        E(P, "pool", "tensor_tensor", reads=[r1, r2], writes=[qr[j]], out=qr[j].ap, in0=r1.ap, in1=r2.ap, op=ALU.add)
                        for hh in range(4):
                            E(P, "pe", "transpose", reads=[qr[j], idb], writes=[pq[j]], out=pq[j][:, hh, :], in_=qr[j][:, hh * 128:(hh + 1) * 128], identity=idb.ap)
                        E(P, "act", "activation", reads=[pq[j]], writes=[stg_.key + f":{tt}:{half}"], out=stg_[:, half * 4:(half + 1) * 4, tt * 128:(tt + 1) * 128],
                          in_=pq[j].ap, func=AF.Copy)
                skeys = [stg_.key + f":{tt}:{half}" for tt in range(nb // 128) for half in range(2)]
                if part == 0:
                    DMA(P, kb.q("st"), qT[:, :, a - 256:b - 256].rearrange("h p t -> p h t"), stg_[:, :, 0:nb], reads=skeys, writes=["qT"] + skeys)
                else:
                    DMA(P, kb.q("st"), kT[:, :, a:b].rearrange("h p t -> p h t"), stg_[:, :, 0:nb], reads=skeys, writes=["kT"] + skeys)
        else:
            for tt in range(nb // 128):
                vb_ = vb[tt % 2]
                for half in range(2):
                    ps = pjs[it % 2]
                    it += 1
                    for k in range(16):
                        E(P, "pe", "matmul", reads=[hT.key + f":{tt}"] + wk, writes=[ps], out=ps.ap, lhsT=hT[:, k, tt * 128:(tt + 1) * 128],
                          rhs=W16[:, k, half * 512:(half + 1) * 512], start=(k == 0), stop=(k == 15))
                    E(P, "act", "activation", reads=[ps], writes=[vb_.key + f":{half}"], out=vb_[:, half * 512:(half + 1) * 512], in_=ps.ap, func=AF.Copy)
                DMA(P, kb.q("st"), vtok[a + tt * 128:a + (tt + 1) * 128, :], vb_.ap, reads=[vb_.key + ":0", vb_.key + ":1"], writes=["vtok", vb_.key + ":0", vb_.key + ":1"])
            if not is_ctx:
                for cc in range(8):
                    ps = pjs[it % 2]
                    g_ = gst[it % 2]
                    it += 1
                    for k in range(16):
                        E(P, "pe", "matmul", reads=hkeys + wk, writes=[ps], out=ps.ap, lhsT=W16[:, k, 1024 + cc * 128:1024 + (cc + 1) * 128], rhs=hT[:, k, 0:nb],
                          start=(k == 0), stop=(k == 15))
                    E(P, "act", "activation", reads=[ps], writes=[g_], out=g_.ap, in_=ps.ap, func=AF.Silu)
                    DMA(P, kb.q("st"), gsil[cc * 128:(cc + 1) * 128, a - 256:b - 256], g_.ap, reads=[g_], writes=["gsil", g_])


def phase_attn(kb):
    P = kb.P
    kb.phase()
    qT = kb.dram("qT", [8, 128, NLAT], BF16); kT = kb.dram("kT", [8, 128, NTOK], BF16)
    vtok = kb.dram("vtok", [NTOK, 1024], BF16); gsil = kb.dram("gsil", [1024, NLAT])
    attT = kb.dram("attT", [1024, NLAT], BF16)
    lqk = kb.dram("da_lqk", [4, 64]); slw = kb.dram("da_sublnw", [128, 1])
    lam_init = 0.8 - 0.6 * math.exp(-0.3 * 1)
    NKC = NTOK // 128
    kTh = [kb.sb(f"akT{i}", [128, NTOK], BF16) for i in range(2)]
    qTh = [kb.sb(f"aqT{i}", [128, NLAT], BF16) for i in range(2)]
    qz = [[kb.sb(f"aqz{i}{m}", [128, NLAT], BF16) for m in range(2)] for i in range(2)]
    vh = [kb.sb(f"avh{i}", [128, NKC, 128], BF16) for i in range(2)]
    Eb = [kb.sb(f"aE{i}", [128, 512], BF16) for i in range(3)]
    ones = kb.sb("aones", [128, 128], BF16)
    ones32 = kb.sb("aones32", [128, 128])
    accS = [kb.sb(f"aaccS{i}", [128, 512]) for i in range(2)]
    lq = kb.sb("alq", [128, 4, 64]); lt = kb.sb("alt", [128, 2, 64]); lam = kb.sb("alam", [128, 4]); swl = kb.sb("aswl", [128, 2])
    rc0 = kb.sb("arc0", [128, 512]); rc1 = kb.sb("arc1", [128, 512]); o0 = kb.sb("ao0", [128, 512]); o1 = kb.sb("ao1", [128, 512])
    sqb = kb.sb("asqb", [128, 512], BF16); rstd = kb.sb("arstd", [128, 512]); gt = [kb.sb(f"agt{i}", [128, 512]) for i in range(2)]
    ob = [kb.sb(f"aob{i}", [128, 512], BF16) for i in range(2)]
    psS = [kb.ps(f"apsS{i}", i, [128, 512]) for i in range(2)]
    acc = [kb.ps(f"apacc{i}", 2 + i, [128, 512]) for i in range(2)]
    sm = [kb.ps(f"apsm{i}", 4 + i, [128, 512]) for i in range(2)]
    psM = kb.ps("apsM", 6, [128, 512])
    E(P, "pool", "memset", writes=[ones], ap=ones.ap, constant=1.0)
    E(P, "pool", "memset", writes=[ones32], ap=ones32.ap, constant=1.0)
    for i in range(2):
        E(P, "pool", "memset", writes=[qz[i][0].key + ":z"], ap=qz[i][0][64:128, :], constant=0.0)
        E(P, "pool", "memset", writes=[qz[i][1].key + ":z"], ap=qz[i][1][0:64, :], constant=0.0)
    DMA(P, "sp", lq.ap, lqk.rearrange("(o a) d -> o a d", o=1).broadcast_to([128, 4, 64]), writes=[lq])
    E(P, "dve", "tensor_tensor", reads=[lq], writes=[lt], out=lt.ap, in0=lq[:, 0:4:2, :], in1=lq[:, 1:4:2, :], op=ALU.mult)
    E(P, "dve", "tensor_reduce", reads=[lt], writes=[lam], out=lam[:, 0:2], in_=lt.ap, axis=AX.X, op=ALU.add)
    E(P, "act", "activation", reads=[lam], writes=[lam], out=lam[:, 0:2], in_=lam[:, 0:2], func=AF.Exp)
    E(P, "dve", "tensor_tensor", reads=[lam], writes=[lam], out=lam[:, 2:3], in0=lam[:, 0:1], in1=lam[:, 1:2], op=ALU.subtract)
    E(P, "dve", "tensor_scalar", reads=[lam], writes=[lam], out=lam[:, 3:4], in0=lam[:, 2:3], scalar1=lam_init, scalar2=-1.0, op0=ALU.add, op1=ALU.mult)
    DMA(P, "act", swl[:, 0:1], slw, writes=[swl])
    E(P, "dve", "tensor_scalar", reads=[swl], writes=[swl], out=swl[:, 1:2], in0=swl[:, 0:1], scalar1=1.0 - lam_init, scalar2=None, op0=ALU.mult)
    ei = 0
    gi = 0
    for hl in range(8):
        kt_, qt_, v_ = kTh[hl % 2], qTh[hl % 2], vh[hl % 2]
        DMA(P, "sp", kt_.ap, kT[hl], reads=["kT"], writes=[kt_])
        DMA(P, "act", qz[hl % 2][0][0:64, :], qT[hl, 0:64, :], reads=["qT"], writes=[qz[hl % 2][0].key + ":d"])
        DMA(P, "act", qz[hl % 2][1][64:128, :], qT[hl, 64:128, :], reads=["qT"], writes=[qz[hl % 2][1].key + ":d"])
        DMA(P, "pool", v_.ap, vtok.rearrange("(kc p) (h e) -> p kc h e", p=128, e=128)[:, :, hl, :], reads=["vtok"], writes=[v_])
        for qb in range(8):
            q0 = qb * 512
            g_ = gt[gi % 2]; ob_ = ob[gi % 2]
            gi += 1
            DMA(P, kb.q(), g_.ap, gsil[hl * 128:(hl + 1) * 128, q0:q0 + 512], reads=["gsil"], writes=[g_])
            steps = [(m, kc) for m in range(2) for kc in range(NKC)]

            def issue_scores(i):
                m, kc = steps[i]
                ps = psS[(ei + i) % 2]; e_ = Eb[(ei + i) % 3]
                qz_ = qz[hl % 2][m]
                E(P, "pe", "matmul", reads=[kt_, qz_.key + ":d", qz_.key + ":z"], writes=[ps], out=ps.ap, lhsT=kt_[:, kc * 128:(kc + 1) * 128],
                  rhs=qz_[:, q0:q0 + 512], start=True, stop=True)
                E(P, "act", "activation", reads=[ps], writes=[e_], out=e_.ap, in_=ps.ap, func=AF.Exp, scale=0.125)

            issue_scores(0)
            for i, (m, kc) in enumerate(steps):
                if i + 1 < len(steps):
                    issue_scores(i + 1)
                e_ = Eb[(ei + i) % 3]
                E(P, "pe", "matmul", reads=[v_, e_], writes=[acc[m]], out=acc[m].ap, lhsT=v_[:, kc, :], rhs=e_.ap, start=(kc == 0), stop=(kc == NKC - 1))
                E(P, "pe", "matmul", reads=[ones, e_], writes=[sm[m]], out=sm[m].ap, lhsT=ones.ap, rhs=e_.ap, start=(kc == 0), stop=(kc == NKC - 1))
            ei += len(steps)
            E(P, "dve", "reciprocal", reads=[sm[0]], writes=[rc0], out=rc0.ap, in_=sm[0].ap)
            E(P, "dve", "reciprocal", reads=[sm[1]], writes=[rc1], out=rc1.ap, in_=sm[1].ap)
            E(P, "dve", "tensor_tensor", reads=[acc[0], rc0], writes=[o0], out=o0.ap, in0=acc[0].ap, in1=rc0.ap, op=ALU.mult)
            E(P, "dve", "tensor_tensor", reads=[acc[1], rc1], writes=[o1], out=o1.ap, in0=acc[1].ap, in1=rc1.ap, op=ALU.mult)
            E(P, "dve", "scalar_tensor_tensor", reads=[o1, lam, o0], writes=[o0], out=o0.ap, in0=o1.ap, scalar=lam[:, 3:4], in1=o0.ap, op0=ALU.mult, op1=ALU.add)
            E(P, "act", "activation", reads=[o0], writes=[sqb], out=sqb.ap, in_=o0.ap, func=AF.Square)
            E(P, "pe", "matmul", reads=[ones, sqb], writes=[psM], out=psM.ap, lhsT=ones.ap, rhs=sqb.ap, start=True, stop=True)
            E(P, "dve", "tensor_scalar", reads=[psM], writes=[rstd], out=rstd.ap, in0=psM.ap, scalar1=1.0 / 128, scalar2=1e-6, op0=ALU.mult, op1=ALU.add)
            E(P, "act", "activation", reads=[rstd], writes=[rstd], out=rstd.ap, in_=rstd.ap, func=AF.Sqrt)
            E(P, "dve", "reciprocal", reads=[rstd], writes=[rstd], out=rstd.ap, in_=rstd.ap)
            E(P, "dve", "scalar_tensor_tensor", reads=[o0, swl, rstd], writes=[o1], out=o1.ap, in0=o0.ap, scalar=swl[:, 1:2], in1=rstd.ap, op0=ALU.mult, op1=ALU.mult)
            E(P, "pool", "tensor_tensor", reads=[o1, g_], writes=[ob_], out=ob_.ap, in0=o1.ap, in1=g_.ap, op=ALU.mult)
            DMA(P, kb.q("st"), attT[hl * 128:(hl + 1) * 128, q0:q0 + 512], ob_.ap, reads=[ob_], writes=["attT"])


L1_BLOCKS = [(512 * k, 512 * (k + 1)) for k in range(8)]


def build_part(ext_in, part):
    kb = KB(ext_in=ext_in, ext_out=["x1h"] if part == "l0" else ["outh"])
    setup_consts(kb)
    phase_mod(kb)
    if part in ("l0", "full"):
        phase_l0_pre(kb)
        for tag, n, col0 in (("l", NLAT, 256), ("c", NCTX, 0)):
            phase_filter(kb, n, tag)
            phase_shortconv(kb, n, tag, col0)
            phase_hyconv(kb, n, tag, col0)
        phase_s5(kb)
        phase_glu(kb)
        phase_post(kb, 0, "mixT", "mixTg", 1024, "xh0", "x1h", TOK_BLOCKS, "outw0")
    if part == "l1":
        kb.phase()
        src = kb.dram("x1h_in", [NTOK, 1024])
        dst = kb.dram("x1h", [NTOK, 1024])
        for i in range(NTOK // 128):
            DMA(kb.P, kb.q(), dst[i * 128:(i + 1) * 128, :], src[i * 128:(i + 1) * 128, :], writes=["x1h"])
    if part in ("l1", "full"):
        phase_l1_proj(kb, "qk")
        phase_l1_proj(kb, "vg")
        phase_attn(kb)
        phase_post(kb, 1, "attT", "attTg", 1024, "x1h", "outh", L1_BLOCKS, "outw1", xin_row0=256)
    return kb


FUSED = True


def _launch(part, pc, extra=None):
    names = list(pc[0].keys()) + (list(extra[0].keys()) if extra else [])
    kb = build_part(names, part)
    nc = kb.finish()
    maps = []
    for r in range(8):
        m = dict(pc[r])
        if extra:
            m.update(extra[r])
        maps.append({k: v for k, v in m.items() if k in kb.D})
    return run_bass_kernel_spmd(nc, maps, core_ids=list(range(8)))


def kernel(**inputs):
    inp = {k: np.asarray(v) for k, v in inputs.items()}
    pc = host_inputs(inp)
    if FUSED:
        res = _launch("full", pc)
    else:
        r0 = _launch("l0", pc)
        res = _launch("l1", pc, [{"x1h_in": r0.results[r]["x1h"]} for r in range(8)])
    out = np.empty((4, NLAT, D), np.float32)
    for r in range(8):
        b, h = r // 2, r % 2
        out[b][:, 1024 * h:1024 * (h + 1)] = res.results[r]["outh"]
    return out
```

```python
import math
from contextlib import ExitStack

import numpy as np
import concourse.bass as bass
import concourse.mybir as mybir
from concourse.bass_utils import run_bass_kernel_spmd

F32 = mybir.dt.float32
BF16 = mybir.dt.bfloat16
I32 = mybir.dt.int32
AF = mybir.ActivationFunctionType
ALU = mybir.AluOpType
AX = mybir.AxisListType

N_DMA_SEMS = 6
D = 2048
NLAT = 4096
NCTX = 256
NTOK = NLAT + NCTX
PAIRS = [[0, 1], [2, 3], [4, 5], [6, 7]]
SAMEH = [[0, 2, 4, 6], [1, 3, 5, 7]]
ALL8 = [list(range(8))]
TWO_PI = 2.0 * math.pi


class Prog:
    ENGS = ("pe", "act", "dve", "pool", "sp")

    def __init__(self, nc):
        self.nc = nc
        self.ops = []
        self.last_w = {}
        self.readers = {}
        self.cnt = {}
        self.epoch = 0
        self.dma_rr = {e: 0 for e in self.ENGS}
        self.dma_cnt = {}
        self.dma_last = {}
        self.cc_cnt = 0
        self.last_on = {}
        self.bar = set()
        self.bar_seen = {}

    def _deps(self, eng, reads, writes):
        deps = set()
        for k in reads:
            if k in self.last_w:
                deps.add(self.last_w[k])
        for k in writes:
            if k in self.last_w:
                deps.add(self.last_w[k])
            for r in self.readers.get(k, ()):
                deps.add(r)
        if self.bar and self.bar_seen.get(eng) is not self.bar:
            deps |= self.bar
            self.bar_seen[eng] = self.bar
        return deps

    def _commit(self, oid, eng, reads, writes):
        for k in reads:
            self.readers.setdefault(k, []).append(oid)
        for k in writes:
            self.last_w[k] = oid
            self.readers[k] = []
        self.last_on[eng] = oid

    def barrier(self):
        b = set(self.last_on.values()) | set(self.dma_last.values())
        self.bar = frozenset(b)
        self.bar_seen = {}
        self.last_w = {}
        self.readers = {}

    def op(self, eng, fn, reads=(), writes=()):
        deps = self._deps(eng, reads, writes)
        oid = len(self.ops)
        self.tot = getattr(self, "tot", {})
        self.tot[eng] = self.tot.get(eng, 0) + 1
        ck = (eng, (self.tot[eng] - 1) // 30000)
        self.cnt[ck] = self.cnt.get(ck, 0) + 1
        self.ops.append(dict(eng=eng, fn=fn, deps=deps, kind="c", sem=("c",) + ck, val=self.cnt[ck]))
        self._commit(oid, eng, reads, writes)
        return oid

    def dma(self, eng, fn, reads=(), writes=()):
        deps = self._deps(eng, reads, writes)
        oid = len(self.ops)
        k = self.dma_rr[eng]
        self.dma_rr[eng] = (k + 1) % N_DMA_SEMS
        key = (eng, k)
        if key in self.dma_last:
            deps.add(self.dma_last[key])
        self.dma_cnt[key] = self.dma_cnt.get(key, 0) + 1
        self.dma_last[key] = oid
        self.ops.append(dict(eng=eng, fn=fn, deps=deps, kind="d", sem=("d",) + key, val=16 * self.dma_cnt[key]))
        self._commit(oid, eng, reads, writes)
        return oid

    def cc(self, fn, reads=(), writes=()):
        deps = self._deps("pool", reads, writes)
        oid = len(self.ops)
        key = ("pool", "cc")
        if key in self.dma_last:
            deps.add(self.dma_last[key])
        self.cc_cnt += 1
        self.dma_last[key] = oid
        self.ops.append(dict(eng="pool", fn=fn, deps=deps, kind="cc", sem=("cc",), val=self.cc_cnt))
        self._commit(oid, "pool", reads, writes)
        return oid

    def emit(self):
        nc = self.nc
        ops = self.ops
        with ExitStack() as st:
            sems = {}
            for ck in self.cnt:
                sems[("c",) + ck] = st.enter_context(nc.semaphore(f"c_{ck[0]}_{ck[1]}"))
            for key in self.dma_cnt:
                sems[("d",) + key] = st.enter_context(nc.semaphore(f"d_{key[0]}_{key[1]}"))
            if self.cc_cnt:
                sems[("cc",)] = st.enter_context(nc.semaphore("cc_sem"))
            block = st.enter_context(nc.Block())

            def run(eng_name):
                def body(eng):
                    waited = {}
                    for op in ops:
                        if op["eng"] != eng_name:
                            continue
                        need = {}
                        for d in op["deps"]:
                            dop = ops[d]
                            if dop["eng"] == "pe" and eng_name == "pe" and dop["kind"] == "c" and op["kind"] == "c":
                                continue
                            s = dop["sem"]
                            if dop["val"] > need.get(s, 0):
                                need[s] = dop["val"]
                        for s, v in need.items():
                            if waited.get(s, 0) >= v:
                                continue
                            eng.wait_ge(sems[s], v)
                            waited[s] = v
                        ins = op["fn"](eng)
                        if op["kind"] == "cc":
                            ins.then_inc(sems[op["sem"]])
                        else:
                            ins.then_inc(sems[op["sem"]], 16 if op["kind"] == "d" else 1)
                    if eng_name == "sp":
                        for key, c in self.dma_cnt.items():
                            eng.wait_ge(sems[("d",) + key], 16 * c)
                        if self.cc_cnt:
                            eng.wait_ge(sems[("cc",)], self.cc_cnt)
                        for ck, c in self.cnt.items():
                            eng.wait_ge(sems[("c",) + ck], c)
                return body

            block.sync(run("sp"))
            block.tensor(run("pe"))
            block.scalar(run("act"))
            block.vector(run("dve"))
            block.gpsimd(run("pool"))


class T:
    _n = 0

    def __init__(self, ap, name, psum_banks=None):
        T._n += 1
        self.ap = ap
        self.key = f"{name}#{T._n}"
        self.psum_banks = psum_banks

    def __getitem__(self, idx):
        return self.ap[idx]


SB_BYTES = 204 * 1024


class KB:
    def __init__(self, ext_in=(), ext_out=()):
        self.nc = bass.Bass("TRN2", target_bir_lowering=False)
        self.P = Prog(self.nc)
        self.ext_in = set(ext_in)
        self.ext_out = set(ext_out)
        self.D = {}
        self.st = ExitStack()
        self.arena = self.st.enter_context(self.nc.sbuf_tensor("arena", [128, SB_BYTES // 2], BF16))
        self.parena = self.st.enter_context(self.nc.psum_tensor("parena", [128, 8 * 1024], BF16))
        self.sb_off = 0
        self.sb_base = 0
        self.rr = {}

    def dram(self, name, shape, dt=F32):
        if name in self.D:
            return self.D[name]
        kind = "ExternalInput" if name in self.ext_in else ("ExternalOutput" if name in self.ext_out else "Internal")
        t = self.nc.dram_tensor(name, list(shape), dt, kind=kind)
        self.D[name] = t.ap()
        return self.D[name]

    @staticmethod
    def _view(base, off, shape, dt):
        n = int(np.prod(shape[1:]))
        sz = 4 if dt in (F32, I32) else 2
        assert off % 4 == 0
        v = base[0:shape[0], off // 2: off // 2 + n * sz // 2]
        if dt != BF16:
            v = v.bitcast(dt)
        if len(shape) == 3:
            v = v.rearrange("p (a b) -> p a b", a=shape[1])
        elif len(shape) == 4:
            v = v.rearrange("p (a b c) -> p a b c", a=shape[1], b=shape[2])
        return v

    def sb(self, name, shape, dt=F32):
        n = int(np.prod(shape[1:]))
        sz = 4 if dt in (F32, I32) else 2
        nbytes = (n * sz + 31) // 32 * 32
        off = self.sb_off
        self.sb_off += nbytes
        assert self.sb_off <= SB_BYTES, f"SBUF arena overflow allocating {name}: {self.sb_off}"
        return T(self._view(self.arena, off, shape, dt), name)

    def ps(self, name, bank, shape, dt=F32, off=0):
        n = int(np.prod(shape[1:]))
        sz = 4 if dt in (F32, I32) else 2
        nb = (off + n * sz + 2047) // 2048
        return T(self._view(self.parena, bank * 2048 + off, shape, dt), name, psum_banks=list(range(bank, bank + nb)))

    def phase(self, persistent=False):
        self.P.barrier()
        if persistent:
            self.sb_base = self.sb_off
        self.sb_off = self.sb_base

    def q(self, group="ld"):
        order = ("sp", "act", "pool")
        i = self.rr.get(group, 0)
        self.rr[group] = i + 1
        return order[i % len(order)]

    def finish(self):
        self.P.emit()
        self.st.close()
        return self.nc


def _keys(reads, writes):
    rk, wk = [], []
    for t in reads:
        if isinstance(t, T):
            if t.psum_banks is not None:
                wk += [f"psb{b}" for b in t.psum_banks]
            else:
                rk.append(t.key)
        else:
            rk.append(t)
    for t in writes:
        if isinstance(t, T):
            if t.psum_banks is not None:
                wk += [f"psb{b}" for b in t.psum_banks]
            else:
                wk.append(t.key)
        else:
            wk.append(t)
    return rk, wk


def E(P, eng, method, reads=(), writes=(), **kw):
    rk, wk = _keys(reads, writes)
    return P.op(eng, lambda e: getattr(e, method)(**kw), rk, wk)


def DMA(P, eng, out, in_, reads=(), writes=(), **kw):
    rk, wk = _keys(reads, writes)
    return P.dma(eng, lambda e: e.dma_start(out=out, in_=in_, **kw), rk, wk)


AG_MAX_BYTES = 1 << 20


def ag_rows(rows, cols, dt):
    sz = 4 if dt in (F32, I32) else 2
    rc = rows
    while rc * cols * sz > AG_MAX_BYTES:
        assert rc % 2 == 0
        rc //= 2
    return rc


def allgather(kb, src_name, dst_name, rows, cols, dt, groups, rc=None):
    P = kb.P
    R = len(groups[0])
    rc = rc or ag_rows(rows, cols, dt)
    nch = rows // rc
    src = kb.dram(src_name, [rows, cols], dt)
    dst = kb.dram(dst_name, [nch * R * rc, cols], dt)
    for ch in range(nch):
        P.cc(lambda e, ch=ch: e.collective_compute("AllGather", ALU.bypass, replica_groups=groups, ins=[src[ch * rc:(ch + 1) * rc, :]],
                                                   outs=[dst[ch * R * rc:(ch + 1) * R * rc, :]]),
             reads=[src_name], writes=[dst_name])
    return nch, rc


def setup_consts(kb):
    P = kb.P
    c = {}
    idf = kb.sb("idf", [128, 128], F32)
    idb = kb.sb("idb", [128, 128], BF16)
    E(P, "pool", "memset", writes=[idf], ap=idf.ap, constant=0.0)
    E(P, "pool", "affine_select", reads=[idf], writes=[idf], out=idf.ap, in_=idf.ap, pattern=[[-1, 128]],
      compare_op=ALU.not_equal, fill=1.0, base=0, channel_multiplier=1)
    E(P, "dve", "tensor_copy", reads=[idf], writes=[idb], out=idb.ap, in_=idf.ap)
    c["idf"] = idf
    c["idb"] = idb
    kb.C = c
    kb.sb_base = kb.sb_off


def phase_mod(kb):
    P = kb.P
    kb.phase()
    cvec = kb.dram("cvec", [5, D])
    modw = kb.dram("modw", [2, D, 768])
    modb = kb.dram("modb", [2, 768])
    modp = kb.dram("modp", [10, 768])
    modg = kb.dram("modg", [80, 768])
    cT = kb.sb("cT", [128, 5, 16])
    cS = kb.sb("cS", [128, 16, 5])
    DMA(P, "sp", cT.ap, cvec.rearrange("b (p k) -> p b k", p=128), writes=[cT])
    E(P, "act", "activation", reads=[cT], writes=[cS], out=cS.ap, in_=cT.ap.rearrange("p b k -> p k b"), func=AF.Silu)
    for l in range(2):
        W = kb.sb(f"modW{l}", [128, 16, 768])
        bias = kb.sb(f"modbias{l}", [5, 768])
        res = kb.sb(f"modres{l}", [5, 768])
        for kq in range(4):
            DMA(P, kb.q(), W[:, kq * 4:(kq + 1) * 4, :], modw[l].rearrange("(p k) n -> p k n", p=128)[:, kq * 4:(kq + 1) * 4, :],
                writes=[W.key + f":{kq}"])
        DMA(P, "pool", bias.ap, modb[l:l + 1, :].broadcast_to([5, 768]), writes=[bias])
        for (n0, nn, bank) in ((0, 512, 0), (512, 256, 1)):
            ps = kb.ps(f"modps{l}_{n0}", bank + 2 * l, [5, nn])
            for k in range(16):
                E(P, "pe", "matmul", reads=[cS, W.key + f":{k // 4}"], writes=[ps], out=ps.ap, lhsT=cS[:, k, :], rhs=W[:, k, n0:n0 + nn],
                  start=(k == 0), stop=(k == 15))
            E(P, "dve", "tensor_tensor", reads=[ps, bias], writes=[res.key + f":{n0}"], out=res[:, n0:n0 + nn], in0=ps.ap,
              in1=bias[:, n0:n0 + nn], op=ALU.add)
        DMA(P, "sp", modp[l * 5:(l + 1) * 5, :], res.ap, reads=[res.key + ":0", res.key + ":512"], writes=["modp"])
    modg2 = kb.dram("modg2", [20, 768])
    P.cc(lambda e: e.collective_compute("AllGather", ALU.bypass, replica_groups=PAIRS, ins=[modp[:, :]], outs=[modg2[:, :]]),
         reads=["modp"], writes=["modg2"])
    P.cc(lambda e: e.collective_compute("AllGather", ALU.bypass, replica_groups=SAMEH, ins=[modg2[:, :]], outs=[modg[:, :]]),
         reads=["modg2"], writes=["modg"])


def load_rows(kb, layer, rows, segs):
    P = kb.P
    modg = kb.D["modg"]
    src = modg.rearrange("(r q) (s j) -> q s r j", q=10, s=3)[layer * 5:(layer + 1) * 5]
    for i, s in enumerate(segs):
        DMA(P, kb.q(), rows[:, i], src[:, s], reads=["modg"], writes=[rows.key + f":{i}"])


def bcast_row(kb, rows, i, sel_ap, out_tile, psums):
    P = kb.P
    for nc4 in range(4):
        ps = psums[nc4 % len(psums)]
        E(P, "pe", "matmul", reads=[rows.key + f":{i}", "sel"], writes=[ps], out=ps.ap, lhsT=sel_ap,
          rhs=rows[:, i].rearrange("q r j -> q (r j)")[:, nc4 * 512:(nc4 + 1) * 512], start=True, stop=True)
        E(P, "act", "activation", reads=[ps], writes=[out_tile.key + f":{nc4}"], out=out_tile[:, nc4 * 512:(nc4 + 1) * 512], in_=ps.ap,
          func=AF.Copy)


def allkeys(t, n=4):
    return [t.key + f":{i}" for i in range(n)]


def norm_tile(kb, src_ap, xt, junk, ssq, t1, hb, A, S, eng_alt):
    P = kb.P
    DMA(P, kb.q(), xt.ap, src_ap, writes=[xt])
    E(P, "act", "activation", reads=[xt], writes=[junk, ssq], out=junk.ap, in_=xt.ap, func=AF.Square, accum_out=ssq[:, 0:1])
    E(P, "dve", "tensor_scalar", reads=[ssq], writes=[ssq], out=ssq[:, 1:2], in0=ssq[:, 0:1], scalar1=1.0 / D, scalar2=1e-6,
      op0=ALU.mult, op1=ALU.add)
    E(P, "act", "activation", reads=[ssq], writes=[ssq], out=ssq[:, 2:3], in_=ssq[:, 1:2], func=AF.Sqrt)
    E(P, "dve", "reciprocal", reads=[ssq], writes=[ssq], out=ssq[:, 3:4], in_=ssq[:, 2:3])
    E(P, "dve", "scalar_tensor_tensor", reads=[xt, ssq] + allkeys(A), writes=[t1], out=t1.ap, in0=xt.ap, scalar=ssq[:, 3:4], in1=A.ap,
      op0=ALU.mult, op1=ALU.mult)
    E(P, "pool", "tensor_tensor", reads=[t1] + allkeys(S), writes=[hb], out=hb.ap, in0=t1.ap, in1=S.ap, op=ALU.add)


def phase_l0_pre(kb):
    P = kb.P
    kb.phase()
    x = kb.dram("x_b", [NLAT, D])
    ctx = kb.dram("ctx_b", [NCTX, D])
    normw = kb.dram("normw", [2, D])
    inw = kb.dram("inw0", [D, 3072])
    selm = kb.dram("selm", [2, 5, 128])
    proj = kb.dram("proj0", [3072, NTOK])
    idb = kb.C["idb"]

    W16 = kb.sb("W16", [128, 16, 3072], BF16)
    wst = kb.sb("wst", [128, 3072])
    A = kb.sb("A", [128, D])
    S = kb.sb("S", [128, D])
    sel = kb.sb("sel", [5, 2, 128])
    rows = kb.sb("rows", [5, 2, 8, 256])
    xts = [kb.sb(f"xt{i}", [128, D]) for i in range(2)]
    junk = kb.sb("junk", [128, D], BF16)
    t1 = kb.sb("t1", [128, D])
    nw = t1
    hbs = [kb.sb(f"hb{i}", [128, D], BF16) for i in range(2)]
    hT = kb.sb("hT", [128, 16, 512], BF16)
    osts = [kb.sb(f"ost{i}", [128, 512]) for i in range(3)]
    ssqs = [kb.sb(f"ssq{i}", [128, 4]) for i in range(2)]
    pTs = [kb.ps(f"pT{i}", 2 * i, [128, 16, 128], BF16) for i in range(2)]
    pjs = [kb.ps(f"pj{i}", 4 + i, [128, 512]) for i in range(4)]
    load_rows(kb, 0, rows, (0, 1))

    for k in range(16):
        DMA(P, kb.q(), wst.ap, inw[k * 128:(k + 1) * 128, :], writes=[wst])
        if k % 2 == 0:
            E(P, "pool", "tensor_copy", reads=[wst], writes=[W16.key + f":{k}"], out=W16[:, k, :], in_=wst.ap)
        else:
            E(P, "act", "activation", reads=[wst], writes=[W16.key + f":{k}"], out=W16[:, k, :], in_=wst.ap, func=AF.Copy)
    DMA(P, "sp", sel.ap, selm.rearrange("a q p -> q a p"), writes=["sel"])

    tiles = [("ctx", 0, 256, 0)] + [("lat", 512 * i, 512, 256 + 512 * i) for i in range(8)]
    cur = None
    ti = 0
    for (srcname, row0, ntok, col0) in tiles:
        if srcname != cur:
            cur = srcname
            sel_ap = sel[:, 0 if srcname == "lat" else 1, :]
            bcast_row(kb, rows, 0, sel_ap, S, pjs)
            bcast_row(kb, rows, 1, sel_ap, A, pjs)
            DMA(P, "act", nw.ap, normw[0:1, :].broadcast_to([128, D]), writes=[nw])
            E(P, "dve", "scalar_tensor_tensor", reads=allkeys(A) + [nw], writes=allkeys(A), out=A.ap, in0=A.ap, scalar=1.0, in1=nw.ap,
              op0=ALU.add, op1=ALU.mult)
        src = x if srcname == "lat" else ctx
        for tt in range(ntok // 128):
            xt = xts[ti % 2]
            hb = hbs[ti % 2]
            ssq = ssqs[ti % 2]
            norm_tile(kb, src[row0 + tt * 128: row0 + (tt + 1) * 128, :], xt, junk, ssq, t1, hb, A, S, ti)
            pT = pTs[ti % 2]
            for k in range(16):
                E(P, "pe", "transpose", reads=[hb, idb], writes=[pT], out=pT[:, k, :], in_=hb[:, k * 128:(k + 1) * 128], identity=idb.ap)
            if ti % 2 == 0:
                E(P, "act", "activation", reads=[pT], writes=[hT.key + f":{tt}"], out=hT[:, :, tt * 128:(tt + 1) * 128], in_=pT.ap, func=AF.Copy)
            else:
                E(P, "dve", "tensor_copy", reads=[pT], writes=[hT.key + f":{tt}"], out=hT[:, :, tt * 128:(tt + 1) * 128], in_=pT.ap)
            ti += 1
        hkeys = [hT.key + f":{tt}" for tt in range(ntok // 128)]
        for cc in range(24):
            ps = pjs[cc % 4]
            for k in range(16):
                E(P, "pe", "matmul", reads=hkeys + [W16.key + f":{k}"], writes=[ps], out=ps[:, 0:ntok], lhsT=W16[:, k, cc * 128:(cc + 1) * 128],
                  rhs=hT[:, k, 0:ntok], start=(k == 0), stop=(k == 15))
            ost = osts[cc % 3]
            if cc % 2 == 0:
                E(P, "act", "activation", reads=[ps], writes=[ost], out=ost[:, 0:ntok], in_=ps[:, 0:ntok], func=AF.Copy)
            else:
                E(P, "dve", "tensor_copy", reads=[ps], writes=[ost], out=ost[:, 0:ntok], in_=ps[:, 0:ntok])
            DMA(P, kb.q("st"), proj[cc * 128:(cc + 1) * 128, col0:col0 + ntok], ost[:, 0:ntok], reads=[ost], writes=["proj0"])


def seg_cols(h, nseg, seg_w=1024, half=512):
    return np.concatenate([np.arange(s * seg_w + half * h, s * seg_w + half * (h + 1)) for s in range(nseg)])


def host_inputs(inp):
    f32 = np.float32
    per_core = []
    cvec = np.concatenate([inp["c"], inp["c_ctx"][None, :]], 0).astype(f32)
    for r in range(8):
        b, h = r // 2, r % 2
        m = {}
        m["cvec"] = cvec
        mcols = np.concatenate([np.arange(2048 * s + 256 * r, 2048 * s + 256 * (r + 1)) for s in range(3)])
        m["modw"] = np.ascontiguousarray(inp["mod_w"][:, :, mcols])
        m["modb"] = np.ascontiguousarray(inp["mod_b"][:, mcols])
        m["x_b"] = np.ascontiguousarray(inp["x"][b])
        m["ctx_b"] = np.ascontiguousarray(inp["ctx"][b])
        m["normw"] = np.ascontiguousarray(inp["norm_w"])
        m["inw0"] = np.ascontiguousarray(inp["ev_in_w"][0][:, seg_cols(h, 6)])
        selm = np.zeros((2, 5, 128), f32)
        selm[0, b, :] = 1.0
        selm[1, 4, :] = 1.0
        m["selm"] = selm
        cq = 512 * h + 128 * b + np.arange(128)
        w3idx = np.concatenate([o * 2048 + d * 1024 + cq for o in range(2) for d in range(2)])
        m["hy_w1"] = np.ascontiguousarray(inp["hy_w1"][0]); m["hy_b1"] = np.ascontiguousarray(inp["hy_b1"][0][:, None])
        m["hy_w2"] = np.ascontiguousarray(inp["hy_w2"][0]); m["hy_b2"] = np.ascontiguousarray(inp["hy_b2"][0][:, None])
        m["hy_w3s"] = np.ascontiguousarray(inp["hy_w3"][0][:, w3idx]); m["hy_b3s"] = np.ascontiguousarray(inp["hy_b3"][0][None, w3idx])
        m["hy_freqT"] = np.ascontiguousarray(inp["hy_freq"][0].T)
        m["hy_skips"] = np.ascontiguousarray(inp["hy_skip"][0][:, cq].reshape(1, 256))
        m["hy_delta"] = np.ascontiguousarray(hy_deltas()[cq][None, :])
        mask0 = np.ones((128, 1), f32); mask0[0, 0] = 0.0
        m["hy_mask0"] = mask0
        for tag, n in (("l", NLAT), ("c", NCTX)):
            hc = hy_consts(n)
            for k in ("featT", "ntn", "cphi", "sphi", "ncphi", "gc", "gs"):
                m[k + tag] = hc[k]
        ccols = np.concatenate([s_ * 1024 + 512 * h + np.arange(512) for s_ in range(3)])
        m["hy_cwT"] = np.ascontiguousarray(inp["hy_conv_w"][0][:, ccols].T)
        m["hy_cbs"] = np.ascontiguousarray(inp["hy_conv_b"][0][ccols][:, None])
        gs_ = slice(32 * h, 32 * h + 32)
        def chp(a):
            return np.ascontiguousarray(a[0][:, gs_].transpose(0, 1, 3, 2).reshape(2, 512, 64))
        m["s5_bre_chp"] = chp(inp["s5_b_re"]); m["s5_bim_chp"] = chp(inp["s5_b_im"])
        rep = lambda a: np.ascontiguousarray(np.repeat(a[0][:, gs_], 16, axis=1))
        m["s5_are_chp"] = rep(inp["s5_a_re"]); m["s5_aim_chp"] = rep(inp["s5_a_im"])
        m["s5_ldt_chp"] = np.ascontiguousarray(np.repeat(inp["s5_log_dt"][0][:, gs_], 16, axis=1)[:, :, None])
        pch = lambda a: np.ascontiguousarray(a[0][:, gs_].transpose(0, 3, 1, 2).reshape(2, 64, 512))
        m["s5_cre_pch"] = pch(inp["s5_c_re"]); m["s5_cim_pch"] = pch(inp["s5_c_im"])
        pg = lambda a: np.ascontiguousarray(np.concatenate([a[0][:, gs_].transpose(0, 2, 1)] * 2, axis=1))
        m["s5_are_pg"] = pg(inp["s5_a_re"]); m["s5_aim_pg"] = pg(inp["s5_a_im"])
        m["s5_ldt_row"] = np.ascontiguousarray(inp["s5_log_dt"][0][:, None, gs_])
        m["s5_ds"] = np.ascontiguousarray(inp["s5_d"][0][512 * h:512 * h + 512, None])
        perm = np.zeros((128, 128), f32); perm[np.arange(128), (np.arange(128) + 64) % 128] = 1.0
        m["s5_perm"] = perm
        sg_ = np.ones((128, 2), f32); sg_[64:, 0] = -1.0; sg_[:64, 1] = -1.0
        m["s5_sgn"] = sg_
        m["s5_tau"] = np.arange(S5_HALF, dtype=f32)[None, :]
        gm = np.zeros((128, 8), f32); gm[np.arange(128), np.arange(128) // 16] = 1.0
        m["s5_gmask"] = gm
        prow = np.concatenate([np.concatenate([64 * k + np.arange(64), 512 + 64 * k + np.arange(64)]) for k in range(8)])
        m["s5_gluw"] = np.ascontiguousarray(inp["s5_glu_w"][0][prow][:, 512 * h:512 * h + 512])
        m["s5_glub"] = np.ascontiguousarray(inp["s5_glu_b"][0][512 * h:512 * h + 512, None])
        krow = []
        for k in range(16):
            for r_ in range(2):
                loc = 64 * k + np.arange(64)
                krow.append(np.where(loc < 512, 512 * r_ + loc, 1024 + 512 * r_ + (loc - 512)))
        krow = np.concatenate(krow)
        m["outw0"] = np.ascontiguousarray(inp["ev_out_w"][0][krow][:, 1024 * h:1024 * h + 1024])
        m["xh0"] = np.ascontiguousarray(np.concatenate([inp["ctx"][b], inp["x"][b]], 0)[:, 1024 * h:1024 * h + 1024])
        hs = np.zeros((128, 2), f32); hs[:, h] = 1.0
        m["hsel"] = hs
        W1 = inp["od_in_w"][0]
        hc = np.arange(1024 * h, 1024 * h + 1024)
        m["inw1qk"] = np.ascontiguousarray(np.concatenate([W1[:, hc], W1[:, 2048 + hc]], 1))
        m["inw1vg"] = np.ascontiguousarray(np.concatenate([W1[:, 4096 + hc], W1[:, 6144 + hc]], 1))
        rc_, rs_ = rope_tables()
        m["ropeC"] = rc_; m["ropeS"] = rs_
        m["qkw"] = np.ascontiguousarray(np.stack([np.tile(inp["da_q_norm"][0], 8), np.tile(inp["da_k_norm"][0], 8)], 0))
        m["da_lqk"] = np.ascontiguousarray(np.stack([inp["da_lq1"][0], inp["da_lk1"][0], inp["da_lq2"][0], inp["da_lk2"][0]], 0))
        m["da_sublnw"] = np.ascontiguousarray(inp["da_subln"][0][:, None])
        k1 = np.concatenate([1024 * r_ + 128 * ch + np.arange(128) for ch in range(8) for r_ in range(2)])
        m["outw1"] = np.ascontiguousarray(inp["od_out_w"][0][k1][:, 1024 * h:1024 * h + 1024])
        per_core.append(m)
    return per_core


import ml_dtypes
_BF = ml_dtypes.bfloat16
_CONST_CACHE = {}


def hy_consts(n):
    if n in _CONST_CACHE:
        return _CONST_CACHE[n]
    N = 2 * n
    nch = n // 128
    a = np.arange(n, dtype=np.float64) + 0.5
    ang = 2.0 * np.pi * np.outer(a, a) / N
    out = {}
    for nm, fn in (("gc", np.cos), ("gs", np.sin)):
        g = fn(ang).astype(np.float32)
        g4 = g.reshape(nch, 128, nch, 128).transpose(2, 1, 0, 3)
        out[nm] = np.ascontiguousarray(g4).astype(_BF)
    phi = np.pi * a / N
    lay = lambda v: np.ascontiguousarray(v.reshape(nch, 128).T.astype(np.float32))
    out["cphi"] = lay(np.cos(phi))
    out["sphi"] = lay(np.sin(phi))
    out["ncphi"] = lay(-np.cos(phi))
    f32 = np.float32
    t = np.arange(n, dtype=f32)
    tn = (t / f32(n)).astype(f32)
    bands = np.linspace(1e-4, 15, 16, dtype=f32)
    angf = (f32(2.0 * math.pi / n) * t[:, None] * bands[None, :]).astype(f32)
    feat = np.concatenate([tn[:, None], np.cos(angf), -np.sin(angf)], axis=-1).astype(f32)
    out["featT"] = np.ascontiguousarray(feat.T)
    out["ntn"] = lay(-tn)
    _CONST_CACHE[n] = out
    return out


def hy_deltas():
    f32 = np.float32
    return np.abs(np.linspace(math.log(1e-2) / 1.5, math.log(1e-2) / 0.3, 1024, dtype=f32)).astype(f32)


def phase_filter(kb, n, tag):
    P = kb.P
    kb.phase()
    nlc = n // 128
    N = 2 * n
    blk = min(512, n)
    w1d = kb.dram("hy_w1", [33, 64]); b1d = kb.dram("hy_b1", [64, 1]); w2d = kb.dram("hy_w2", [64, 64]); b2d = kb.dram("hy_b2", [64, 1])
    w3d = kb.dram("hy_w3s", [64, 512]); b3d = kb.dram("hy_b3s", [1, 512]); frd = kb.dram("hy_freqT", [64, 2]); skd = kb.dram("hy_skips", [1, 256])
    dld = kb.dram("hy_delta", [1, 128]); m0d = kb.dram("hy_mask0", [128, 1])
    featd = kb.dram(f"featT{tag}", [33, n]); ntnd = kb.dram(f"ntn{tag}", [128, nlc])
    cphd = kb.dram(f"cphi{tag}", [128, nlc]); sphd = kb.dram(f"sphi{tag}", [128, nlc]); ncphd = kb.dram(f"ncphi{tag}", [128, nlc])
    gcd = kb.dram(f"gc{tag}", [nlc, 128, nlc, 128], BF16); gsd = kb.dram(f"gs{tag}", [nlc, 128, nlc, 128], BF16)
    KF = kb.dram(f"KF{tag}", [2 * n, 256])

    w1 = kb.sb("w1", [33, 64]); w2 = kb.sb("w2", [64, 64]); w3 = kb.sb("w3", [64, 512])
    b12 = kb.sb("b12", [64, 2]); fr = kb.sb("fr", [64, 2]); fs = kb.sb("fs", [64, 2]); fb = kb.sb("fb", [64, 2])
    b3r = kb.sb("b3r", [128, 512]); skr = kb.sb("skr", [128, 256]); dlr = kb.sb("dlr", [128, 128]); m0 = kb.sb("m0", [128, 1])
    ntn = kb.sb("ntn", [128, nlc]); cph = kb.sb("cph", [128, nlc]); sph = kb.sb("sph", [128, nlc]); ncph = kb.sb("ncph", [128, nlc])
    ones = kb.sb("ones", [128, 128])
    feat = kb.sb("feat", [33, n]); h1T = kb.sb("h1T", [64, n]); h2T = kb.sb("h2T", [64, n])
    tt = kb.sb("tt", [64, 512]); ti = kb.sb("ti", [64, 512], I32)
    hall = kb.sb("hall", [128, nlc, 512])
    dec = kb.sb("dec", [128, 128]); absh = kb.sb("absh", [128, 512])
    rs = kb.sb("rs", [128, 256])
    Pp = kb.sb("Pp", [128, nlc, 256], BF16); Pm = kb.sb("Pm", [128, nlc, 256], BF16)
    tmpa = kb.sb("tmpa", [128, 256]); tmpb = kb.sb("tmpb", [128, 256])
    G = [[kb.sb(f"G{i}{j}", [128, nlc, 128], BF16) for j in range(2)] for i in range(2)]
    ko = [[kb.sb(f"ko{i}{j}", [128, 256]) for j in range(2)] for i in range(2)]
    psA = kb.ps("psA", 0, [64, 512]); psB = kb.ps("psB", 1, [128, 512]); psL = kb.ps("psL", 2, [128, 512])
    psT = [kb.ps(f"psT{i}", 3 + i, [128, 256]) for i in range(4)]

    for (t_, d_) in ((w1, w1d), (w2, w2d), (w3, w3d), (m0, m0d), (ntn, ntnd), (cph, cphd), (sph, sphd), (ncph, ncphd), (feat, featd)):
        DMA(P, kb.q(), t_.ap, d_, writes=[t_])
    DMA(P, kb.q(), b12[:, 0:1], b1d, writes=[b12.key + ":0"]); DMA(P, kb.q(), b12[:, 1:2], b2d, writes=[b12.key + ":1"])
    DMA(P, kb.q(), fr.ap, frd, writes=[fr])
    DMA(P, kb.q(), b3r.ap, b3d.broadcast_to([128, 512]), writes=[b3r])
    DMA(P, kb.q(), skr.ap, skd.broadcast_to([128, 256]), writes=[skr])
    DMA(P, kb.q(), dlr.ap, dld.broadcast_to([128, 128]), writes=[dlr])
    E(P, "pool", "memset", writes=[ones], ap=ones.ap, constant=1.0)
    E(P, "dve", "tensor_scalar", reads=[fr], writes=[fs], out=fs.ap, in0=fr.ap, scalar1=1.0 / TWO_PI, scalar2=None, op0=ALU.mult)
    E(P, "dve", "tensor_tensor", reads=[fs, b12.key + ":0", b12.key + ":1"], writes=[fb], out=fb.ap, in0=fs.ap, in1=b12.ap, op=ALU.mult)
    E(P, "dve", "tensor_scalar", reads=[skr], writes=[skr], out=skr.ap, in0=skr.ap, scalar1=2.0 / N, scalar2=None, op0=ALU.mult)

    import os
    _stop = int(os.environ.get("FILT_STOP", "99"))
    if _stop <= 0:
        return
    for layer, (wt, src, dst) in enumerate(((w1, feat, h1T), (w2, h1T, h2T))):
        for b0 in range(0, n, blk):
            E(P, "pe", "matmul", reads=[wt, src], writes=[psA], out=psA[:, 0:blk], lhsT=wt.ap, rhs=src[:, b0:b0 + blk], start=True, stop=True)
            E(P, "dve", "tensor_scalar", reads=[psA, fs, fb], writes=[tt], out=tt[:, 0:blk], in0=psA[:, 0:blk], scalar1=fs[:, layer:layer + 1],
              scalar2=fb[:, layer:layer + 1], op0=ALU.mult, op1=ALU.add)
            E(P, "dve", "tensor_copy", reads=[tt], writes=[ti], out=ti[:, 0:blk], in_=tt[:, 0:blk])
            E(P, "dve", "tensor_tensor", reads=[tt, ti], writes=[tt], out=tt[:, 0:blk], in0=tt[:, 0:blk], in1=ti[:, 0:blk], op=ALU.subtract)
            E(P, "act", "activation", reads=[tt], writes=[dst], out=dst[:, b0:b0 + blk], in_=tt[:, 0:blk], func=AF.Sin, scale=TWO_PI)
    if _stop <= 1:
        return
    for lc in range(nlc):
        E(P, "pe", "matmul", reads=[h2T, w3], writes=[psB], out=psB.ap, lhsT=h2T[:, lc * 128:(lc + 1) * 128], rhs=w3.ap, start=True, stop=True)
        hk = hall.key + f":{lc}"
        E(P, "dve", "tensor_tensor", reads=[psB, b3r], writes=[hk], out=hall[:, lc, :], in0=psB.ap, in1=b3r.ap, op=ALU.add)
        E(P, "act", "activation", reads=[dlr, ntn], writes=[dec], out=dec.ap, in_=dlr.ap, func=AF.Exp, scale=ntn[:, lc:lc + 1])
        for q4 in range(4):
            E(P, "pool" if q4 % 2 else "dve", "tensor_tensor", reads=[hk, dec], writes=[hk], out=hall[:, lc, q4 * 128:(q4 + 1) * 128],
              in0=hall[:, lc, q4 * 128:(q4 + 1) * 128], in1=dec.ap, op=ALU.mult)
        if lc == 0:
            for q4 in (1, 3):
                E(P, "dve", "tensor_scalar", reads=[hk, m0], writes=[hk], out=hall[:, 0, q4 * 128:(q4 + 1) * 128],
                  in0=hall[:, 0, q4 * 128:(q4 + 1) * 128], scalar1=m0[:, 0:1], scalar2=None, op0=ALU.mult)
        E(P, "act", "activation", reads=[hk], writes=[absh], out=absh.ap, in_=hall[:, lc, :], func=AF.Abs)
        E(P, "pe", "matmul", reads=[ones, absh], writes=[psL], out=psL.ap, lhsT=ones.ap, rhs=absh.ap, start=(lc == 0), stop=(lc == nlc - 1))
    if _stop <= 2:
        return
    l1v = psL.ap.rearrange("p (o d c) -> p o d c", o=2, d=2)
    rs3 = rs.ap.rearrange("p (o c) -> p o c", o=2)
    E(P, "act", "activation", reads=[psL], writes=[absh], out=absh.ap, in_=psL.ap, func=AF.Copy)
    ab4 = absh.ap.rearrange("p (o d c) -> p o d c", o=2, d=2)
    E(P, "dve", "tensor_tensor", reads=[absh], writes=[rs], out=rs3, in0=ab4[:, :, 0, :], in1=ab4[:, :, 1, :], op=ALU.add)
    E(P, "dve", "reciprocal", reads=[rs], writes=[rs], out=rs.ap, in_=rs.ap)
    E(P, "dve", "tensor_scalar", reads=[rs], writes=[rs], out=rs.ap, in0=rs.ap, scalar1=2.0 / N, scalar2=None, op0=ALU.mult)
    for lc in range(nlc):
        hk = hall.key + f":{lc}"
        h4 = hall[:, lc, :].rearrange("p (o d c) -> p o d c", o=2, d=2)
        E(P, "dve", "tensor_tensor", reads=[hk], writes=[tmpa], out=tmpa.ap.rearrange("p (o c) -> p o c", o=2), in0=h4[:, :, 0, :], in1=h4[:, :, 1, :], op=ALU.add)
        E(P, "pool", "tensor_tensor", reads=[hk], writes=[tmpb], out=tmpb.ap.rearrange("p (o c) -> p o c", o=2), in0=h4[:, :, 0, :], in1=h4[:, :, 1, :], op=ALU.subtract)
        E(P, "dve", "tensor_tensor", reads=[tmpa, rs], writes=[Pp.key + f":{lc}"], out=Pp[:, lc, :], in0=tmpa.ap, in1=rs.ap, op=ALU.mult)
        E(P, "pool", "tensor_tensor", reads=[tmpb, rs], writes=[Pm.key + f":{lc}"], out=Pm[:, lc, :], in0=tmpb.ap, in1=rs.ap, op=ALU.mult)
    if _stop <= 3:
        return
    pkeys = [Pp.key + f":{lc}" for lc in range(nlc)]
    mkeys = [Pm.key + f":{lc}" for lc in range(nlc)]
    for fc in range(nlc):
        gc_, gs_ = G[fc % 2]
        DMA(P, "sp", gc_.ap, gcd[fc], writes=[gc_])
        DMA(P, "act", gs_.ap, gsd[fc], writes=[gs_])
        for i, (g_, src, keys) in enumerate(((gc_, Pp, pkeys), (gs_, Pp, pkeys), (gc_, Pm, mkeys), (gs_, Pm, mkeys))):
            for lc in range(nlc):
                E(P, "pe", "matmul", reads=[g_] + keys, writes=[psT[i]], out=psT[i].ap, lhsT=g_[:, lc, :], rhs=src[:, lc, :], start=(lc == 0),
                  stop=(lc == nlc - 1))
        kre, kim = ko[fc % 2]
        _sub = int(os.environ.get("FILT_SUB", "0"))
        if _sub == 1:
            continue
        E(P, "dve", "tensor_scalar", reads=[psT[0], cph], writes=[kre], out=kre.ap, in0=psT[0].ap, scalar1=cph[:, fc:fc + 1], scalar2=None, op0=ALU.mult)
        E(P, "dve", "scalar_tensor_tensor", reads=[psT[1], sph, kre], writes=[kre], out=kre.ap, in0=psT[1].ap, scalar=sph[:, fc:fc + 1], in1=kre.ap,
          op0=ALU.mult, op1=ALU.add)
        E(P, "pool", "tensor_tensor", reads=[kre, skr], writes=[kre], out=kre.ap, in0=kre.ap, in1=skr.ap, op=ALU.add)
        E(P, "dve", "tensor_scalar", reads=[psT[3], ncph], writes=[kim], out=kim.ap, in0=psT[3].ap, scalar1=ncph[:, fc:fc + 1], scalar2=None, op0=ALU.mult)
        E(P, "dve", "scalar_tensor_tensor", reads=[psT[2], sph, kim], writes=[kim], out=kim.ap, in0=psT[2].ap, scalar=sph[:, fc:fc + 1], in1=kim.ap,
          op0=ALU.mult, op1=ALU.add)
        if _sub == 2:
            continue
        DMA(P, "pool", KF[fc * 128:(fc + 1) * 128, :], kre.ap, reads=[kre], writes=[f"KF{tag}"])
        DMA(P, "pool", KF[n + fc * 128:n + (fc + 1) * 128, :], kim.ap, reads=[kim], writes=[f"KF{tag}"])
    if _stop <= 4:
        return
    allgather(kb, f"KF{tag}", f"KFg{tag}", 2 * n, 256, F32, SAMEH)


def phase_shortconv(kb, n, tag, col0):
    P = kb.P
    kb.phase()
    nlc = n // 128
    proj = kb.dram("proj0", [3072, NTOK])
    cwd = kb.dram("hy_cwT", [1536, 3]); cbd = kb.dram("hy_cbs", [1536, 1])
    outs = [kb.dram(f"{nm}{tag}", [n, 512], BF16) for nm in ("vT", "x1T", "x2gT")]
    idb = kb.C["idb"]
    us = [kb.sb(f"u{i}", [128, n]) for i in range(2)]
    scs = [kb.sb(f"sc{i}", [128, n]) for i in range(2)]
    hg = kb.sb("hg", [128, n])
    scb = kb.sb("scb", [128, n], BF16)
    cws = [kb.sb(f"cw{i}", [128, 4]) for i in range(2)]
    stg = [kb.sb(f"stg{i}", [128, 8, 128], BF16) for i in range(2)]
    pTs = [kb.ps(f"scpT{i}", i, [128, 8, 128], BF16) for i in range(2)]
    it = 0
    g8 = min(8, nlc)
    for seg in range(3):
        for cc in range(4):
            u, sc, cw = us[it % 2], scs[it % 2], cws[it % 2]
            r0 = seg * 512 + cc * 128
            DMA(P, kb.q(), u.ap, proj[r0:r0 + 128, col0:col0 + n], reads=["proj0"], writes=[u])
            DMA(P, kb.q(), cw[:, 0:3], cwd[r0:r0 + 128, :], writes=[cw.key + ":w"])
            DMA(P, kb.q(), cw[:, 3:4], cbd[r0:r0 + 128, :], writes=[cw.key + ":b"])
            ck = [cw.key + ":w", cw.key + ":b"]
            E(P, "act", "activation", reads=[u] + ck, writes=[sc], out=sc.ap, in_=u.ap, func=AF.Identity, bias=cw[:, 3:4], scale=cw[:, 1:2])
            E(P, "dve", "scalar_tensor_tensor", reads=[u, sc] + ck, writes=[sc], out=sc[:, 1:n], in0=u[:, 0:n - 1], scalar=cw[:, 0:1], in1=sc[:, 1:n],
              op0=ALU.mult, op1=ALU.add)
            E(P, "dve", "scalar_tensor_tensor", reads=[u, sc] + ck, writes=[sc], out=sc[:, 0:n - 1], in0=u[:, 1:n], scalar=cw[:, 2:3], in1=sc[:, 0:n - 1],
              op0=ALU.mult, op1=ALU.add)
            if seg == 2:
                DMA(P, kb.q(), hg.ap, proj[3 * 512 + cc * 128:3 * 512 + (cc + 1) * 128, col0:col0 + n], reads=["proj0"], writes=[hg])
                E(P, "act", "activation", reads=[hg], writes=[hg], out=hg.ap, in_=hg.ap, func=AF.Silu)
                E(P, "dve", "tensor_tensor", reads=[sc, hg], writes=[scb], out=scb.ap, in0=sc.ap, in1=hg.ap, op=ALU.mult)
            else:
                E(P, "act", "activation", reads=[sc], writes=[scb], out=scb.ap, in_=sc.ap, func=AF.Copy)
            outd = outs[seg].rearrange("(tc p) c -> p tc c", p=128)
            for t0 in range(0, nlc, g8):
                j = (t0 // g8) % 2
                for tq in range(g8):
                    E(P, "pe", "transpose", reads=[scb, idb], writes=[pTs[j]], out=pTs[j][:, tq, :], in_=scb[:, (t0 + tq) * 128:(t0 + tq + 1) * 128],
                      identity=idb.ap)
                if j == 0:
                    E(P, "dve", "tensor_copy", reads=[pTs[j]], writes=[stg[j]], out=stg[j][:, 0:g8, :], in_=pTs[j][:, 0:g8, :])
                else:
                    E(P, "act", "activation", reads=[pTs[j]], writes=[stg[j]], out=stg[j][:, 0:g8, :], in_=pTs[j][:, 0:g8, :], func=AF.Copy)
                DMA(P, kb.q("st"), outd[:, t0:t0 + g8, cc * 128:(cc + 1) * 128], stg[j][:, 0:g8, :], reads=[stg[j]], writes=[f"{('vT', 'x1T', 'x2gT')[seg]}{tag}"])
            it += 1


def phase_hyconv(kb, n, tag, col0):
    P = kb.P
    kb.phase()
    nlc = n // 128
    idb = kb.C["idb"]
    gcd = kb.dram(f"gc{tag}", [nlc, 128, nlc, 128], BF16); gsd = kb.dram(f"gs{tag}", [nlc, 128, nlc, 128], BF16)
    KFg = kb.dram(f"KFg{tag}", [4 * 2 * n, 256])
    rc = ag_rows(2 * n, 256, F32)
    nch = 2 * n // rc
    kv = KFg.rearrange("(ch k p) (o c) -> ch o p k c", ch=nch, k=4, o=2)

    def kslice(ri, o, fc):
        row = ri * n + fc * 128
        return kv[row // rc, o, row % rc: row % rc + 128]
    srcs = [kb.dram(f"{nm}{tag}", [n, 512], BF16).rearrange("(tc p) c -> p tc c", p=128) for nm in ("vT", "x1T", "x2gT")]
    mixT = kb.dram("mixT", [1024, NTOK], BF16)
    A = kb.sb("hyA", [128, nlc, 512], BF16)
    B = kb.sb("hyB", [128, nlc, 512], BF16)
    Yp = kb.sb("Yp", [128, nlc, 512], BF16)
    Yq = kb.sb("Yq", [128, nlc, 512], BF16)
    G = [[kb.sb(f"hG{i}{j}", [128, nlc, 128], BF16) for j in range(2)] for i in range(2)]
    Kt = [[kb.sb(f"hK{i}{j}", [128, 4, 128]) for j in range(2)] for i in range(2)]
    UV = [[kb.sb(f"hUV{i}{j}", [128, 512]) for j in range(2)] for i in range(2)]
    tmp = [kb.sb(f"htmp{i}", [128, 512]) for i in range(4)]
    mt = [kb.sb(f"hmt{i}", [128, 512], BF16) for i in range(2)]
    g4 = min(4, nlc)
    mstg = [kb.sb(f"hmst{i}", [128, 4, g4 * 128], BF16) for i in range(2)]
    psUV = [[kb.ps(f"psU{i}{j}", 2 * i + j, [128, 512]) for j in range(2)] for i in range(2)]
    psI = [kb.ps(f"psI{i}", 4 + i, [128, 512]) for i in range(2)]
    psM = [kb.ps(f"psM{i}", 6 + i, [128, 4, 128], BF16) for i in range(2)]

    def akeys(t_):
        return [t_.key + f":{i}" for i in range(nlc)]

    for tcn in range(nlc):
        DMA(P, kb.q(), A[:, tcn, :], srcs[0][:, tcn, :], reads=[f"vT{tag}"], writes=[A.key + f":{tcn}"])
        DMA(P, kb.q(), B[:, tcn, :], srcs[1][:, tcn, :], reads=[f"x1T{tag}"], writes=[B.key + f":{tcn}"])

    for o in range(2):
        inT = A if o == 0 else B
        for fc in range(nlc):
            gc_, gs_ = G[fc % 2]
            kre, kim = Kt[fc % 2]
            DMA(P, "sp", gc_.ap, gcd[fc], writes=[gc_])
            DMA(P, "act", gs_.ap, gsd[fc], writes=[gs_])
            DMA(P, "pool", kre.ap, kslice(0, o, fc), reads=[f"KFg{tag}"], writes=[kre])
            DMA(P, "pool", kim.ap, kslice(1, o, fc), reads=[f"KFg{tag}"], writes=[kim])
            pu, pv = psUV[fc % 2]
            for (g_, ps_) in ((gc_, pu), (gs_, pv)):
                for tcn in range(nlc):
                    E(P, "pe", "matmul", reads=[g_, inT.key + f":{tcn}"], writes=[ps_], out=ps_.ap, lhsT=g_[:, tcn, :], rhs=inT[:, tcn, :], start=(tcn == 0),
                      stop=(tcn == nlc - 1))
            us_, vs_ = UV[fc % 2]
            E(P, "act", "activation", reads=[pu], writes=[us_], out=us_.ap, in_=pu.ap, func=AF.Copy)
            E(P, "act", "activation", reads=[pv], writes=[vs_], out=vs_.ap, in_=pv.ap, func=AF.Copy)
            kr = kre.ap.rearrange("p k c -> p (k c)")
            ki = kim.ap.rearrange("p k c -> p (k c)")
            E(P, "dve", "tensor_tensor", reads=[us_, kre], writes=[tmp[0]], out=tmp[0].ap, in0=us_.ap, in1=kr, op=ALU.mult)
            E(P, "dve", "tensor_tensor", reads=[vs_, kim], writes=[tmp[1]], out=tmp[1].ap, in0=vs_.ap, in1=ki, op=ALU.mult)
            E(P, "dve", "tensor_tensor", reads=[tmp[0], tmp[1]], writes=[Yp.key + f":{fc}"], out=Yp[:, fc, :], in0=tmp[0].ap, in1=tmp[1].ap, op=ALU.add)
            E(P, "pool", "tensor_tensor", reads=[vs_, kre], writes=[tmp[2]], out=tmp[2].ap, in0=vs_.ap, in1=kr, op=ALU.mult)
            E(P, "pool", "tensor_tensor", reads=[us_, kim], writes=[tmp[3]], out=tmp[3].ap, in0=us_.ap, in1=ki, op=ALU.mult)
            E(P, "pool", "tensor_tensor", reads=[tmp[2], tmp[3]], writes=[Yq.key + f":{fc}"], out=Yq[:, fc, :], in0=tmp[2].ap, in1=tmp[3].ap, op=ALU.subtract)
        if o == 0:
            for tcn in range(nlc):
                DMA(P, kb.q(), A[:, tcn, :], srcs[2][:, tcn, :], reads=[f"x2gT{tag}"], writes=[A.key + f":{tcn}"])
        for tcn in range(nlc):
            gc_, gs_ = G[tcn % 2]
            DMA(P, "sp", gc_.ap, gcd[tcn], writes=[gc_])
            DMA(P, "act", gs_.ap, gsd[tcn], writes=[gs_])
            pi = psI[tcn % 2]
            for gi, (g_, Y_) in enumerate(((gc_, Yp), (gs_, Yq))):
                for fc in range(nlc):
                    E(P, "pe", "matmul", reads=[g_, Y_.key + f":{fc}"], writes=[pi], out=pi.ap, lhsT=g_[:, fc, :], rhs=Y_[:, fc, :],
                      start=(gi == 0 and fc == 0), stop=(gi == 1 and fc == nlc - 1))
            if o == 0:
                E(P, "dve", "tensor_tensor", reads=[pi, B.key + f":{tcn}"], writes=[B.key + f":{tcn}"], out=B[:, tcn, :], in0=pi.ap, in1=B[:, tcn, :], op=ALU.mult)
            else:
                m_ = mt[tcn % 2]
                E(P, "dve", "tensor_tensor", reads=[pi, A.key + f":{tcn}"], writes=[m_], out=m_.ap, in0=pi.ap, in1=A[:, tcn, :], op=ALU.mult)
                pm = psM[tcn % 2]
                for cc in range(4):
                    E(P, "pe", "transpose", reads=[m_, idb], writes=[pm], out=pm[:, cc, :], in_=m_[:, cc * 128:(cc + 1) * 128], identity=idb.ap)
                sidx = (tcn // g4) % 2
                q = tcn % g4
                E(P, "act", "activation", reads=[pm], writes=[mstg[sidx].key + f":{q}"], out=mstg[sidx][:, :, q * 128:(q + 1) * 128], in_=pm.ap, func=AF.Copy)
                if q == g4 - 1:
                    t0 = (tcn // g4) * g4 * 128
                    DMA(P, kb.q("st"), mixT[0:512, col0 + t0:col0 + t0 + g4 * 128].rearrange("(k c) t -> c k t", k=4), mstg[sidx].ap,
                        reads=[mstg[sidx].key + f":{i}" for i in range(g4)], writes=["mixT"])


S5_HALF = 1024
S5_PIECES = [(0, 256), (256, 1280), (1280, 2304), (2304, 3328), (3328, 4352)]


def phase_s5(kb):
    P = kb.P
    kb.phase()
    proj = kb.dram("proj0", [3072, NTOK])
    bre = kb.dram("s5_bre_chp", [2, 512, 64]); bim = kb.dram("s5_bim_chp", [2, 512, 64])
    are_c = kb.dram("s5_are_chp", [2, 512, 64]); aim_c = kb.dram("s5_aim_chp", [2, 512, 64]); ldt_c = kb.dram("s5_ldt_chp", [2, 512, 1])
    cre = kb.dram("s5_cre_pch", [2, 64, 512]); cim = kb.dram("s5_cim_pch", [2, 64, 512])
    are_p = kb.dram("s5_are_pg", [2, 128, 32]); aim_p = kb.dram("s5_aim_pg", [2, 128, 32]); ldt_r = kb.dram("s5_ldt_row", [2, 1, 32])
    dd = kb.dram("s5_ds", [512, 1]); permd = kb.dram("s5_perm", [128, 128]); sgnd = kb.dram("s5_sgn", [128, 2]); taud = kb.dram("s5_tau", [1, S5_HALF])
    gmaskd = kb.dram("s5_gmask", [128, 8])
    gT = kb.dram("s5_gT", [512, NTOK], BF16)
    g32 = kb.dram("s5_g32", [512, NTOK])

    H = S5_HALF
    su32 = kb.sb("su32", [128, NTOK]); yacc = kb.sb("yacc", [128, NTOK]); su16 = kb.sb("su16", [128, NTOK], BF16)
    tau = kb.sb("tau", [128, H])
    sets = [dict(tt=kb.sb(f"s5tt{i}", [128, H]), ti=kb.sb(f"s5ti{i}", [128, H], I32), Ct=kb.sb(f"Ct{i}", [128, H]), St=kb.sb(f"St{i}", [128, H]),
                 vv=kb.sb(f"vv{i}", [128, H]), st=kb.sb(f"st{i}", [128, H])) for i in range(2)]
    dts = [kb.sb(f"dtile{i}", [128, H]) for i in range(2)]
    carries = kb.sb("carries", [128, 8])
    ta2 = [kb.sb(f"s5ta2{i}", [128, H]) for i in range(2)]
    tb2 = [kb.sb(f"s5tb2{i}", [128, H]) for i in range(2)]
    cnt = {"a": 0, "b": 0}
    hpi = kb.sb("hpi", [128, 1])
    Sb = [kb.sb(f"Sb{i}", [128, H], BF16) for i in range(2)]
    ta = [kb.sb(f"s5ta{i}", [128, H]) for i in range(2)]
    tb = [kb.sb(f"s5tb{i}", [128, H]) for i in range(2)]
    perm = kb.sb("perm", [128, 128]); sgn = kb.sb("sgn", [128, 2]); gmask = kb.sb("gmask", [128, 8])
    thp = kb.sb("thp", [128, 2, 32]); mdp = kb.sb("mdp", [128, 2, 32]); ldr = kb.sb("ldr", [128, 2, 32]); aip = kb.sb("aip", [128, 2, 32])
    dcol = kb.sb("dcol", [128, 1])
    thpq = [kb.sb(f"thpq{q}", [128, 2, 32]) for q in range(5)]
    cb = {nm: kb.sb("c_" + nm, [128, 64]) for nm in ("bre", "bim", "are", "aim", "mag", "cs", "sn", "lr", "li", "den", "cr", "ci", "t0", "t1", "t2", "t3")}
    cti = kb.sb("c_ti", [128, 64], I32)
    ldc = kb.sb("ldc", [128, 2])
    Ball = [kb.sb(f"Ball{v}", [128, 128]) for v in range(2)]
    Bpad = [[kb.sb(f"Bpad{v}_{g}", [128, 128], BF16) for g in range(8)] for v in range(2)]
    Call = kb.sb("Call", [128, 128])
    Cpad = [kb.sb(f"Cpad{g}", [128, 128], BF16) for g in range(8)]
    psB = [kb.ps(f"s5bu{v}", 2 * v, [128, 1024]) for v in range(2)]
    psW = kb.ps("s5sw", 4, [128, 1024])
    psY = kb.ps("s5y", 6, [128, 1024])

    DMA(P, "sp", perm.ap, permd, writes=[perm]); DMA(P, "act", sgn.ap, sgnd, writes=[sgn]); DMA(P, "pool", gmask.ap, gmaskd, writes=[gmask])
    DMA(P, "sp", tau.ap, taud.broadcast_to([128, H]), writes=[tau])
    for d in range(2):
        DMA(P, kb.q(), thp[:, d, :], are_p[d], writes=[thp.key + f":{d}"])
        DMA(P, kb.q(), aip[:, d, :], aim_p[d], writes=[aip.key + f":{d}"])
        DMA(P, kb.q(), ldr[:, d, :], ldt_r[d].broadcast_to([128, 32]), writes=[ldr.key + f":{d}"])
    tk = [thp.key + ":0", thp.key + ":1"]; ak = [aip.key + ":0", aip.key + ":1"]; lk = [ldr.key + ":0", ldr.key + ":1"]
    E(P, "act", "activation", reads=lk, writes=lk, out=ldr.ap, in_=ldr.ap, func=AF.Exp)
    E(P, "dve", "tensor_tensor", reads=tk + lk, writes=[mdp], out=mdp.ap, in0=thp.ap, in1=ldr.ap, op=ALU.mult)
    E(P, "act", "activation", reads=[mdp], writes=[mdp], out=mdp.ap, in_=mdp.ap, func=AF.Exp)
    E(P, "dve", "tensor_tensor", reads=ak + lk, writes=tk, out=thp.ap, in0=aip.ap, in1=ldr.ap, op=ALU.mult)
    E(P, "dve", "tensor_scalar", reads=tk, writes=tk, out=thp.ap, in0=thp.ap, scalar1=1.0 / TWO_PI, scalar2=None, op0=ALU.mult)

    for q in range(5):
        E(P, "dve", "tensor_scalar", reads=tk, writes=[thpq[q]], out=thpq[q].ap, in0=thp.ap, scalar1=float(S5_PIECES[q][0]), scalar2=None, op0=ALU.mult)
    E(P, "pool", "memset", writes=[hpi], ap=hpi.ap, constant=math.pi / 2)

    def sincos(src, n, cs_out, sn_out, t_, ti_, eng_sub="pool"):
        E(P, "dve", "tensor_copy", reads=[src[1]], writes=[ti_[1]], out=ti_[0], in_=src[0])
        E(P, eng_sub, "tensor_tensor", reads=[src[1], ti_[1]], writes=[t_[1]], out=t_[0], in0=src[0], in1=ti_[0], op=ALU.subtract)
        E(P, "act", "activation", reads=[t_[1]], writes=[sn_out[1]], out=sn_out[0], in_=t_[0], func=AF.Sin, scale=TWO_PI)
        E(P, "dve", "tensor_scalar", reads=[src[1]], writes=[t_[1]], out=t_[0], in0=src[0], scalar1=0.25, scalar2=None, op0=ALU.add)
        E(P, "dve", "tensor_copy", reads=[t_[1]], writes=[ti_[1]], out=ti_[0], in_=t_[0])
        E(P, eng_sub, "tensor_tensor", reads=[t_[1], ti_[1]], writes=[t_[1]], out=t_[0], in0=t_[0], in1=ti_[0], op=ALU.subtract)
        E(P, "act", "activation", reads=[t_[1]], writes=[cs_out[1]], out=cs_out[0], in_=t_[0], func=AF.Sin, scale=TWO_PI)

    for cc in range(4):
        r0 = 4 * 512 + cc * 128
        DMA(P, "sp", su32.ap, proj[r0:r0 + 128, :], reads=["proj0"], writes=[su32])
        DMA(P, "act", dcol.ap, dd[cc * 128:(cc + 1) * 128, :], writes=[dcol])
        E(P, "dve", "tensor_scalar", reads=[su32, dcol], writes=[yacc], out=yacc.ap, in0=su32.ap, scalar1=dcol[:, 0:1], scalar2=None, op0=ALU.mult)
        for d in range(2):
            for nm, src in (("bre", bre), ("bim", bim), ("are", are_c), ("aim", aim_c)):
                DMA(P, kb.q(), cb[nm].ap, src[d, cc * 128:(cc + 1) * 128, :], writes=[cb[nm]])
            DMA(P, kb.q(), ldc[:, 0:1], ldt_c[d, cc * 128:(cc + 1) * 128, :], writes=[ldc])
            E(P, "act", "activation", reads=[ldc], writes=[ldc], out=ldc[:, 1:2], in_=ldc[:, 0:1], func=AF.Exp)
            E(P, "act", "activation", reads=[cb["are"], ldc], writes=[cb["mag"]], out=cb["mag"].ap, in_=cb["are"].ap, func=AF.Exp, scale=ldc[:, 1:2])
            E(P, "dve", "tensor_scalar", reads=[cb["aim"], ldc], writes=[cb["t0"]], out=cb["t0"].ap, in0=cb["aim"].ap, scalar1=ldc[:, 1:2], scalar2=1.0 / TWO_PI,
              op0=ALU.mult, op1=ALU.mult)
            sincos((cb["t0"].ap, cb["t0"]), 64, (cb["cs"].ap, cb["cs"]), (cb["sn"].ap, cb["sn"]), (cb["t1"].ap, cb["t1"]), (cti.ap, cti), eng_sub="dve")
            E(P, "dve", "tensor_tensor", reads=[cb["mag"], cb["cs"]], writes=[cb["lr"]], out=cb["lr"].ap, in0=cb["mag"].ap, in1=cb["cs"].ap, op=ALU.mult)
            E(P, "dve", "tensor_scalar", reads=[cb["lr"]], writes=[cb["lr"]], out=cb["lr"].ap, in0=cb["lr"].ap, scalar1=-1.0, scalar2=None, op0=ALU.add)
            E(P, "dve", "tensor_tensor", reads=[cb["mag"], cb["sn"]], writes=[cb["li"]], out=cb["li"].ap, in0=cb["mag"].ap, in1=cb["sn"].ap, op=ALU.mult)
            E(P, "dve", "tensor_tensor", reads=[cb["are"]], writes=[cb["den"]], out=cb["den"].ap, in0=cb["are"].ap, in1=cb["are"].ap, op=ALU.mult)
            E(P, "dve", "tensor_tensor", reads=[cb["aim"]], writes=[cb["t0"]], out=cb["t0"].ap, in0=cb["aim"].ap, in1=cb["aim"].ap, op=ALU.mult)
            E(P, "dve", "tensor_tensor", reads=[cb["den"], cb["t0"]], writes=[cb["den"]], out=cb["den"].ap, in0=cb["den"].ap, in1=cb["t0"].ap, op=ALU.add)
            E(P, "dve", "reciprocal", reads=[cb["den"]], writes=[cb["den"]], out=cb["den"].ap, in_=cb["den"].ap)
            E(P, "dve", "tensor_tensor", reads=[cb["lr"], cb["are"]], writes=[cb["t0"]], out=cb["t0"].ap, in0=cb["lr"].ap, in1=cb["are"].ap, op=ALU.mult)
            E(P, "dve", "tensor_tensor", reads=[cb["li"], cb["aim"]], writes=[cb["t1"]], out=cb["t1"].ap, in0=cb["li"].ap, in1=cb["aim"].ap, op=ALU.mult)
            E(P, "dve", "tensor_tensor", reads=[cb["t0"], cb["t1"]], writes=[cb["cr"]], out=cb["cr"].ap, in0=cb["t0"].ap, in1=cb["t1"].ap, op=ALU.add)
            E(P, "dve", "tensor_tensor", reads=[cb["cr"], cb["den"]], writes=[cb["cr"]], out=cb["cr"].ap, in0=cb["cr"].ap, in1=cb["den"].ap, op=ALU.mult)
            E(P, "dve", "tensor_tensor", reads=[cb["li"], cb["are"]], writes=[cb["t0"]], out=cb["t0"].ap, in0=cb["li"].ap, in1=cb["are"].ap, op=ALU.mult)
            E(P, "dve", "tensor_tensor", reads=[cb["lr"], cb["aim"]], writes=[cb["t1"]], out=cb["t1"].ap, in0=cb["lr"].ap, in1=cb["aim"].ap, op=ALU.mult)
            E(P, "dve", "tensor_tensor", reads=[cb["t0"], cb["t1"]], writes=[cb["ci"]], out=cb["ci"].ap, in0=cb["t0"].ap, in1=cb["t1"].ap, op=ALU.subtract)
            E(P, "dve", "tensor_tensor", reads=[cb["ci"], cb["den"]], writes=[cb["ci"]], out=cb["ci"].ap, in0=cb["ci"].ap, in1=cb["den"].ap, op=ALU.mult)
            E(P, "dve", "tensor_tensor", reads=[cb["cr"], cb["bre"]], writes=[cb["t0"]], out=cb["t0"].ap, in0=cb["cr"].ap, in1=cb["bre"].ap, op=ALU.mult)
            E(P, "dve", "tensor_tensor", reads=[cb["ci"], cb["bim"]], writes=[cb["t1"]], out=cb["t1"].ap, in0=cb["ci"].ap, in1=cb["bim"].ap, op=ALU.mult)
            E(P, "dve", "tensor_tensor", reads=[cb["cr"], cb["bim"]], writes=[cb["t2"]], out=cb["t2"].ap, in0=cb["cr"].ap, in1=cb["bim"].ap, op=ALU.mult)
            E(P, "dve", "tensor_tensor", reads=[cb["ci"], cb["bre"]], writes=[cb["t3"]], out=cb["t3"].ap, in0=cb["ci"].ap, in1=cb["bre"].ap, op=ALU.mult)
            for v in range(2):
                E(P, "dve", "tensor_tensor", reads=[cb["t0"], cb["t1"]], writes=[Ball[v]], out=Ball[v][:, 64 * v:64 * v + 64], in0=cb["t0"].ap, in1=cb["t1"].ap,
                  op=ALU.subtract)
                E(P, "dve", "tensor_tensor", reads=[cb["t2"], cb["t3"]], writes=[Ball[v]], out=Ball[v][:, 64 * (1 - v):64 * (1 - v) + 64], in0=cb["t2"].ap,
                  in1=cb["t3"].ap, op=ALU.add)
                for g in range(8):
                    E(P, "pool" if g % 2 else "dve", "tensor_scalar", reads=[Ball[v], gmask], writes=[Bpad[v][g]], out=Bpad[v][g].ap, in0=Ball[v].ap,
                      scalar1=gmask[:, g:g + 1], scalar2=None, op0=ALU.mult)
            DMA(P, "sp", Call[0:64, :], cre[d, :, cc * 128:(cc + 1) * 128], writes=[Call.key + ":r"])
            DMA(P, "act", Call[64:128, :], cim[d, :, cc * 128:(cc + 1) * 128], writes=[Call.key + ":i"])
            E(P, "dve", "tensor_scalar", reads=[Call.key + ":i"], writes=[Call.key + ":i"], out=Call[64:128, :], in0=Call[64:128, :], scalar1=-1.0, scalar2=None,
              op0=ALU.mult)
            for g in range(8):
                E(P, "pool", "memset", writes=[Cpad[g]], ap=Cpad[g].ap, constant=0.0)
                E(P, "act", "activation", reads=[Call.key + ":r", Call.key + ":i"], writes=[Cpad[g]], out=Cpad[g][:, 16 * g:16 * g + 16], in_=Call[:, 16 * g:16 * g + 16],
                  func=AF.Copy)
            if d == 0:
                E(P, "act", "activation", reads=[su32], writes=[su16], out=su16.ap, in_=su32.ap, func=AF.Copy)
            else:
                E(P, "act", "activation", reads=[su32], writes=[su16], out=su16[:, 0:256], in_=su32[:, 0:256][:, ::-1], func=AF.Copy)
                E(P, "act", "activation", reads=[su32], writes=[su16], out=su16[:, 256:NTOK], in_=su32[:, 256:NTOK][:, ::-1], func=AF.Copy)
            items = [(g, qn) for qn in range(5) for g in range(8)]

            def subblocks(n_h):
                return [(o, min(o + 512, n_h)) for o in range(0, n_h, 512)]

            def stage_a(k):
                g, qn = items[k]
                gi = cc * 8 + g
                W_ = sets[k % 2]
                tt, ti, Ct, St, vv = (W_[x] for x in ("tt", "ti", "Ct", "St", "vv"))
                dtile = dts[k % 2]
                thc = thp[:, d, gi:gi + 1]
                q_lo, q_hi = S5_PIECES[qn]
                n_h = q_hi - q_lo
                if qn == 0 and g == 0:
                    E(P, "pool", "memset", writes=[carries], ap=carries.ap, constant=0.0)
                E(P, "act", "activation", reads=[tau, mdp], writes=[dtile], out=dtile[:, 0:n_h], in_=tau[:, 0:n_h], func=AF.Identity, scale=0.0,
                  bias=mdp[:, d, gi:gi + 1])
                E(P, "act", "activation", reads=[tau, thpq[qn]] + tk, writes=[tt], out=tt[:, 0:n_h], in_=tau[:, 0:n_h], func=AF.Identity, scale=thc,
                  bias=thpq[qn][:, d, gi:gi + 1])
                E(P, "dve", "tensor_copy", reads=[tt], writes=[ti], out=ti[:, 0:n_h], in_=tt[:, 0:n_h])
                E(P, "dve", "tensor_tensor", reads=[tt, ti], writes=[tt], out=tt[:, 0:n_h], in0=tt[:, 0:n_h], in1=ti[:, 0:n_h], op=ALU.subtract)
                E(P, "act", "activation", reads=[tt], writes=[St], out=St[:, 0:n_h], in_=tt[:, 0:n_h], func=AF.Sin, scale=TWO_PI)
                E(P, "act", "activation", reads=[tt], writes=[vv], out=vv[:, 0:n_h], in_=tt[:, 0:n_h], func=AF.Abs)
                E(P, "act", "activation", reads=[vv, hpi], writes=[Ct], out=Ct[:, 0:n_h], in_=vv[:, 0:n_h], func=AF.Sin, scale=-TWO_PI, bias=hpi[:, 0:1])
                for v in range(2):
                    for (o0, o1) in subblocks(n_h):
                        E(P, "pe", "matmul", reads=[Bpad[v][g], su16], writes=[psB[v]], out=psB[v][:, o0:o1], lhsT=Bpad[v][g].ap, rhs=su16[:, q_lo + o0:q_lo + o1],
                          start=True, stop=True)
                ta_, tb_ = ta[k % 2], tb[k % 2]
                E(P, "dve", "tensor_tensor", reads=[psB[0], Ct], writes=[ta_], out=ta_[:, 0:n_h], in0=psB[0][:, 0:n_h], in1=Ct[:, 0:n_h], op=ALU.mult)
                E(P, "dve", "scalar_tensor_tensor", reads=[psB[1], St, sgn], writes=[tb_], out=tb_[:, 0:n_h], in0=psB[1][:, 0:n_h], scalar=sgn[:, 0:1], in1=St[:, 0:n_h],
                  op0=ALU.mult, op1=ALU.mult)
                E(P, "pool", "tensor_tensor", reads=[ta_, tb_, vv], writes=[vv], out=vv[:, 0:n_h], in0=ta_[:, 0:n_h], in1=tb_[:, 0:n_h], op=ALU.add)

            def stage_b(k):
                g, qn = items[k]
                W_ = sets[k % 2]
                Ct, St, vv, st = (W_[x] for x in ("Ct", "St", "vv", "st"))
                dtile = dts[k % 2]
                q_lo, q_hi = S5_PIECES[qn]
                n_h = q_hi - q_lo
                E(P, "dve", "tensor_tensor_scan", reads=[dtile, carries, vv], writes=[st], out=st[:, 0:n_h], data0=dtile[:, 0:n_h], data1=vv[:, 0:n_h],
                  initial=carries[:, g:g + 1], op0=ALU.mult, op1=ALU.add)
                E(P, "act", "activation", reads=[st], writes=[carries], out=carries[:, g:g + 1], in_=st[:, n_h - 1:n_h], func=AF.Copy)
                for (o0, o1) in subblocks(n_h):
                    E(P, "pe", "matmul", reads=[perm, st], writes=[psW], out=psW[:, o0:o1], lhsT=perm.ap, rhs=st[:, o0:o1], start=True, stop=True)
                ta_, tb_ = ta2[k % 2], tb2[k % 2]
                E(P, "pool", "tensor_tensor", reads=[st, Ct], writes=[ta_], out=ta_[:, 0:n_h], in0=st[:, 0:n_h], in1=Ct[:, 0:n_h], op=ALU.mult)
                E(P, "dve", "scalar_tensor_tensor", reads=[psW, St, sgn], writes=[tb_], out=tb_[:, 0:n_h], in0=psW[:, 0:n_h], scalar=sgn[:, 1:2], in1=St[:, 0:n_h],
                  op0=ALU.mult, op1=ALU.mult)
                sb_ = Sb[k % 2]
                E(P, "dve", "tensor_tensor", reads=[ta_, tb_], writes=[sb_], out=sb_[:, 0:n_h], in0=ta_[:, 0:n_h], in1=tb_[:, 0:n_h], op=ALU.add)
                for (o0, o1) in subblocks(n_h):
                    E(P, "pe", "matmul", reads=[Cpad[g], sb_], writes=[psY], out=psY[:, o0:o1], lhsT=Cpad[g].ap, rhs=sb_[:, o0:o1], start=(g == 0), stop=(g == 7))
                if g == 7:
                    a, b = q_lo, q_hi
                    if d == 0:
                        yv = yacc[:, a:b]
                    elif a < 256:
                        yv = yacc[:, 256 - b:256 - a][:, ::-1]
                    else:
                        yv = yacc[:, 4608 - b:4608 - a][:, ::-1]
                    E(P, "dve", "tensor_tensor", reads=[psY, yacc], writes=[yacc], out=yv, in0=psY[:, 0:n_h], in1=yv, op=ALU.add)

            stage_a(0)
            for k in range(len(items)):
                if k + 1 < len(items):
                    stage_a(k + 1)
                stage_b(k)
        E(P, "act", "activation", reads=[yacc], writes=[yacc], out=yacc.ap, in_=yacc.ap, func=AF.Gelu_apprx_tanh)
        E(P, "pool", "tensor_copy", reads=[yacc], writes=[su16], out=su16.ap, in_=yacc.ap)
        DMA(P, "sp", g32[cc * 128:(cc + 1) * 128, :], yacc.ap, reads=[yacc], writes=["s5_g32"])
        DMA(P, "act", gT[cc * 128:(cc + 1) * 128, :], su16.ap, reads=[su16], writes=["s5_gT"])
    allgather(kb, "s5_gT", "s5_gTg", 512, NTOK, BF16, PAIRS)


TOK_BLOCKS = [(0, 256)] + [(256 + 512 * k, 256 + 512 * (k + 1)) for k in range(8)]


def phase_glu(kb):
    P = kb.P
    kb.phase()
    proj = kb.dram("proj0", [3072, NTOK])
    rc = ag_rows(512, NTOK, BF16)
    gTg = kb.dram("s5_gTg", [(512 // rc) * 2 * rc, NTOK], BF16)
    g32 = kb.dram("s5_g32", [512, NTOK])
    gwd = kb.dram("s5_gluw", [1024, 512]); gbd = kb.dram("s5_glub", [512, 1])
    mixT = kb.dram("mixT", [1024, NTOK], BF16)
    W16 = kb.sb("gW16", [128, 8, 512], BF16)
    wst = [kb.sb(f"gwst{i}", [128, 512]) for i in range(2)]
    gb = kb.sb("gb", [128, 4])
    gall = [kb.sb(f"gall{i}", [128, 8, 512], BF16) for i in range(2)]
    gt = [kb.sb(f"ggt{i}", [128, 512]) for i in range(2)]
    sgt = [kb.sb(f"gsg{i}", [128, 512]) for i in range(2)]
    sig = [kb.sb(f"gsig{i}", [128, 512]) for i in range(2)]
    ob = [kb.sb(f"gob{i}", [128, 512], BF16) for i in range(2)]
    ps = [kb.ps(f"gps{i}", i, [128, 512]) for i in range(2)]
    for k in range(8):
        DMA(P, kb.q(), wst[k % 2].ap, gwd[k * 128:(k + 1) * 128, :], writes=[wst[k % 2]])
        E(P, "act" if k % 2 else "pool", "activation" if k % 2 else "tensor_copy", reads=[wst[k % 2]], writes=[W16.key + f":{k}"],
          **(dict(out=W16[:, k, :], in_=wst[k % 2].ap, func=AF.Copy) if k % 2 else dict(out=W16[:, k, :], in_=wst[k % 2].ap)))
    for c4 in range(4):
        DMA(P, kb.q(), gb[:, c4:c4 + 1], gbd[c4 * 128:(c4 + 1) * 128, :], writes=[gb.key + f":{c4}"])
    wk = [W16.key + f":{k}" for k in range(8)]
    gsrc = gTg.rearrange("(k p) t -> p k t", p=128)
    it = 0
    for bi, (a, b) in enumerate(TOK_BLOCKS):
        nb = b - a
        ga = gall[bi % 2]
        DMA(P, kb.q(), ga[:, :, 0:nb], gsrc[:, :, a:b], reads=["s5_gTg"], writes=[ga])
        for c4 in range(4):
            j = it % 2
            it += 1
            for k in range(8):
                E(P, "pe", "matmul", reads=[ga] + wk, writes=[ps[j]], out=ps[j][:, 0:nb], lhsT=W16[:, k, c4 * 128:(c4 + 1) * 128], rhs=ga[:, k, 0:nb],
                  start=(k == 0), stop=(k == 7))
            E(P, "act", "activation", reads=[ps[j], gb.key + f":{c4}"], writes=[sig[j]], out=sig[j][:, 0:nb], in_=ps[j][:, 0:nb], func=AF.Sigmoid, bias=gb[:, c4:c4 + 1])
            DMA(P, kb.q(), gt[j][:, 0:nb], g32[c4 * 128:(c4 + 1) * 128, a:b], reads=["s5_g32"], writes=[gt[j]])
            DMA(P, kb.q(), sgt[j][:, 0:nb], proj[5 * 512 + c4 * 128:5 * 512 + (c4 + 1) * 128, a:b], reads=["proj0"], writes=[sgt[j]])
            E(P, "act", "activation", reads=[sgt[j]], writes=[sgt[j]], out=sgt[j][:, 0:nb], in_=sgt[j][:, 0:nb], func=AF.Silu)
            E(P, "dve", "tensor_tensor", reads=[gt[j], sig[j]], writes=[sig[j]], out=sig[j][:, 0:nb], in0=gt[j][:, 0:nb], in1=sig[j][:, 0:nb], op=ALU.mult)
            E(P, "pool", "tensor_tensor", reads=[sig[j], sgt[j]], writes=[ob[j]], out=ob[j][:, 0:nb], in0=sig[j][:, 0:nb], in1=sgt[j][:, 0:nb], op=ALU.mult)
            DMA(P, kb.q("st"), mixT[512 + c4 * 128:512 + (c4 + 1) * 128, a:b], ob[j][:, 0:nb], reads=[ob[j]], writes=["mixT"])


def load_gate_half(kb, layer, sel, hsel, Gl, Gc, Gfull, rows, psums):
    P = kb.P
    load_rows(kb, layer, rows, (2,))
    for (G_, si) in ((Gl, 0), (Gc, 1)):
        if G_ is None:
            continue
        bcast_row(kb, rows, 0, sel[:, si, :], Gfull, psums)
        E(P, "dve", "tensor_scalar", reads=allkeys(Gfull) + ["hsel"], writes=[G_], out=G_.ap, in0=Gfull[:, 0:1024], scalar1=hsel[:, 0:1], scalar2=None, op0=ALU.mult)
        E(P, "dve", "scalar_tensor_tensor", reads=allkeys(Gfull) + ["hsel", G_], writes=[G_], out=G_.ap, in0=Gfull[:, 1024:2048], scalar=hsel[:, 1:2], in1=G_.ap,
          op0=ALU.mult, op1=ALU.add)


def phase_post(kb, layer, src_name, gathered_name, krows, xin_name, xout_name, blocks, wname, xin_row0=0):
    P = kb.P
    kb.phase()
    ntok_src = kb.D[src_name].shape[1] if src_name in kb.D else None
    ncols = NTOK if layer == 0 else NLAT
    nch, rc = allgather(kb, src_name, gathered_name, krows, ncols, BF16, PAIRS)
    kch = 2 * krows // 128
    mg = kb.D[gathered_name].rearrange("(k p) t -> p k t", p=128)
    outw = kb.dram(wname, [2 * krows, 1024])
    xin = kb.dram(xin_name, [ncols, 1024]) if xin_name not in kb.D else kb.D[xin_name][xin_row0:xin_row0 + ncols, :]
    xout = kb.dram(xout_name, [ncols, 1024])
    selm = kb.dram("selm", [2, 5, 128]); hseld = kb.dram("hsel", [128, 2])
    W16 = kb.sb("pW16", [128, kch, 1024], BF16)
    wst = [kb.sb(f"pwst{i}", [128, 1024]) for i in range(2)]
    sel = kb.sb("psel", [5, 2, 128]); hsel = kb.sb("phsel", [128, 2])
    rows = kb.sb("prows", [5, 1, 8, 256])
    Gfull = kb.sb("pGfull", [128, D])
    Gl = kb.sb("pGl", [128, 1024]); Gc = kb.sb("pGc", [128, 1024]) if layer == 0 else None
    mall = [kb.sb(f"pmall{i}", [128, kch, 512], BF16) for i in range(2)]
    xt = [kb.sb(f"pxt{i}", [128, 1024]) for i in range(2)]
    tmp = [kb.sb(f"ptmp{i}", [128, 512]) for i in range(2)]
    xn = [kb.sb(f"pxn{i}", [128, 1024]) for i in range(2)]
    ps = [kb.ps(f"pps{i}", i, [128, 512]) for i in range(4)]
    bps = [kb.ps(f"pbps{i}", 4 + i, [128, 512]) for i in range(4)]
    DMA(P, "sp", sel.ap, selm.rearrange("a q p -> q a p"), writes=["sel"])
    DMA(P, "act", hsel.ap, hseld, writes=["hsel"])
    for k in range(kch):
        DMA(P, kb.q(), wst[k % 2].ap, outw[k * 128:(k + 1) * 128, :], writes=[wst[k % 2]])
        if k % 2:
            E(P, "act", "activation", reads=[wst[k % 2]], writes=[W16.key + f":{k}"], out=W16[:, k, :], in_=wst[k % 2].ap, func=AF.Copy)
        else:
            E(P, "pool", "tensor_copy", reads=[wst[k % 2]], writes=[W16.key + f":{k}"], out=W16[:, k, :], in_=wst[k % 2].ap)
    load_gate_half(kb, layer, sel, hsel, Gl, Gc, Gfull, rows, bps)
    wk = [W16.key + f":{k}" for k in range(kch)]
    it = 0
    ti = 0
    for bi, (a, b) in enumerate(blocks):
        nb = b - a
        ma = mall[bi % 2]
        DMA(P, kb.q(), ma[:, :, 0:nb], mg[:, :, a:b], reads=[gathered_name], writes=[ma])
        G_ = Gc if (layer == 0 and a < 256) else Gl
        for ts in range(nb // 128):
            x_ = xt[ti % 2]; xn_ = xn[ti % 2]
            ti += 1
            DMA(P, kb.q(), x_.ap, xin[a + ts * 128:a + (ts + 1) * 128, :], writes=[x_])
            for nchunk in range(2):
                p_ = ps[it % 4]; t_ = tmp[it % 2]
                it += 1
                for k in range(kch):
                    E(P, "pe", "matmul", reads=[ma] + wk, writes=[p_], out=p_.ap, lhsT=ma[:, k, ts * 128:(ts + 1) * 128], rhs=W16[:, k, nchunk * 512:(nchunk + 1) * 512],
                      start=(k == 0), stop=(k == kch - 1))
                E(P, "dve", "tensor_tensor", reads=[p_, G_], writes=[t_], out=t_.ap, in0=p_.ap, in1=G_[:, nchunk * 512:(nchunk + 1) * 512], op=ALU.mult)
                E(P, "pool", "tensor_tensor", reads=[t_, x_], writes=[xn_.key + f":{nchunk}"], out=xn_[:, nchunk * 512:(nchunk + 1) * 512], in0=t_.ap,
                  in1=x_[:, nchunk * 512:(nchunk + 1) * 512], op=ALU.add)
            DMA(P, kb.q("st"), xout[a + ts * 128:a + (ts + 1) * 128, :], xn_.ap, reads=[xn_.key + ":0", xn_.key + ":1"], writes=[xout_name])


def rope_tables():
    f32 = np.float32
    n = NLAT
    rows = n // 64
    row = np.broadcast_to(np.arange(rows, dtype=f32)[:, None], (rows, 64)).reshape(n)
    col = np.broadcast_to(np.arange(64, dtype=f32)[None, :], (rows, 64)).reshape(n)
    freqs = (f32(10000.0) ** (-np.arange(16, dtype=f32) / f32(16))).astype(f32)
    ar = (row[:, None] * freqs[None, :]).astype(f32)
    ac = (col[:, None] * freqs[None, :]).astype(f32)
    C64 = np.concatenate([np.cos(ar), np.cos(ar), np.cos(ac), np.cos(ac)], 1).astype(f32)
    S64 = np.concatenate([-np.sin(ar), np.sin(ar), -np.sin(ac), np.sin(ac)], 1).astype(f32)
    C = np.concatenate([np.ones((NCTX, 64), f32), C64], 0)
    S = np.concatenate([np.zeros((NCTX, 64), f32), S64], 0)
    return np.ascontiguousarray(np.tile(C, (1, 8))), np.ascontiguousarray(np.tile(S, (1, 8)))


def l1_norm_setup(kb, W16, inw, ncols_w, tagp):
    P = kb.P
    wst = kb.sb(tagp + "wst", [128, ncols_w])
    for k in range(16):
        DMA(P, kb.q(), wst.ap, inw[k * 128:(k + 1) * 128, :], writes=[wst])
        if k % 2 == 0:
            E(P, "pool", "tensor_copy", reads=[wst], writes=[W16.key + f":{k}"], out=W16[:, k, :], in_=wst.ap)
        else:
            E(P, "act", "activation", reads=[wst], writes=[W16.key + f":{k}"], out=W16[:, k, :], in_=wst.ap, func=AF.Copy)


def l1_load_x(kb, xt, x1g, tile_idx):
    P = kb.P
    for r in range(2):
        row0 = (tile_idx * 2 + r) * 128
        DMA(P, kb.q(), xt[:, r * 1024:(r + 1) * 1024], x1g[row0:row0 + 128, :], reads=["x1g"], writes=[xt.key + f":{r}"])


def norm_tile2(kb, xt, junk, ssq, t1, hb, A, S):
    P = kb.P
    xk = [xt.key + ":0", xt.key + ":1"]
    E(P, "act", "activation", reads=xk, writes=[junk, ssq], out=junk.ap, in_=xt.ap, func=AF.Square, accum_out=ssq[:, 0:1])
    E(P, "dve", "tensor_scalar", reads=[ssq], writes=[ssq], out=ssq[:, 1:2], in0=ssq[:, 0:1], scalar1=1.0 / D, scalar2=1e-6, op0=ALU.mult, op1=ALU.add)
    E(P, "act", "activation", reads=[ssq], writes=[ssq], out=ssq[:, 2:3], in_=ssq[:, 1:2], func=AF.Sqrt)
    E(P, "dve", "reciprocal", reads=[ssq], writes=[ssq], out=ssq[:, 3:4], in_=ssq[:, 2:3])
    E(P, "dve", "scalar_tensor_tensor", reads=xk + [ssq] + allkeys(A), writes=[t1], out=t1.ap, in0=xt.ap, scalar=ssq[:, 3:4], in1=A.ap, op0=ALU.mult, op1=ALU.mult)
    E(P, "pool", "tensor_tensor", reads=[t1] + allkeys(S), writes=[hb], out=hb.ap, in0=t1.ap, in1=S.ap, op=ALU.add)


def phase_l1_proj(kb, which):
    P = kb.P
    kb.phase()
    if which == "qk":
        allgather(kb, "x1h", "x1g", NTOK, 1024, F32, PAIRS, rc=128)
    x1g = kb.dram("x1g", [34 * 2 * 128, 1024])
    normw = kb.dram("normw", [2, D]); selm = kb.dram("selm", [2, 5, 128])
    inw = kb.dram("inw1" + which, [D, 2048])
    idb = kb.C["idb"]
    W16 = kb.sb("l1W16", [128, 16, 2048], BF16)
    A = kb.sb("l1A", [128, D]); S = kb.sb("l1S", [128, D]); sel = kb.sb("l1sel", [5, 2, 128]); rows = kb.sb("l1rows", [5, 2, 8, 256])
    xt = kb.sb("l1xt", [128, D]); junk = kb.sb("l1junk", [128, D], BF16); t1 = kb.sb("l1t1", [128, D]); nw = t1
    hbs = [kb.sb(f"l1hb{i}", [128, D], BF16) for i in range(2)]
    hT = kb.sb("l1hT", [128, 16, 512], BF16)
    ssqs = [kb.sb(f"l1ssq{i}", [128, 4]) for i in range(2)]
    pTs = [kb.ps(f"l1pT{i}", 2 * i, [128, 16, 128], BF16) for i in range(2)]
    pjs = [kb.ps(f"l1pj{i}", 4 + i, [128, 512]) for i in range(2)]
    pq = [kb.ps(f"l1pq{i}", 6 + i, [128, 4, 128], BF16) for i in range(2)]
    l1_norm_setup(kb, W16, inw, 2048, "l1")
    DMA(P, "sp", sel.ap, selm.rearrange("a q p -> q a p"), writes=["sel"])
    load_rows(kb, 1, rows, (0, 1))
    wk = [W16.key + f":{k}" for k in range(16)]
    if which == "qk":
        qT = kb.dram("qT", [8, 128, NLAT], BF16); kT = kb.dram("kT", [8, 128, NTOK], BF16)
        ropeC = kb.dram("ropeC", [NTOK, 512]); ropeS = kb.dram("ropeS", [NTOK, 512]); qkw = kb.dram("qkw", [2, 512])
        wrep = kb.sb("wrep", [128, 2, 512])
        Ct = [kb.sb(f"rC{i}", [128, 512]) for i in range(2)]; St_ = [kb.sb(f"rS{i}", [128, 512]) for i in range(2)]
        psets = [dict(sq=kb.sb(f"rsq{i}", [128, 512]), ss=kb.sb(f"rss{i}", [128, 32]), xn=kb.sb(f"rxn{i}", [128, 512]), r1=kb.sb(f"rr1{i}", [128, 512]),
                      r2=kb.sb(f"rr2{i}", [128, 512])) for i in range(2)]
        qr = [kb.sb(f"rqr{i}", [128, 512], BF16) for i in range(2)]
        stg = [kb.sb(f"rstg{i}", [128, 8, 512], BF16) for i in range(2)]
        for i in range(2):
            DMA(P, kb.q(), wrep[:, i, :], qkw[i:i + 1, :].broadcast_to([128, 512]), writes=[wrep.key + f":{i}"])
    else:
        vtok = kb.dram("vtok", [NTOK, 1024], BF16); gsil = kb.dram("gsil", [1024, NLAT])
        vb = [kb.sb(f"vvb{i}", [128, 1024], BF16) for i in range(2)]
        gst = [kb.sb(f"vgst{i}", [128, 512]) for i in range(2)]

    cur = None
    ti = 0
    it = 0
    for (a, b) in TOK_BLOCKS:
        nb = b - a
        is_ctx = a < 256
        if cur != is_ctx:
            cur = is_ctx
            sel_ap = sel[:, 1 if is_ctx else 0, :]
            bcast_row(kb, rows, 0, sel_ap, S, pjs)
            bcast_row(kb, rows, 1, sel_ap, A, pjs)
            DMA(P, "act", nw.ap, normw[1:2, :].broadcast_to([128, D]), writes=[nw])
            E(P, "dve", "scalar_tensor_tensor", reads=allkeys(A) + [nw], writes=allkeys(A), out=A.ap, in0=A.ap, scalar=1.0, in1=nw.ap, op0=ALU.add, op1=ALU.mult)
        for tt in range(nb // 128):
            hb = hbs[ti % 2]; ssq = ssqs[ti % 2]
            l1_load_x(kb, xt, x1g, (a // 128) + tt)
            norm_tile2(kb, xt, junk, ssq, t1, hb, A, S)
            pT = pTs[ti % 2]
            for k in range(16):
                E(P, "pe", "transpose", reads=[hb, idb], writes=[pT], out=pT[:, k, :], in_=hb[:, k * 128:(k + 1) * 128], identity=idb.ap)
            if ti % 2 == 0:
                E(P, "act", "activation", reads=[pT], writes=[hT.key + f":{tt}"], out=hT[:, :, tt * 128:(tt + 1) * 128], in_=pT.ap, func=AF.Copy)
            else:
                E(P, "dve", "tensor_copy", reads=[pT], writes=[hT.key + f":{tt}"], out=hT[:, :, tt * 128:(tt + 1) * 128], in_=pT.ap)
            ti += 1
        hkeys = [hT.key + f":{tt}" for tt in range(nb // 128)]
        if which == "qk":
            units = [(part, tt, half) for part in ((1,) if is_ctx else (0, 1)) for tt in range(nb // 128) for half in range(2)]

            def unit_mm(u):
                part, tt, half = units[u]
                j = (it + u) % 2
                ps = pjs[j]
                tok0 = a + tt * 128
                for k in range(16):
                    E(P, "pe", "matmul", reads=[hT.key + f":{tt}"] + wk, writes=[ps], out=ps.ap, lhsT=hT[:, k, tt * 128:(tt + 1) * 128],
                      rhs=W16[:, k, part * 1024 + half * 512: part * 1024 + (half + 1) * 512], start=(k == 0), stop=(k == 15))
                DMA(P, kb.q(), Ct[j].ap, ropeC[tok0:tok0 + 128, :], writes=[Ct[j]])
                DMA(P, kb.q(), St_[j].ap, ropeS[tok0:tok0 + 128, :], writes=[St_[j]])

            def unit_post(u):
                part, tt, half = units[u]
                j = (it + u) % 2
                ps = pjs[j]
                S_ = psets[j]
                sq, ss, xn, r1, r2 = S_["sq"], S_["ss"], S_["xn"], S_["r1"], S_["r2"]
                stg_ = stg[part]
                E(P, "act", "activation", reads=[ps], writes=[sq], out=sq.ap, in_=ps.ap, func=AF.Square)
                E(P, "dve", "tensor_reduce", reads=[sq], writes=[ss], out=ss[:, 0:8], in_=sq.ap.rearrange("p (g d) -> p g d", d=64), axis=AX.X, op=ALU.add)
                E(P, "dve", "tensor_scalar", reads=[ss], writes=[ss], out=ss[:, 8:16], in0=ss[:, 0:8], scalar1=1.0 / 64, scalar2=1e-6, op0=ALU.mult, op1=ALU.add)
                E(P, "act", "activation", reads=[ss], writes=[ss], out=ss[:, 16:24], in_=ss[:, 8:16], func=AF.Sqrt)
                E(P, "dve", "reciprocal", reads=[ss], writes=[ss], out=ss[:, 24:32], in_=ss[:, 16:24])
                for g in range(8):
                    E(P, "dve", "scalar_tensor_tensor", reads=[ps, ss, wrep.key + f":{part}"], writes=[xn], out=xn[:, g * 64:(g + 1) * 64], in0=ps[:, g * 64:(g + 1) * 64],
                      scalar=ss[:, 24 + g:25 + g], in1=wrep[:, part, g * 64:(g + 1) * 64], op0=ALU.mult, op1=ALU.mult)
                xsw = xn.ap.rearrange("p (gp w d) -> p gp w d", w=2, d=16)[:, :, ::-1, :]
                E(P, "pool", "tensor_tensor", reads=[xn, Ct[j]], writes=[r1], out=r1.ap, in0=xn.ap, in1=Ct[j].ap, op=ALU.mult)
                E(P, "dve", "tensor_tensor", reads=[xn, St_[j]], writes=[r2], out=r2.ap.rearrange("p (gp w d) -> p gp w d", w=2, d=16), in0=xsw,
                  in1=St_[j].ap.rearrange("p (gp w d) -> p gp w d", w=2, d=16), op=ALU.mult)
                E(P, "pool", "tensor_tensor", reads=[r1, r2], writes=[qr[j]], out=qr[j].ap, in0=r1.ap, in1=r2.ap, op=ALU.add)
                for hh in range(4):
                    E(P, "pe", "transpose", reads=[qr[j], idb], writes=[pq[j]], out=pq[j][:, hh, :], in_=qr[j][:, hh * 128:(hh + 1) * 128], identity=idb.ap)
                E(P, "act", "activation", reads=[pq[j]], writes=[stg_.key + f":{tt}:{half}"], out=stg_[:, half * 4:(half + 1) * 4, tt * 128:(tt + 1) * 128],
                  in_=pq[j].ap, func=AF.Copy)

            unit_mm(0)
            for u in range(len(units)):
                if u + 1 < len(units):
                    unit_mm(u + 1)
                unit_post(u)
            it += len(units)
            for part in ((1,) if is_ctx else (0, 1)):
                stg_ = stg[part]
                skeys = [stg_.key + f":{tt}:{half}" for tt in range(nb // 128) for half in range(2)]
                if part == 0:
                    DMA(P, kb.q("st"), qT[:, :, a - 256:b - 256].rearrange("h p t -> p h t"), stg_[:, :, 0:nb], reads=skeys, writes=["qT"] + skeys)
                else:
                    DMA(P, kb.q("st"), kT[:, :, a:b].rearrange("h p t -> p h t"), stg_[:, :, 0:nb], reads=skeys, writes=["kT"] + skeys)
        else:
            for tt in range(nb // 128):
                vb_ = vb[tt % 2]
                for half in range(2):
                    ps = pjs[it % 2]
                    it += 1
                    for k in range(16):
                        E(P, "pe", "matmul", reads=[hT.key + f":{tt}"] + wk, writes=[ps], out=ps.ap, lhsT=hT[:, k, tt * 128:(tt + 1) * 128],
                          rhs=W16[:, k, half * 512:(half + 1) * 512], start=(k == 0), stop=(k == 15))
                    E(P, "act", "activation", reads=[ps], writes=[vb_.key + f":{half}"], out=vb_[:, half * 512:(half + 1) * 512], in_=ps.ap, func=AF.Copy)
                DMA(P, kb.q("st"), vtok[a + tt * 128:a + (tt + 1) * 128, :], vb_.ap, reads=[vb_.key + ":0", vb_.key + ":1"], writes=["vtok", vb_.key + ":0", vb_.key + ":1"])
            if not is_ctx:
                for cc in range(8):
                    ps = pjs[it % 2]
                    g_ = gst[it % 2]
                    it += 1
                    for k in range(16):
                        E(P, "pe", "matmul", reads=hkeys + wk, writes=[ps], out=ps.ap, lhsT=W16[:, k, 1024 + cc * 128:1024 + (cc + 1) * 128], rhs=hT[:, k, 0:nb],
                          start=(k == 0), stop=(k == 15))
                    E(P, "act", "activation", reads=[ps], writes=[g_], out=g_.ap, in_=ps.ap, func=AF.Silu)
                    DMA(P, kb.q("st"), gsil[cc * 128:(cc + 1) * 128, a - 256:b - 256], g_.ap, reads=[g_], writes=["gsil", g_])


def phase_attn(kb):
    P = kb.P
    kb.phase()
    qT = kb.dram("qT", [8, 128, NLAT], BF16); kT = kb.dram("kT", [8, 128, NTOK], BF16)
    vtok = kb.dram("vtok", [NTOK, 1024], BF16); gsil = kb.dram("gsil", [1024, NLAT])
    attT = kb.dram("attT", [1024, NLAT], BF16)
    lqk = kb.dram("da_lqk", [4, 64]); slw = kb.dram("da_sublnw", [128, 1])
    lam_init = 0.8 - 0.6 * math.exp(-0.3 * 1)
    NKC = NTOK // 128
    kTh = [kb.sb(f"akT{i}", [128, NTOK], BF16) for i in range(2)]
    qTh = [kb.sb(f"aqT{i}", [128, NLAT], BF16) for i in range(2)]
    qz = [[kb.sb(f"aqz{i}{m}", [128, NLAT], BF16) for m in range(2)] for i in range(2)]
    vh = [kb.sb(f"avh{i}", [128, NKC, 128], BF16) for i in range(2)]
    Eb = [kb.sb(f"aE{i}", [128, 512], BF16) for i in range(3)]
    ones = kb.sb("aones", [128, 128], BF16)
    ones32 = kb.sb("aones32", [128, 128])
    accS = [kb.sb(f"aaccS{i}", [128, 512]) for i in range(2)]
    lq = kb.sb("alq", [128, 4, 64]); lt = kb.sb("alt", [128, 2, 64]); lam = kb.sb("alam", [128, 4]); swl = kb.sb("aswl", [128, 2])
    rc0 = kb.sb("arc0", [128, 512]); rc1 = kb.sb("arc1", [128, 512]); o0 = kb.sb("ao0", [128, 512]); o1 = kb.sb("ao1", [128, 512])
    sqb = kb.sb("asqb", [128, 512], BF16); rstd = kb.sb("arstd", [128, 512]); gt = [kb.sb(f"agt{i}", [128, 512]) for i in range(2)]
    ob = [kb.sb(f"aob{i}", [128, 512], BF16) for i in range(2)]
    psS = [kb.ps(f"apsS{i}", i, [128, 512]) for i in range(2)]
    acc = [kb.ps(f"apacc{i}", 2 + i, [128, 512]) for i in range(2)]
    sm = [kb.ps(f"apsm{i}", 4 + i, [128, 512]) for i in range(2)]
    psM = kb.ps("apsM", 6, [128, 512])
    E(P, "pool", "memset", writes=[ones], ap=ones.ap, constant=1.0)
    E(P, "pool", "memset", writes=[ones32], ap=ones32.ap, constant=1.0)
    for i in range(2):
        E(P, "pool", "memset", writes=[qz[i][0].key + ":z"], ap=qz[i][0][64:128, :], constant=0.0)
        E(P, "pool", "memset", writes=[qz[i][1].key + ":z"], ap=qz[i][1][0:64, :], constant=0.0)
    DMA(P, "sp", lq.ap, lqk.rearrange("(o a) d -> o a d", o=1).broadcast_to([128, 4, 64]), writes=[lq])
    E(P, "dve", "tensor_tensor", reads=[lq], writes=[lt], out=lt.ap, in0=lq[:, 0:4:2, :], in1=lq[:, 1:4:2, :], op=ALU.mult)
    E(P, "dve", "tensor_reduce", reads=[lt], writes=[lam], out=lam[:, 0:2], in_=lt.ap, axis=AX.X, op=ALU.add)
    E(P, "act", "activation", reads=[lam], writes=[lam], out=lam[:, 0:2], in_=lam[:, 0:2], func=AF.Exp)
    E(P, "dve", "tensor_tensor", reads=[lam], writes=[lam], out=lam[:, 2:3], in0=lam[:, 0:1], in1=lam[:, 1:2], op=ALU.subtract)
    E(P, "dve", "tensor_scalar", reads=[lam], writes=[lam], out=lam[:, 3:4], in0=lam[:, 2:3], scalar1=lam_init, scalar2=-1.0, op0=ALU.add, op1=ALU.mult)
    DMA(P, "act", swl[:, 0:1], slw, writes=[swl])
    E(P, "dve", "tensor_scalar", reads=[swl], writes=[swl], out=swl[:, 1:2], in0=swl[:, 0:1], scalar1=1.0 - lam_init, scalar2=None, op0=ALU.mult)
    ei = 0
    gi = 0
    for hl in range(8):
        kt_, qt_, v_ = kTh[hl % 2], qTh[hl % 2], vh[hl % 2]
        DMA(P, "sp", kt_.ap, kT[hl], reads=["kT"], writes=[kt_])
        DMA(P, "act", qz[hl % 2][0][0:64, :], qT[hl, 0:64, :], reads=["qT"], writes=[qz[hl % 2][0].key + ":d"])
        DMA(P, "act", qz[hl % 2][1][64:128, :], qT[hl, 64:128, :], reads=["qT"], writes=[qz[hl % 2][1].key + ":d"])
        DMA(P, "pool", v_.ap, vtok.rearrange("(kc p) (h e) -> p kc h e", p=128, e=128)[:, :, hl, :], reads=["vtok"], writes=[v_])
        for qb in range(8):
            q0 = qb * 512
            g_ = gt[gi % 2]; ob_ = ob[gi % 2]
            gi += 1
            DMA(P, kb.q(), g_.ap, gsil[hl * 128:(hl + 1) * 128, q0:q0 + 512], reads=["gsil"], writes=[g_])
            steps = [(m, kc) for m in range(2) for kc in range(NKC)]

            def issue_scores(i):
                m, kc = steps[i]
                ps = psS[(ei + i) % 2]; e_ = Eb[(ei + i) % 3]
                qz_ = qz[hl % 2][m]
                E(P, "pe", "matmul", reads=[kt_, qz_.key + ":d", qz_.key + ":z"], writes=[ps], out=ps.ap, lhsT=kt_[:, kc * 128:(kc + 1) * 128],
                  rhs=qz_[:, q0:q0 + 512], start=True, stop=True)
                E(P, "act", "activation", reads=[ps], writes=[e_], out=e_.ap, in_=ps.ap, func=AF.Exp, scale=0.125)

            issue_scores(0)
            for i, (m, kc) in enumerate(steps):
                if i + 1 < len(steps):
                    issue_scores(i + 1)
                e_ = Eb[(ei + i) % 3]
                E(P, "pe", "matmul", reads=[v_, e_], writes=[acc[m]], out=acc[m].ap, lhsT=v_[:, kc, :], rhs=e_.ap, start=(kc == 0), stop=(kc == NKC - 1))
                E(P, "pe", "matmul", reads=[ones, e_], writes=[sm[m]], out=sm[m].ap, lhsT=ones.ap, rhs=e_.ap, start=(kc == 0), stop=(kc == NKC - 1))
            ei += len(steps)
            E(P, "dve", "reciprocal", reads=[sm[0]], writes=[rc0], out=rc0.ap, in_=sm[0].ap)
            E(P, "dve", "reciprocal", reads=[sm[1]], writes=[rc1], out=rc1.ap, in_=sm[1].ap)
            E(P, "dve", "tensor_tensor", reads=[acc[0], rc0], writes=[o0], out=o0.ap, in0=acc[0].ap, in1=rc0.ap, op=ALU.mult)
            E(P, "dve", "tensor_tensor", reads=[acc[1], rc1], writes=[o1], out=o1.ap, in0=acc[1].ap, in1=rc1.ap, op=ALU.mult)
            E(P, "dve", "scalar_tensor_tensor", reads=[o1, lam, o0], writes=[o0], out=o0.ap, in0=o1.ap, scalar=lam[:, 3:4], in1=o0.ap, op0=ALU.mult, op1=ALU.add)
            E(P, "act", "activation", reads=[o0], writes=[sqb], out=sqb.ap, in_=o0.ap, func=AF.Square)
            E(P, "pe", "matmul", reads=[ones, sqb], writes=[psM], out=psM.ap, lhsT=ones.ap, rhs=sqb.ap, start=True, stop=True)
            E(P, "dve", "tensor_scalar", reads=[psM], writes=[rstd], out=rstd.ap, in0=psM.ap, scalar1=1.0 / 128, scalar2=1e-6, op0=ALU.mult, op1=ALU.add)
            E(P, "act", "activation", reads=[rstd], writes=[rstd], out=rstd.ap, in_=rstd.ap, func=AF.Sqrt)
            E(P, "dve", "reciprocal", reads=[rstd], writes=[rstd], out=rstd.ap, in_=rstd.ap)
            E(P, "dve", "scalar_tensor_tensor", reads=[o0, swl, rstd], writes=[o1], out=o1.ap, in0=o0.ap, scalar=swl[:, 1:2], in1=rstd.ap, op0=ALU.mult, op1=ALU.mult)
            E(P, "pool", "tensor_tensor", reads=[o1, g_], writes=[ob_], out=ob_.ap, in0=o1.ap, in1=g_.ap, op=ALU.mult)
            DMA(P, kb.q("st"), attT[hl * 128:(hl + 1) * 128, q0:q0 + 512], ob_.ap, reads=[ob_], writes=["attT"])


L1_BLOCKS = [(512 * k, 512 * (k + 1)) for k in range(8)]


def build_part(ext_in, part):
    kb = KB(ext_in=ext_in, ext_out=["x1h"] if part == "l0" else ["outh"])
    setup_consts(kb)
    phase_mod(kb)
    if part in ("l0", "full"):
        phase_l0_pre(kb)
        for tag, n, col0 in (("l", NLAT, 256), ("c", NCTX, 0)):
            phase_filter(kb, n, tag)
            phase_shortconv(kb, n, tag, col0)
            phase_hyconv(kb, n, tag, col0)
        phase_s5(kb)
        phase_glu(kb)
        phase_post(kb, 0, "mixT", "mixTg", 1024, "xh0", "x1h", TOK_BLOCKS, "outw0")
    if part == "l1":
        kb.phase()
        src = kb.dram("x1h_in", [NTOK, 1024])
        dst = kb.dram("x1h", [NTOK, 1024])
        for i in range(NTOK // 128):
            DMA(kb.P, kb.q(), dst[i * 128:(i + 1) * 128, :], src[i * 128:(i + 1) * 128, :], writes=["x1h"])
    if part in ("l1", "full"):
        phase_l1_proj(kb, "qk")
        phase_l1_proj(kb, "vg")
        phase_attn(kb)
        phase_post(kb, 1, "attT", "attTg", 1024, "x1h", "outh", L1_BLOCKS, "outw1", xin_row0=256)
    return kb


FUSED = True


def _launch(part, pc, extra=None):
    names = list(pc[0].keys()) + (list(extra[0].keys()) if extra else [])
    kb = build_part(names, part)
    nc = kb.finish()
    maps = []
    for r in range(8):
        m = dict(pc[r])
        if extra:
            m.update(extra[r])
        maps.append({k: v for k, v in m.items() if k in kb.D})
    return run_bass_kernel_spmd(nc, maps, core_ids=list(range(8)))


def kernel(**inputs):
    inp = {k: np.asarray(v) for k, v in inputs.items()}
    pc = host_inputs(inp)
    if FUSED:
        res = _launch("full", pc)
    else:
        r0 = _launch("l0", pc)
        res = _launch("l1", pc, [{"x1h_in": r0.results[r]["x1h"]} for r in range(8)])
    out = np.empty((4, NLAT, D), np.float32)
    for r in range(8):
        b, h = r // 2, r % 2
        out[b][:, 1024 * h:1024 * (h + 1)] = res.results[r]["outh"]
    return out
```
